# Optimizing a Trainium2 kernel written in Bass

```python
import math
import jax, jax.numpy as jnp
from jax import lax
import numpy as np

D_MODEL = 1024
BATCH = 8
SEQ = 2048
DEPTH = 4

N_MIXERS = 3
MEM_LEN = 256
DA_HEADS = 8
DA_HEAD_DIM = 64
DA_QBLOCK = 128
ROPE_THETA = 10000.0
POOL_WINDOWS = (2, 4, 8, 16)
POOL_GROUPS = len(POOL_WINDOWS)
POOL_GROUP_DIM = D_MODEL // POOL_GROUPS
RET_HEADS = 4
RET_QK_DIM = D_MODEL // RET_HEADS
RET_V_DIM = 2 * RET_QK_DIM
RET_CHUNK = 128
XA_HEADS = 4
XA_HEAD_DIM = D_MODEL // XA_HEADS
D_FF = 2816
DEEPNORM_ALPHA = (2 * DEPTH) ** 0.25
DEEPNORM_BETA = (8 * DEPTH) ** -0.25
LN_EPS = 1e-5

N_DA = len(range(0, DEPTH, N_MIXERS))
N_POOL = len(range(1, DEPTH, N_MIXERS))
N_RET = len(range(2, DEPTH, N_MIXERS))

kernel_name = "hybrid_diffattn_pool_retention_deepnorm"


def layer_norm(x, g, b):
    xf = x.astype(jnp.float32)
    mu = jnp.mean(xf, axis=-1, keepdims=True)
    var = jnp.mean(jnp.square(xf - mu), axis=-1, keepdims=True)
    y = (xf - mu) * lax.rsqrt(var + LN_EPS) * g.astype(jnp.float32) + b.astype(jnp.float32)
    return y.astype(x.dtype)


def rope(x, cos, sin):
    half = x.shape[-1] // 2
    xf = x.astype(jnp.float32)
    x1, x2 = xf[..., :half], xf[..., half:]
    return jnp.concatenate([x1 * cos - x2 * sin, x2 * cos + x1 * sin], axis=-1).astype(x.dtype)


def swiglu_ffn(x, w_in, w_out):
    h = x @ w_in
    g, u = jnp.split(h, 2, axis=-1)
    return (jax.nn.silu(g) * u) @ w_out


def diff_attention(x, w_qkv, w_o, lam_q, lam_k, subln_g, cos, sin, lam_init):
    B, S, _ = x.shape
    H, dk = DA_HEADS, DA_HEAD_DIM
    q, k, v = jnp.split(x @ w_qkv, 3, axis=-1)
    q = q.reshape(B, S, H, 2, dk)
    k = k.reshape(B, S, H, 2, dk)
    v = v.reshape(B, S, H, 2 * dk)
    c5, s5 = cos[:, :, None, None, :], sin[:, :, None, None, :]
    q = rope(q, c5, s5)
    k = rope(k, c5, s5)
    lq = lam_q.astype(jnp.float32)
    lk = lam_k.astype(jnp.float32)
    lam = jnp.exp(jnp.sum(lq[0] * lk[0])) - jnp.exp(jnp.sum(lq[1] * lk[1])) + lam_init
    scale = dk ** -0.5
    nb = S // DA_QBLOCK
    qb = q.reshape(B, nb, DA_QBLOCK, H, 2, dk).transpose(1, 0, 2, 3, 4, 5)
    kpos = jnp.arange(S)

    def block(args):
        qblk, i = args
        s = jnp.einsum('bqhcd,bkhcd->bhcqk', qblk, k).astype(jnp.float32) * scale
        qpos = i * DA_QBLOCK + jnp.arange(DA_QBLOCK)
        mask = kpos[None, :] <= qpos[:, None]
        s = jnp.where(mask, s, jnp.finfo(jnp.float32).min)
        p = jax.nn.softmax(s, axis=-1)
        a = p[:, :, 0] - lam * p[:, :, 1]
        return jnp.einsum('bhqk,bkhe->bqhe', a.astype(v.dtype), v)

    o = lax.map(block, (qb, jnp.arange(nb)))
    o = o.transpose(1, 0, 2, 3, 4).reshape(B, S, H, 2 * dk).astype(jnp.float32)
    o = o * lax.rsqrt(jnp.mean(jnp.square(o), axis=-1, keepdims=True) + LN_EPS)
    o = o * subln_g.astype(jnp.float32) * (1.0 - lam_init)
    return o.reshape(B, S, H * 2 * dk).astype(x.dtype) @ w_o


def pool_mixer(x, w_grp, b_grp, scale):
    B, S, D = x.shape
    xg = x.reshape(B, S, POOL_GROUPS, POOL_GROUP_DIM).astype(jnp.float32)
    cs = jnp.cumsum(xg, axis=1)
    t = jnp.arange(S)
    pooled = []
    for g, w in enumerate(POOL_WINDOWS):
        c = cs[:, :, g]
        prev = jnp.pad(c, ((0, 0), (w, 0), (0, 0)))[:, :S]
        cnt = jnp.minimum(t + 1, w).astype(jnp.float32)[None, :, None]
        pooled.append((c - prev) / cnt)
    pooled = jnp.stack(pooled, axis=2) - xg
    y = jnp.einsum('bsgc,gcd->bsgd', pooled.astype(x.dtype), w_grp) + b_grp
    return y.reshape(B, S, D) * scale


def retention(x, w_qkvg, w_o, cos, sin):
    B, S, _ = x.shape
    H, dk, dv, C = RET_HEADS, RET_QK_DIM, RET_V_DIM, RET_CHUNK
    proj = x @ w_qkvg
    q, k, v, g = jnp.split(proj, [H * dk, 2 * H * dk, 2 * H * dk + H * dv], axis=-1)
    c4, s4 = cos[:, :, None, :], sin[:, :, None, :]
    q = rope(q.reshape(B, S, H, dk), c4, s4).astype(jnp.float32)
    k = rope(k.reshape(B, S, H, dk), c4, s4).astype(jnp.float32) * (dk ** -0.5)
    v = v.reshape(B, S, H, dv).astype(jnp.float32)
    log_gamma = jnp.log(1.0 - jnp.exp2(-5.0 - jnp.arange(H, dtype=jnp.float32)))
    idx = jnp.arange(C, dtype=jnp.float32)
    rel = idx[:, None] - idx[None, :]
    d_intra = jnp.where(rel[None] >= 0,
                        jnp.exp(jnp.maximum(rel, 0.0)[None] * log_gamma[:, None, None]), 0.0)
    q_decay = jnp.exp((idx[:, None] + 1.0) * log_gamma[None, :])
    k_decay = jnp.exp((C - 1.0 - idx[:, None]) * log_gamma[None, :])
    chunk_decay = jnp.exp(C * log_gamma)
    nc = S // C

    def to_chunks(a):
        return a.reshape(B, nc, C, H, a.shape[-1]).transpose(1, 0, 2, 3, 4)

    def step(R, inp):
        qc, kc, vc = inp
        att = jnp.einsum('bihd,bjhd->bhij', qc, kc) * d_intra[None]
        inner = jnp.einsum('bhij,bjhe->bihe', att, vc)
        cross = jnp.einsum('bihd,bhde->bihe', qc, R) * q_decay[None, :, :, None]
        R = R * chunk_decay[None, :, None, None] + jnp.einsum(
            'bjhd,bjhe->bhde', kc * k_decay[None, :, :, None], vc)
        return R, inner + cross

    R0 = jnp.zeros((B, H, dk, dv), jnp.float32)
    _, o = lax.scan(step, R0, (to_chunks(q), to_chunks(k), to_chunks(v)))
    o = o.transpose(1, 0, 2, 3, 4).reshape(B, S, H, dv)
    mu = jnp.mean(o, axis=-1, keepdims=True)
    var = jnp.mean(jnp.square(o - mu), axis=-1, keepdims=True)
    o = ((o - mu) * lax.rsqrt(var + LN_EPS)).reshape(B, S, H * dv).astype(x.dtype)
    return (jax.nn.silu(g) * o) @ w_o


def memory_cross_attention(x, mem, wq, wkv, wo):
    B, S, _ = x.shape
    M = mem.shape[1]
    q = (x @ wq).reshape(B, S, XA_HEADS, XA_HEAD_DIM)
    k, v = jnp.split(mem @ wkv, 2, axis=-1)
    k = k.reshape(B, M, XA_HEADS, XA_HEAD_DIM)
    v = v.reshape(B, M, XA_HEADS, XA_HEAD_DIM)
    s = jnp.einsum('bshd,bmhd->bhsm', q, k).astype(jnp.float32) * (XA_HEAD_DIM ** -0.5)
    p = jax.nn.softmax(s, axis=-1).astype(v.dtype)
    o = jnp.einsum('bhsm,bmhd->bshd', p, v).reshape(B, S, D_MODEL)
    return o @ wo


def setup_inputs(seed: int = 0) -> dict:
    key = jax.random.key(seed)
    ks = jax.random.split(key, 24)
    f32 = jnp.float32
    D, F = D_MODEL, D_FF
    nrm = lambda k, shape, std: jax.random.normal(k, shape, f32) * std
    x = nrm(ks[0], (BATCH, SEQ, D), 1.0)
    mem = nrm(ks[1], (BATCH, MEM_LEN, D), 1.0)
    offset = jax.random.randint(ks[2], (BATCH, 1), 0, 4096, dtype=jnp.int32)
    positions = (jnp.arange(SEQ, dtype=jnp.int32)[None, :] + offset).astype(jnp.int32)
    ffn_w_in = nrm(ks[3], (DEPTH, 2, D, 2 * F), D ** -0.5)
    ffn_w_out = nrm(ks[4], (DEPTH, 2, F, D), F ** -0.5 * DEEPNORM_BETA)
    ln_g = 1.0 + nrm(ks[5], (DEPTH, 4, D), 0.02)
    ln_b = nrm(ks[6], (DEPTH, 4, D), 0.02)
    da_w_qkv = nrm(ks[7], (N_DA, D, 3 * DA_HEADS * 2 * DA_HEAD_DIM), D ** -0.5)
    da_w_o = nrm(ks[8], (N_DA, DA_HEADS * 2 * DA_HEAD_DIM, D),
                 (DA_HEADS * 2 * DA_HEAD_DIM) ** -0.5 * DEEPNORM_BETA)
    da_lam_q = nrm(ks[9], (N_DA, 2, DA_HEAD_DIM), 0.1)
    da_lam_k = nrm(ks[10], (N_DA, 2, DA_HEAD_DIM), 0.1)
    da_subln_g = 1.0 + nrm(ks[11], (N_DA, 2 * DA_HEAD_DIM), 0.02)
    pool_w = nrm(ks[12], (N_POOL, POOL_GROUPS, POOL_GROUP_DIM, POOL_GROUP_DIM),
                 POOL_GROUP_DIM ** -0.5 * DEEPNORM_BETA)
    pool_b = nrm(ks[13], (N_POOL, POOL_GROUPS, POOL_GROUP_DIM), 0.02)
    pool_scale = 1.0 + nrm(ks[14], (N_POOL, D), 0.02)
    ret_w_qkvg = nrm(ks[15], (N_RET, D, 2 * RET_HEADS * RET_QK_DIM + 2 * RET_HEADS * RET_V_DIM),
                     D ** -0.5)
    ret_w_o = nrm(ks[16], (N_RET, RET_HEADS * RET_V_DIM, D),
                  (RET_HEADS * RET_V_DIM) ** -0.5 * DEEPNORM_BETA)
    xa_wq = nrm(ks[17], (DEPTH, D, D), D ** -0.5)
    xa_wkv = nrm(ks[18], (DEPTH, D, 2 * D), D ** -0.5)
    xa_wo = nrm(ks[19], (DEPTH, D, D), D ** -0.5 * DEEPNORM_BETA)
    return {"x": x, "mem": mem, "positions": positions,
            "ffn_w_in": ffn_w_in, "ffn_w_out": ffn_w_out, "ln_g": ln_g, "ln_b": ln_b,
            "da_w_qkv": da_w_qkv, "da_w_o": da_w_o, "da_lam_q": da_lam_q,
            "da_lam_k": da_lam_k, "da_subln_g": da_subln_g,
            "pool_w": pool_w, "pool_b": pool_b, "pool_scale": pool_scale,
            "ret_w_qkvg": ret_w_qkvg, "ret_w_o": ret_w_o,
            "xa_wq": xa_wq, "xa_wkv": xa_wkv, "xa_wo": xa_wo}


def reference(x, mem, positions, ffn_w_in, ffn_w_out, ln_g, ln_b,
              da_w_qkv, da_w_o, da_lam_q, da_lam_k, da_subln_g,
              pool_w, pool_b, pool_scale, ret_w_qkvg, ret_w_o,
              xa_wq, xa_wkv, xa_wo):
    pos = positions.astype(jnp.float32)[..., None]
    da_inv = 1.0 / (ROPE_THETA ** (jnp.arange(0, DA_HEAD_DIM, 2, dtype=jnp.float32) / DA_HEAD_DIM))
    da_ang = pos * da_inv[None, None, :]
    da_cos, da_sin = jnp.cos(da_ang), jnp.sin(da_ang)
    ret_inv = 1.0 / (ROPE_THETA ** jnp.linspace(0.0, 1.0, RET_QK_DIM // 2, dtype=jnp.float32))
    ret_ang = pos * ret_inv[None, None, :]
    ret_cos, ret_sin = jnp.cos(ret_ang), jnp.sin(ret_ang)
    a = DEEPNORM_ALPHA
    for i in range(DEPTH):
        m, j = i % N_MIXERS, i // N_MIXERS
        x = layer_norm(a * x + 0.5 * swiglu_ffn(x, ffn_w_in[i, 0], ffn_w_out[i, 0]),
                       ln_g[i, 0], ln_b[i, 0])
        if m == 0:
            lam_init = 0.8 - 0.6 * math.exp(-0.3 * i)
            h = diff_attention(x, da_w_qkv[j], da_w_o[j], da_lam_q[j], da_lam_k[j],
                               da_subln_g[j], da_cos, da_sin, lam_init)
        elif m == 1:
            h = pool_mixer(x, pool_w[j], pool_b[j], pool_scale[j])
        else:
            h = retention(x, ret_w_qkvg[j], ret_w_o[j], ret_cos, ret_sin)
        x = layer_norm(a * x + h, ln_g[i, 1], ln_b[i, 1])
        x = layer_norm(a * x + memory_cross_attention(x, mem, xa_wq[i], xa_wkv[i], xa_wo[i]),
                       ln_g[i, 2], ln_b[i, 2])
        x = layer_norm(a * x + 0.5 * swiglu_ffn(x, ffn_w_in[i, 1], ffn_w_out[i, 1]),
                       ln_g[i, 3], ln_b[i, 3])
    return x
```

```python
import math
from contextlib import ExitStack

import numpy as np
import concourse.bass as bass
import concourse.mybir as mybir
from concourse.bass_utils import run_bass_kernel_spmd

F32 = mybir.dt.float32
BF16 = mybir.dt.bfloat16
I32 = mybir.dt.int32
AF = mybir.ActivationFunctionType
ALU = mybir.AluOpType
AX = mybir.AxisListType
DSZ = {F32: 4, BF16: 2, I32: 4}
ENGS = ("pe", "act", "dve", "pool", "sp")

S = 2048
D = 1024
NT = 16
DEPTH = 4
FF = 2816
NJ = 22
MEM = 256
ALPHA = (2 * DEPTH) ** 0.25
LN_EPS = 1e-5
EPS_S = LN_EPS / (ALPHA * ALPHA)


def ap_interval(ap):
    sz = DSZ[ap.dtype]
    pat = ap.ap
    pstep = pat[0][0]
    off = int(ap.offset)
    lo = off % pstep if pstep > 0 else off
    span = 0
    for st, cnt in pat[1:]:
        span += abs(st) * (cnt - 1)
    lo_b, hi_b = lo * sz, (lo + span + 1) * sz
    if ap.tensor.name == "ps":
        lo_b = (lo_b // 2048) * 2048
        hi_b = ((hi_b + 2047) // 2048) * 2048
    return ap.tensor.name, lo_b, hi_b


class Prog:
    def __init__(self, nc):
        self.nc = nc
        self.ops = {e: [] for e in ENGS}
        self.cnt = {e: 0 for e in ENGS}
        self.seen = {e: {} for e in ENGS}
        self.rec = {}
        self.dma_cnt = {}
        self.dma_sem_names = []

    def _overl(self, key):
        name, lo, hi = key
        return [r for r in self.rec.get(name, ()) if r[0] < hi and lo < r[1]]

    def _deps(self, eng, reads, writes):
        deps = {}

        def add(k, v):
            if k == "pe" and eng == "pe":
                return
            if deps.get(k, 0) < v:
                deps[k] = v

        for key in reads:
            for r in self._overl(key):
                if r[2] is not None:
                    add(*r[2])
                if key[0] == "ps":
                    for k, v in r[3].items():
                        if k != eng:
                            add(k, v)
        for key in writes:
            for r in self._overl(key):
                if r[2] is not None:
                    add(*r[2])
                for k, v in r[3].items():
                    add(k, v)
        out = []
        seen = self.seen[eng]
        for k, v in deps.items():
            if seen.get(k, 0) < v:
                seen[k] = v
                out.append((k, v))
        return out

    def _update(self, ident, reads, writes):
        for name, lo, hi in reads:
            lst = self.rec.setdefault(name, [])
            covered = False
            for r in lst:
                if r[0] < hi and lo < r[1]:
                    if r[3].get(ident[0], 0) < ident[1]:
                        r[3][ident[0]] = ident[1]
                    if r[0] <= lo and hi <= r[1]:
                        covered = True
            if not covered:
                lst.append([lo, hi, None, {ident[0]: ident[1]}])
        for name, lo, hi in writes:
            lst = self.rec.setdefault(name, [])
            lst[:] = [r for r in lst if not (lo <= r[0] and r[1] <= hi)]
            lst.append([lo, hi, ident, {}])

    def op(self, eng, fn, reads=(), writes=(), signal=True):
        rk = [ap_interval(a) for a in reads]
        wk = [ap_interval(a) for a in writes]
        waits = self._deps(eng, rk, wk)
        if signal:
            self.cnt[eng] += 1
            ident = (eng, self.cnt[eng])
        else:
            ident = (eng, self.cnt[eng] + 1)
        self._update(ident, rk, wk)
        self.ops[eng].append((waits, fn, eng if signal else None, 1))

    def dma(self, eng, slot, pairs, reads=(), writes=()):
        rk = [ap_interval(a) for a in reads]
        wk = [ap_interval(a) for a in writes]
        key = "dma:" + slot
        waits = self._deps(eng, rk, wk)
        if key not in self.dma_cnt:
            self.dma_cnt[key] = 0
            self.dma_sem_names.append(key)
        self.dma_cnt[key] += 16 * len(pairs)
        ident = (key, self.dma_cnt[key])
        self._update(ident, rk, wk)
        for n, (o, i) in enumerate(pairs):
            self.ops[eng].append((waits if n == 0 else [],
                                  lambda e, o=o, i=i: e.dma_start(out=o, in_=i), key, 16))
        return ident

    def wait_all(self, eng, idents):
        waits = []
        for k, v in idents:
            if self.seen[eng].get(k, 0) < v:
                self.seen[eng][k] = v
                waits.append((k, v))
        self.ops[eng].append((waits, None, None, 0))

    def check(self):
        pos = {e: 0 for e in ENGS}
        val = {}
        prog = True
        while prog:
            prog = False
            for e in ENGS:
                lst = self.ops[e]
                while pos[e] < len(lst):
                    waits, fn, sk, inc = lst[pos[e]]
                    if all(val.get(k, 0) >= v for k, v in waits):
                        if sk is not None:
                            val[sk] = val.get(sk, 0) + inc
                        pos[e] += 1
                        prog = True
                    else:
                        break
        for e in ENGS:
            if pos[e] != len(self.ops[e]):
                waits = self.ops[e][pos[e]][0]
                raise RuntimeError(f"deadlock: engine {e} stuck at op {pos[e]}/{len(self.ops[e])} waits={waits}")

    def emit(self):
        nc = self.nc
        self.check()
        with ExitStack() as es:
            sems = {}
            for e in ("pe", "act", "dve", "pool"):
                sems[e] = es.enter_context(nc.semaphore("s_" + e))
            for k in self.dma_sem_names:
                sems[k] = es.enter_context(nc.semaphore("s_" + k.replace(":", "_")))
            block = es.enter_context(nc.Block())

            def run(e, lst):
                for waits, fn, sk, inc in lst:
                    for k, v in waits:
                        e.wait_ge(sems[k], v)
                    if fn is None:
                        continue
                    ins = fn(e)
                    if sk is not None:
                        ins.then_inc(sems[sk], inc)

            @block.tensor
            def _(e):
                run(e, self.ops["pe"])

            @block.scalar
            def _(e):
                run(e, self.ops["act"])

            @block.vector
            def _(e):
                run(e, self.ops["dve"])

            @block.gpsimd
            def _(e):
                run(e, self.ops["pool"])

            @block.sync
            def _(e):
                run(e, self.ops["sp"])


WEIGHT_SPECS = [
    ("ffn_w_in", [4, 2, 1024, 5632]), ("ffn_w_out", [4, 2, 2816, 1024]),
    ("ln_g", [4, 4, 1024]), ("ln_b", [4, 4, 1024]),
    ("da_w_qkv", [2, 1024, 3072]), ("da_w_o", [2, 1024, 1024]),
    ("da_lam_q", [2, 2, 64]), ("da_lam_k", [2, 2, 64]), ("da_subln_g", [2, 128]),
    ("pool_w", [1, 4, 256, 256]), ("pool_b", [1, 4, 256]), ("pool_scale", [1, 1024]),
    ("ret_w_qkvg", [1, 1024, 6144]), ("ret_w_o", [1, 2048, 1024]),
    ("xa_wq", [4, 1024, 1024]), ("xa_wkv", [4, 1024, 2048]), ("xa_wo", [4, 1024, 1024]),
]

DEBUG_NO_LN = False
DA_STOP = 0
ARENA_BYTES = 109056
OFF_WST = 72192
OFF_LNP = 88576
OFF_XB = 96768
OFF_XN = 100864


def build_program(plan):
    nc = bass.Bass("TRN2", target_bir_lowering=False)
    dr = {}
    dr["x"] = nc.dram_tensor("x", [S, D], F32, kind="ExternalInput").ap()
    dr["mem"] = nc.dram_tensor("mem", [MEM, D], F32, kind="ExternalInput").ap()
    dr["positions"] = nc.dram_tensor("positions", [1, S], I32, kind="ExternalInput").ap()
    dr["ident"] = nc.dram_tensor("ident", [128, 128], F32, kind="ExternalInput").ap()
    dr["trimask"] = nc.dram_tensor("trimask", [128, 128], F32, kind="ExternalInput").ap()
    dr["dacst"] = nc.dram_tensor("dacst", [128, 2], F32, kind="ExternalInput").ap()
    dr["retcst"] = nc.dram_tensor("retcst", [128, 16], F32, kind="ExternalInput").ap()
    dr["retdt"] = nc.dram_tensor("retdt", [4, 128, 128], F32, kind="ExternalInput").ap()
    dr["poolmat"] = nc.dram_tensor("poolmat", [8, 128, 128], F32, kind="ExternalInput").ap()
    dr["poolcnt"] = nc.dram_tensor("poolcnt", [1, 4, 128], F32, kind="ExternalInput").ap()
    wshape = dict(WEIGHT_SPECS)

    def W(name):
        if name not in dr:
            dr[name] = nc.dram_tensor(name, wshape[name], F32, kind="ExternalInput").ap()
        return dr[name]
    y = nc.dram_tensor("y", [S, D], F32, kind="ExternalOutput").ap()
    P = Prog(nc)

    with ExitStack() as es:
        def T(name, shape, dt):
            return es.enter_context(nc.sbuf_tensor(name, shape, dt))

        xres = T("xres", [128, NT, D], F32)
        xT = T("xT", [128, 8, S], BF16)
        ident = T("ident_sb", [128, 128], BF16)
        small = T("small", [128, 128], F32)
        xst = T("xst", [128, 2, 16], F32)
        dsm = T("dsm", [128, 8], F32)
        dsc = T("dsc", [128, 16], F32)
        dacst = T("dacst_sb", [128, 2], F32)
        retcst = T("retcst_sb", [128, 16], F32)
        retdt = T("retdt_sb", [128, 4, 128], F32)
        attb_sb = T("attb_sb", [128, 2, 128], BF16)
        tri = T("tri_sb", [128, 128], BF16)
        ones = T("ones", [128, 128], BF16)
        mhalf = T("mhalf", [128, 4], F32)
        arena = T("arena", [128, ARENA_BYTES // 2], BF16)
        ps = es.enter_context(nc.psum_tensor("ps", [128, 8, 512], F32))

        def av(off, shape, dt):
            n = int(np.prod(shape)) * DSZ[dt]
            a = arena[:, off // 2:(off + n) // 2]
            if dt != BF16:
                a = a.bitcast(dt)
            if len(shape) == 2:
                a = a.rearrange("p (a b) -> p a b", b=shape[1])
            elif len(shape) == 3:
                a = a.rearrange("p (a b c) -> p a b c", b=shape[1], c=shape[2])
            return a

        wst = av(OFF_WST, (4, 2048), BF16)
        lnp = av(OFF_LNP, (2, D), F32)
        wres = av(49152, (11, D), BF16)
        st = {"wst": 0, "pa": 0, "pb": 0}

        def next_wst():
            s = st["wst"]
            st["wst"] = (s + 1) % 4
            return s

        def bankA():
            b = st["pa"]
            st["pa"] = (b + 1) % 4
            return b

        def bankB():
            b = st["pb"]
            st["pb"] = (b + 1) % 4
            return 4 + b

        def mm(out, lhsT, rhs, start, stop):
            P.op("pe", lambda e: e.matmul(out, lhsT, rhs, start=start, stop=stop),
                 reads=[lhsT, rhs], writes=[out], signal=stop)

        P.dma("pool", "ident", [(ident[:], dr["ident"])], writes=[ident[:]])
        P.op("dve", lambda e: e.memset(mhalf[:], -0.5), writes=[mhalf[:]])
        P.op("dve", lambda e: e.memset(ones[:], 1.0), writes=[ones[:]])
        P.dma("pool", "tri", [(tri[:], dr["trimask"])], writes=[tri[:]])
        P.dma("sp", "dacst", [(dacst[:], dr["dacst"])], writes=[dacst[:]])
        P.dma("sp", "retcst", [(retcst[:], dr["retcst"]), (retdt[:], dr["retdt"].rearrange("h p n -> p h n"))],
              writes=[retcst[:], retdt[:]])

        xb_bufs = [av(OFF_XB + i * 2048, (D,), BF16) for i in range(2)]
        xn_bufs = [av(OFF_XN + i * 4096, (D,), F32) for i in range(2)]
        cnt = {"xb": 0, "xn": 0, "sm": 0}

        def load_ln(i, k):
            P.dma("sp", "lnp", [(lnp[:, 0, :], W("ln_g")[i, k:k + 1, :].partition_broadcast(128)),
                                (lnp[:, 1, :], W("ln_b")[i, k:k + 1, :].partition_broadcast(128))],
                  writes=[lnp[:]])

        ln_q = []
        final_store = {"on": False, "idents": []}
        yv = y.rearrange("(t p) d -> p t d", p=128)

        def _s1(en):
            t = en["t"]
            q = cnt["sm"] % 8
            cnt["sm"] += 1
            stf = small[:, q * 16:q * 16 + 12]
            stt = stf.rearrange("p (a b) -> p a b", b=6)
            mv = small[:, q * 16 + 12:q * 16 + 14]
            rs = small[:, q * 16 + 14:q * 16 + 15]
            nmr = small[:, q * 16 + 15:q * 16 + 16]
            en.update(mv=mv, rs=rs, nmr=nmr)
            for h in range(2):
                src = xres[:, t, h * 512:(h + 1) * 512]
                P.op("dve", lambda e, h=h, src=src: e.bn_stats(stt[:, h, :], src), reads=[src], writes=[stt[:, h, :]])
            P.op("dve", lambda e: e.bn_aggr(mv, stf), reads=[stf], writes=[mv])
            P.op("dve", lambda e: e.tensor_scalar(rs, mv[:, 1:2], EPS_S, None, ALU.add), reads=[mv], writes=[rs])
            P.op("pool", lambda e: e.tensor_tensor(rs, rs, mhalf[:, 0:1], ALU.pow), reads=[rs, mhalf[:, 0:1]], writes=[rs])

        def _s2(en):
            t, mv, rs, nmr = en["t"], en["mv"], en["rs"], en["nmr"]
            P.op("dve", lambda e: e.tensor_scalar(nmr, mv[:, 0:1], rs, -1.0, ALU.mult, ALU.mult), reads=[mv, rs], writes=[nmr])
            xn = xn_bufs[cnt["xn"] % 2]
            cnt["xn"] += 1
            en["xn"] = xn
            P.op("act", lambda e: e.activation(xn[:], xres[:, t, :], AF.Identity, bias=nmr, scale=rs),
                 reads=[xres[:, t, :], nmr, rs], writes=[xn[:]])

        def _s3(en):
            t = en["t"]
            xn = en["xn"]
            P.op("dve", lambda e: e.tensor_tensor(xn[:], xn[:], lnp[:, 0, :], ALU.mult), reads=[xn[:], lnp[:, 0, :]], writes=[xn[:]])
            P.op("dve", lambda e: e.tensor_tensor(xres[:, t, :], xn[:], lnp[:, 1, :], ALU.add),
                 reads=[xn[:], lnp[:, 1, :]], writes=[xres[:, t, :]])
            if final_store["on"]:
                if t % 4 == 3:
                    t4 = t // 4
                    final_store["idents"].append(
                        P.dma("sp", f"yout{t4}", [(yv[:, t4 * 4:(t4 + 1) * 4, :], xres[:, t4 * 4:(t4 + 1) * 4, :])],
                              reads=[xres[:, t4 * 4:(t4 + 1) * 4, :]]))
                return "done"

        def _idle(en):
            pass

        def _s3b(en):
            t = en["t"]
            xb = xb_bufs[cnt["xb"] % 2]
            cnt["xb"] += 1
            en["xb"] = xb
            P.op("act", lambda e: e.copy(xb[:], xres[:, t, :]), reads=[xres[:, t, :]], writes=[xb[:]])

        def _s4a(en):
            xb = en["xb"]
            b = bankA()
            en["b"] = b
            pt = ps[:, b, :].bitcast(BF16).rearrange("p (c n) -> p c n", n=128)
            for c in range(8):
                P.op("pe", lambda e, c=c: e.transpose(pt[:, c, :], xb[:, c * 128:(c + 1) * 128], ident[:]),
                     reads=[xb[:, c * 128:(c + 1) * 128], ident[:]], writes=[ps[:, b, :]], signal=(c == 7))

        def _s4b(en):
            t, b = en["t"], en["b"]
            pt = ps[:, b, :].bitcast(BF16).rearrange("p (c n) -> p c n", n=128)
            dst = xT[:, :, t * 128:(t + 1) * 128]
            if en["n"] % 2:
                P.op("act", lambda e: e.copy(dst, pt), reads=[ps[:, b, :]], writes=[dst])
            else:
                P.op("dve", lambda e: e.tensor_copy(dst, pt), reads=[ps[:, b, :]], writes=[dst])

        def _s4(en):
            _s4a(en)
            _s4b(en)

        STAGES = (_s1, _s2, _s3, _idle, _s3b, _s4)
        NST = len(STAGES)

        def _ln_advance():
            for en in list(ln_q):
                r = STAGES[en["stage"]](en)
                en["stage"] = NST if r == "done" else en["stage"] + 1
            ln_q[:] = [en for en in ln_q if en["stage"] < NST]

        def ln_tail(t, cast_only=False):
            cnt["ln"] = cnt.get("ln", 0) + 1
            ln_q.append({"t": t, "stage": 4 if cast_only else 0, "n": cnt["ln"]})
            _ln_advance()

        def ln_flush():
            while ln_q:
                _ln_advance()

        def accum(t, h, bank, scale):
            dst = xres[:, t, h * 512:(h + 1) * 512]
            P.op("dve", lambda e: e.scalar_tensor_tensor(dst, ps[:, bank, :], scale, dst, ALU.mult, ALU.add),
                 reads=[ps[:, bank, :], dst], writes=[dst])

        def make_stream(loaders, look=3):
            state = {"n": 0, "slots": []}

            def get(n):
                while state["n"] < min(len(loaders), n + 1 + look):
                    sl = next_wst()
                    loaders[state["n"]](sl)
                    state["slots"].append(sl)
                    state["n"] += 1
                return state["slots"][n]
            get.peek = lambda n: state["slots"][n]
            return get

        def col_loader(src):
            def f(sl):
                P.dma("pool", f"wst{sl}", [(wst[:, sl, :].rearrange("p (c n) -> p c n", c=8),
                                            src.rearrange("(c p) n -> p c n", p=128))], writes=[wst[:, sl, :]])
            return f

        def ffn(i, k):
            w_in = W("ffn_w_in")[i, k]
            w_out = W("ffn_w_out")[i, k]
            actT = av(0, (11, S), BF16)
            sil = [av(45056 + q * 2048, (512,), F32) for q in range(2)]
            load_ln(i, 0 if k == 0 else 3)
            pending = []

            def win_loader(j):
                def f(sl):
                    wv = wst[:, sl, :].rearrange("p (g c n) -> p g c n", g=2, c=8)
                    P.dma("pool", f"wst{sl}",
                          [(wv[:, 0], w_in[:, j * 128:(j + 1) * 128].rearrange("(c p) n -> p c n", p=128)),
                           (wv[:, 1], w_in[:, FF + j * 128:FF + (j + 1) * 128].rearrange("(c p) n -> p c n", p=128))],
                          writes=[wst[:, sl, :]])
                return f
            get = make_stream([win_loader(j) for j in range(NJ)])
            for bi, (j0, j1) in enumerate(((0, 11), (11, 22))):
                nj = j1 - j0
                wo_v = w_out[j0 * 128:j1 * 128, :].rearrange("(c p) n -> p c n", p=128)
                P.dma("pool", "wres", [(wres[:, 0:6, :], wo_v[:, 0:6, :]), (wres[:, 6:nj, :], wo_v[:, 6:nj, :])],
                      writes=[wres[:, 0:nj, :]])
                for j in range(j0, j1):
                    s = get(j)
                    wv = wst[:, s, :].rearrange("p (g c n) -> p g c n", g=2, c=8)
                    for tg in range(4):
                        bg, bu = bankA(), bankA()
                        xs = lambda c: xT[:, c, tg * 512:(tg + 1) * 512]
                        for c in range(8):
                            mm(ps[:, bg, :], wv[:, 0, c, :], xs(c), c == 0, c == 7)
                        for c in range(8):
                            mm(ps[:, bu, :], wv[:, 1, c, :], xs(c), c == 0, c == 7)
                        sl = sil[(j * 4 + tg) % 2]
                        P.op("act", lambda e, sl=sl, bg=bg: e.activation(sl, ps[:, bg, :], AF.Silu),
                             reads=[ps[:, bg, :]], writes=[sl])
                        dst = actT[:, j - j0, tg * 512:(tg + 1) * 512]
                        P.op("dve", lambda e, dst=dst, bu=bu, sl=sl: e.tensor_tensor(dst, ps[:, bu, :], sl, ALU.mult),
                             reads=[ps[:, bu, :], sl], writes=[dst])
                for t in range(NT):
                    for h in range(2):
                        b = bankB()
                        for jj in range(nj):
                            mm(ps[:, b, :], actT[:, jj, t * 128:(t + 1) * 128], wres[:, jj, h * 512:(h + 1) * 512],
                               jj == 0, jj == nj - 1)
                        accum(t, h, b, 0.5 / ALPHA)
                    if bi == 1:
                        ln_tail(t)
            ln_flush()

        pre_hook = {"fn": None}

        def call_pre_hook():
            fn, pre_hook["fn"] = pre_hook["fn"], None
            if fn is not None:
                fn()

        def prep_mem(memT):
            mf = av(OFF_XN, (2, D), F32)
            mb = av(OFF_XB, (2, D), BF16)
            P.dma("sp", "memin", [(mf, dr["mem"].rearrange("(t p) d -> p t d", p=128))], writes=[mf])
            P.op("act", lambda e: e.copy(mb, mf), reads=[mf], writes=[mb])
            for mt in range(2):
                b = bankA()
                pt = ps[:, b, :].bitcast(BF16).rearrange("p (c n) -> p c n", n=128)
                for c in range(8):
                    src = mb[:, mt, c * 128:(c + 1) * 128]
                    P.op("pe", lambda e, c=c, src=src, pt=pt: e.transpose(pt[:, c, :], src, ident[:]),
                         reads=[src, ident[:]], writes=[ps[:, b, :]], signal=(c == 7))
                dst = memT[:, :, mt * 128:(mt + 1) * 128]
                P.op("act", lambda e, dst=dst, pt=pt: e.copy(dst, pt), reads=[ps[:, b, :]], writes=[dst])

        def xa(i):
            wq, wkv, wo = W("xa_wq")[i], W("xa_wkv")[i], W("xa_wo")[i]
            qT = av(0, (8, S), BF16)
            kT = av(32768, (8, MEM), BF16)
            vx = av(36864, (2, D), BF16)
            pf = [av(40960 + q * 4096, (4, 256), F32) for q in range(2)]
            pn = [av(49152 + q * 2048, (4, 256), BF16) for q in range(2)]
            pT = av(53248, (8, 512), BF16)
            oT = av(61440, (8, 512), BF16)
            wo_sb = av(OFF_WST, (8, D), BF16)
            memT = av(53248, (8, MEM), BF16)
            prep_mem(memT)
            loaders = [col_loader(wkv[:, cc * 256:(cc + 1) * 256]) for cc in range(8)]
            loaders += [col_loader(wq[:, cc * 256:(cc + 1) * 256]) for cc in range(4)]
            get = make_stream(loaders)
            ev = {"n": 0}

            def evac(dst, src):
                ev["n"] += 1
                if ev["n"] % 2:
                    P.op("act", lambda e: e.copy(dst, src), reads=[src], writes=[dst])
                else:
                    P.op("dve", lambda e: e.tensor_copy(dst, src), reads=[src], writes=[dst])

            for cc in range(4):
                wv = wst[:, get(cc), :].rearrange("p (c n) -> p c n", c=8)
                for sub in range(2):
                    b = bankA()
                    for kc in range(8):
                        mm(ps[:, b, 0:MEM], wv[:, kc, sub * 128:(sub + 1) * 128], memT[:, kc, :], kc == 0, kc == 7)
                    evac(kT[:, cc * 2 + sub, :], ps[:, b, 0:MEM])
            for cc in range(4):
                wv = wst[:, get(4 + cc), :].rearrange("p (c n) -> p c n", c=8)
                for mt in range(2):
                    b = bankA()
                    for kc in range(8):
                        mm(ps[:, b, 0:256], memT[:, kc, mt * 128:(mt + 1) * 128], wv[:, kc, :], kc == 0, kc == 7)
                    evac(vx[:, mt, cc * 256:(cc + 1) * 256], ps[:, b, 0:256])
            yield
            load_ln(i, 2)
            for cc in range(4):
                wv = wst[:, get(8 + cc), :].rearrange("p (c n) -> p c n", c=8)
                for sub in range(2):
                    for tg in range(4):
                        b = bankA()
                        for kc in range(8):
                            mm(ps[:, b, :], wv[:, kc, sub * 128:(sub + 1) * 128], xT[:, kc, tg * 512:(tg + 1) * 512],
                               kc == 0, kc == 7)
                        evac(qT[:, cc * 2 + sub, tg * 512:(tg + 1) * 512], ps[:, b, :])
            wo_v = wo.rearrange("(c p) n -> p c n", p=128)
            P.dma("pool", "wo", [(wo_sb[:, 0:4, :], wo_v[:, 0:4, :]), (wo_sb[:, 4:8, :], wo_v[:, 4:8, :])],
                  writes=[wo_sb])
            scale = 1.0 / 16.0

            def sc_view(t):
                b2 = 4 + 2 * (t % 2)
                return ps[:, b2:b2 + 2, :].rearrange("p b (h m) -> p (b h) m", m=MEM)

            def stage_scores(t):
                sc = sc_view(t)
                for h in range(4):
                    for dc in range(2):
                        mm(sc[:, h, :], qT[:, 2 * h + dc, t * 128:(t + 1) * 128], kT[:, 2 * h + dc, :], dc == 0, dc == 1)

            def stage_softmax(t):
                q = t % 2
                sc = sc_view(t)
                mx, nb, sm, rs = (xst[:, q, 4 * n:4 * n + 4] for n in range(4))
                P.op("dve", lambda e: e.tensor_reduce(mx, sc, AX.X, ALU.max), reads=[sc], writes=[mx])
                P.op("dve", lambda e: e.tensor_scalar(nb, mx, -scale, None, ALU.mult), reads=[mx], writes=[nb])
                for h in range(4):
                    P.op("act", lambda e, h=h: e.activation(
                        pf[q][:, h, :], sc[:, h, :], AF.Exp, bias=nb[:, h:h + 1], scale=scale, accum_out=sm[:, h:h + 1]),
                        reads=[sc[:, h, :], nb], writes=[pf[q][:, h, :], sm[:, h:h + 1]])
                P.op("dve", lambda e: e.reciprocal(rs, sm), reads=[sm], writes=[rs])
                for h in range(4):
                    P.op("act", lambda e, h=h: e.activation(pn[q][:, h, :], pf[q][:, h, :], AF.Identity, scale=rs[:, h:h + 1]),
                         reads=[pf[q][:, h, :], rs], writes=[pn[q][:, h, :]])

            def stage_transpose(t):
                q = t % 2
                tl = t % 4
                b = bankA()
                pt = ps[:, b, :].bitcast(BF16).rearrange("p (c n) -> p c n", n=128)
                for hm in range(8):
                    src = pn[q][:, hm // 2, (hm % 2) * 128:(hm % 2 + 1) * 128]
                    P.op("pe", lambda e, hm=hm, src=src: e.transpose(pt[:, hm, :], src, ident[:]),
                         reads=[src, ident[:]], writes=[ps[:, b, :]], signal=(hm == 7))
                evac(pT[:, :, tl * 128:(tl + 1) * 128], pt)

            def group_tail(tg):
                for c in range(8):
                    b = bankA()
                    for mc in range(2):
                        mm(ps[:, b, :], vx[:, mc, c * 128:(c + 1) * 128], pT[:, (c // 2) * 2 + mc, :], mc == 0, mc == 1)
                    evac(oT[:, c, :], ps[:, b, :])
                for tl in range(4):
                    tt = tg * 4 + tl
                    for h in range(2):
                        b = bankA()
                        for c in range(8):
                            mm(ps[:, b, :], oT[:, c, tl * 128:(tl + 1) * 128], wo_sb[:, c, h * 512:(h + 1) * 512], c == 0, c == 7)
                        accum(tt, h, b, 1.0 / ALPHA)
                    ln_tail(tt)

            stage_scores(0)
            for t in range(NT + 1):
                if t < NT:
                    stage_softmax(t)
                if t + 1 < NT:
                    stage_scores(t + 1)
                if t >= 1:
                    stage_transpose(t - 1)
                    if (t - 1) % 4 == 3:
                        group_tail((t - 1) // 4)
            ln_flush()

        def poolmix(i):
            j = i // 3
            pmat = av(0, (8, 128), BF16)
            btab = av(2048, (D,), F32)
            stab = av(6144, (D,), F32)
            icnt = av(10240, (4, 128), F32)
            xbp = [av(12288 + q * 2048, (D,), BF16) for q in range(3)]
            poT = [av(18432 + q * 2048, (8, 128), BF16) for q in range(2)]
            tmp = av(22528, (D,), F32)
            tms = av(26624, (128,), F32)
            load_ln(i, 1)
            sl = next_wst()
            pw = wst[:, sl, :].rearrange("p (g c n) -> p g c n", g=4, c=2)
            P.dma("pool", f"wst{sl}", [(pw, W("pool_w")[j].rearrange("g (c p) n -> p g c n", p=128))], writes=[wst[:, sl, :]])
            P.dma("pool", "pmat", [(pmat, dr["poolmat"].rearrange("m p n -> p m n"))], writes=[pmat])
            P.dma("sp", "ptab", [(btab, W("pool_b")[j:j + 1].rearrange("o g d -> o (g d)").partition_broadcast(128)),
                                 (stab, W("pool_scale")[j:j + 1, :].partition_broadcast(128)),
                                 (av(10240, (512,), F32), dr["poolcnt"].rearrange("o g n -> o (g n)").partition_broadcast(128))],
                  writes=[btab, stab, icnt])
            P.op("dve", lambda e: e.tensor_scalar(stab, stab, 1.0 / ALPHA, None, ALU.mult), reads=[stab], writes=[stab])
            P.op("dve", lambda e: e.tensor_tensor(btab, btab, stab, ALU.mult), reads=[btab, stab], writes=[btab])
            wins = (2, 4, 8, 16)
            pending = []
            def win_stage(t):
                xc = xbp[t % 3]
                xp = xbp[(t - 1) % 3]
                P.op("act", lambda e: e.copy(xc, xres[:, t, :]), reads=[xres[:, t, :]], writes=[xc])
                po = poT[t % 2]
                for half in range(2):
                    b = bankA()
                    for cl in range(4):
                        c = half * 4 + cl
                        g = c // 2
                        out = ps[:, b, cl * 128:(cl + 1) * 128]
                        mm(out, xc[:, c * 128:(c + 1) * 128], pmat[:, 2 * g, :], True, t == 0)
                        if t > 0:
                            mm(out, xp[:, c * 128:(c + 1) * 128], pmat[:, 2 * g + 1, :], False, True)
                    for cl in range(4):
                        c = half * 4 + cl
                        g = c // 2
                        out = ps[:, b, cl * 128:(cl + 1) * 128]
                        xt = xT[:, c, t * 128:(t + 1) * 128]
                        if t == 0:
                            P.op("dve", lambda e, out=out, g=g: e.tensor_tensor(tms, out, icnt[:, g, :], ALU.mult),
                                 reads=[out, icnt[:, g, :]], writes=[tms])
                            P.op("dve", lambda e, c=c, xt=xt: e.tensor_tensor(po[:, c, :], tms, xt, ALU.subtract),
                                 reads=[tms, xt], writes=[po[:, c, :]])
                        else:
                            P.op("dve", lambda e, c=c, xt=xt, out=out, g=g: e.scalar_tensor_tensor(
                                po[:, c, :], out, 1.0 / wins[g], xt, ALU.mult, ALU.subtract),
                                reads=[out, xt], writes=[po[:, c, :]])

            def proj_stage(t):
                po = poT[t % 2]
                q = t % 2
                b2 = 4 + 2 * q
                yv2 = ps[:, b2:b2 + 2, :].rearrange("p b (g n) -> p (b g) n", n=256)
                for g in range(4):
                    for kc in range(2):
                        mm(yv2[:, g, :], po[:, 2 * g + kc, :], pw[:, g, kc, :], kc == 0, kc == 1)
                yflat = ps[:, b2:b2 + 2, :].rearrange("p b n -> p (b n)")
                P.op("dve", lambda e: e.tensor_tensor(tmp, yflat, stab, ALU.mult), reads=[yflat, stab], writes=[tmp])
                P.op("dve", lambda e: e.tensor_tensor(tmp, tmp, btab, ALU.add), reads=[tmp, btab], writes=[tmp])
                P.op("dve", lambda e: e.tensor_tensor(xres[:, t, :], xres[:, t, :], tmp, ALU.add),
                     reads=[xres[:, t, :], tmp], writes=[xres[:, t, :]])
                ln_tail(t)

            for t in range(NT + 1):
                if t < NT:
                    win_stage(t)
                if t >= 1:
                    proj_stage(t - 1)
            ln_flush()

        TWO_PI = 2.0 * math.pi
        C1 = 6.28125
        C2 = TWO_PI - C1

        def rope_tables(cosT, sinT, invf, sgn, tmp_i, tmp_a, tmp_b):
            P.dma("sp", "posin", [(tmp_i, dr["positions"].partition_broadcast(128))], writes=[tmp_i])
            ang = tmp_a
            P.op("dve", lambda e: e.tensor_copy(ang, tmp_i), reads=[tmp_i], writes=[ang])
            P.op("dve", lambda e: e.tensor_scalar(ang, ang, invf, None, ALU.mult), reads=[ang, invf], writes=[ang])
            kf = tmp_b
            ki = tmp_i
            P.op("dve", lambda e: e.tensor_scalar(kf, ang, 1.0 / TWO_PI, None, ALU.mult), reads=[ang], writes=[kf])
            P.op("dve", lambda e: e.tensor_copy(ki, kf), reads=[kf], writes=[ki])
            P.op("dve", lambda e: e.tensor_copy(kf, ki), reads=[ki], writes=[kf])
            P.op("dve", lambda e: e.scalar_tensor_tensor(ang, kf, -C1, ang, ALU.mult, ALU.add), reads=[kf, ang], writes=[ang])
            P.op("dve", lambda e: e.scalar_tensor_tensor(ang, kf, -C2, ang, ALU.mult, ALU.add), reads=[kf, ang], writes=[ang])

            def wrap(r):
                P.op("dve", lambda e: e.tensor_scalar(kf, r, math.pi, None, ALU.is_gt), reads=[r], writes=[kf])
                P.op("dve", lambda e: e.scalar_tensor_tensor(r, kf, -TWO_PI, r, ALU.mult, ALU.add), reads=[kf, r], writes=[r])
                P.op("dve", lambda e: e.tensor_scalar(kf, r, -math.pi, None, ALU.is_lt), reads=[r], writes=[kf])
                P.op("dve", lambda e: e.scalar_tensor_tensor(r, kf, TWO_PI, r, ALU.mult, ALU.add), reads=[kf, r], writes=[r])
            wrap(ang)
            P.op("act", lambda e: e.activation(sinT, ang, AF.Sin), reads=[ang], writes=[sinT])
            if sgn is not None:
                P.op("dve", lambda e: e.tensor_scalar(sinT, sinT, sgn, None, ALU.mult), reads=[sinT, sgn], writes=[sinT])
            P.op("dve", lambda e: e.tensor_scalar(ang, ang, 0.5 * math.pi, None, ALU.add), reads=[ang], writes=[ang])
            wrap(ang)
            P.op("act", lambda e: e.activation(cosT, ang, AF.Sin), reads=[ang], writes=[cosT])

        def diffattn(i):
            j = i // 3
            lam_init = 0.8 - 0.6 * math.exp(-0.3 * i)
            wqkv = W("da_w_qkv")[j]
            w_o = W("da_w_o")[j]
            onT = av(0, (4, S), BF16)
            qT = [av(16384 + q * 4096, (S,), BF16) for q in range(2)]
            kz = [[av(24576 + c * 4096, (S,), BF16) for c in range(2)],
                  [av(OFF_LNP + c * 4096, (S,), BF16) for c in range(2)]]
            vv = [av(32768 + q * 4160, (NT, 130), BF16) for q in range(2)]
            cosT = av(41088, (S,), F32)
            sinT = av(49280, (S,), F32)
            tA = av(57472, (512,), F32)
            tB = av(59520, (512,), F32)
            wo_sb = av(61568, (4, D), BF16)
            gtab = av(OFF_XB, (128,), F32)
            tt = av(OFF_XB + 512, (128,), F32)
            dds = [av(OFF_XB + 1024 + q * 512, (128,), F32) for q in range(2)]
            dn = [av(OFF_XB + 2048 + q * 256, (128,), BF16) for q in range(2)]
            junk = av(OFF_XB + 2560, (128,), BF16)
            accS = av(OFF_XN, (8, 130), F32)
            daw = av(OFF_WST, (5, 8, 128), BF16)
            eT = [av(OFF_WST + 10240 + q * 1024, (512,), BF16) for q in range(5)]
            rope_tables(cosT, sinT, dacst[:, 0:1], dacst[:, 1:2],
                        av(16384, (S,), I32), av(24576, (S,), F32), av(32768, (S,), F32))
            if DA_STOP:
                P.op("dve", lambda e: e.memset(onT, 0.0), writes=[onT])
            for q in range(2):
                for c in range(2):
                    P.op("dve", lambda e, q=q, c=c: e.memset(kz[q][c], 0.0), writes=[kz[q][c]])
            for q in range(2):
                P.op("dve", lambda e, q=q: e.memset(vv[q][:, :, 128:130], 1.0), writes=[vv[q]])
            lq = av(57472, (128,), F32)
            lk = av(57472 + 512, (128,), F32)
            P.dma("sp", "dalam", [(lq, W("da_lam_q")[j:j + 1].rearrange("o c d -> o (c d)").partition_broadcast(128)),
                                  (lk, W("da_lam_k")[j:j + 1].rearrange("o c d -> o (c d)").partition_broadcast(128)),
                                  (gtab, W("da_subln_g")[j:j + 1, :].partition_broadcast(128))],
                  writes=[lq, lk, gtab])
            P.op("dve", lambda e: e.tensor_tensor(lq, lq, lk, ALU.mult), reads=[lq, lk], writes=[lq])
            P.op("dve", lambda e: e.tensor_reduce(dsm[:, 0:2], lq.rearrange("p (c d) -> p c d", d=64), AX.X, ALU.add),
                 reads=[lq], writes=[dsm[:, 0:2]])
            P.op("act", lambda e: e.activation(dsm[:, 0:2], dsm[:, 0:2], AF.Exp), reads=[dsm[:, 0:2]], writes=[dsm[:, 0:2]])
            P.op("dve", lambda e: e.tensor_tensor(dsm[:, 2:3], dsm[:, 1:2], dsm[:, 0:1], ALU.subtract),
                 reads=[dsm[:, 0:2]], writes=[dsm[:, 2:3]])
            P.op("dve", lambda e: e.tensor_scalar(dsm[:, 2:3], dsm[:, 2:3], -lam_init, None, ALU.add),
                 reads=[dsm[:, 2:3]], writes=[dsm[:, 2:3]])
            P.op("dve", lambda e: e.tensor_scalar(gtab, gtab, 1.0 - lam_init, None, ALU.mult), reads=[gtab], writes=[gtab])
            nlam = dsm[:, 2:3]
            epsc = dsm[:, 5:6]
            P.op("dve", lambda e: e.memset(epsc, LN_EPS), writes=[epsc])

            def load_w(h):
                srcs = [wqkv[:, n * 1024 + h * 128:n * 1024 + (h + 1) * 128].rearrange("(c p) n -> p c n", p=128) for n in range(3)]
                P.dma("pool", "daw", [(daw[:, n], srcs[n]) for n in range(3)], writes=[daw[:, 0:3]])
                for n in range(2):
                    src = daw[:, n].rearrange("p c (g f d) -> p c g f d", g=2, f=2)
                    dst = daw[:, 3 + n].rearrange("p c (g f d) -> p c g f d", g=2, f=2)
                    for f in range(2):
                        sv = src[:, :, :, 1 - f, :].rearrange("p c g d -> p (c g) d")
                        dv = dst[:, :, :, f, :].rearrange("p c g d -> p (c g) d")
                        if f == 0:
                            P.op("act", lambda e, sv=sv, dv=dv: e.copy(dv, sv), reads=[daw[:, n]], writes=[daw[:, 3 + n]])
                        else:
                            P.op("dve", lambda e, sv=sv, dv=dv: e.tensor_copy(dv, sv), reads=[daw[:, n]], writes=[daw[:, 3 + n]])

            def project(h):
                pp = h % 2
                for n, dstT in ((0, qT[pp]), (1, None)):
                    for tg in range(4):
                        ba, bb = bankA(), bankA()
                        cols = slice(tg * 512, (tg + 1) * 512)
                        for kc in range(8):
                            mm(ps[:, ba, :], daw[:, n, kc, :], xT[:, kc, cols], kc == 0, kc == 7)
                        for kc in range(8):
                            mm(ps[:, bb, :], daw[:, 3 + n, kc, :], xT[:, kc, cols], kc == 0, kc == 7)
                        P.op("dve", lambda e, ba=ba, cols=cols: e.tensor_tensor(tA, ps[:, ba, :], cosT[:, cols], ALU.mult),
                             reads=[ps[:, ba, :], cosT[:, cols]], writes=[tA])
                        P.op("dve", lambda e, bb=bb, cols=cols: e.tensor_tensor(tB, ps[:, bb, :], sinT[:, cols], ALU.mult),
                             reads=[ps[:, bb, :], sinT[:, cols]], writes=[tB])
                        if dstT is not None:
                            P.op("dve", lambda e, dstT=dstT, cols=cols: e.tensor_tensor(dstT[:, cols], tA, tB, ALU.add),
                                 reads=[tA, tB], writes=[dstT[:, cols]])
                        else:
                            for c in range(2):
                                rows = slice(c * 64, (c + 1) * 64)
                                dk = kz[pp][c]
                                P.op("dve", lambda e, dk=dk, rows=rows, cols=cols: e.tensor_tensor(dk[rows, cols], tA[rows, :], tB[rows, :], ALU.add),
                                     reads=[tA, tB], writes=[dk[:, cols]])
                for t4 in range(4):
                    b = bankA()
                    for tl in range(4):
                        t = t4 * 4 + tl
                        for kc in range(8):
                            mm(ps[:, b, tl * 128:(tl + 1) * 128], xT[:, kc, t * 128:(t + 1) * 128], daw[:, 2, kc, :], kc == 0, kc == 7)
                    dst = vv[pp][:, t4 * 4:(t4 + 1) * 4, 0:128]
                    P.op("act", lambda e, dst=dst, b=b: e.copy(dst, ps[:, b, :].rearrange("p (a n) -> p a n", n=128)),
                         reads=[ps[:, b, :]], writes=[dst])

            ACC_ORDER = [(0, 0), (1, 0), (0, 1), (1, 1), (0, 2), (1, 2), (0, 3), (1, 3)]
            ACC_IDX = {cq: k for k, cq in enumerate(ACC_ORDER)}

            def acc_ap(c, qt):
                k = ACC_IDX[(c, qt)]
                return ps[:, 4 + k // 3, (k % 3) * 130:(k % 3) * 130 + 130]

            deferred = []

            def run_deferred(n=None):
                while deferred and (n is None or n > 0):
                    deferred.pop(0)()
                    if n is not None:
                        n -= 1

            def attend(h, G):
                pp = h % 2
                items = [(jj, c) for jj in range(4 * G + 4) for c in range(2)]
                LAG = 3
                info = {}

                def scores(idx):
                    jj, c = items[idx]
                    q0 = max(jj - 4 * G, 0) * 128
                    sb = bankA()
                    mm(ps[:, sb, q0:512], kz[pp][c][:, jj * 128:(jj + 1) * 128], qT[pp][:, G * 512 + q0:(G + 1) * 512], True, True)
                    et = eT[(idx + 5 * (h * 4 + G)) % 5]
                    P.op("act", lambda e: e.activation(et[:, q0:512], ps[:, sb, q0:512], AF.Exp, scale=0.125),
                         reads=[ps[:, sb, q0:512]], writes=[et[:, q0:512]])
                    if jj >= 4 * G:
                        P.op("dve", lambda e: e.tensor_tensor(et[:, q0:q0 + 128], et[:, q0:q0 + 128], tri[:], ALU.mult),
                             reads=[et[:, q0:q0 + 128], tri[:]], writes=[et[:, q0:q0 + 128]])
                    info[idx] = et

                seq = [(idx, qt) for idx, (jj, c) in enumerate(items) for qt in range(max(jj - 4 * G, 0), 4)]
                first, last = {}, {}
                for n_, (idx, qt) in enumerate(seq):
                    bk = 4 + ACC_IDX[(items[idx][1], qt)] // 3
                    first.setdefault(bk, n_)
                    last[bk] = n_
                pos = {iq: n_ for n_, iq in enumerate(seq)}

                def pv(idx):
                    jj, c = items[idx]
                    et = info.pop(idx)
                    if DA_STOP == 5:
                        return
                    for qt in range(max(jj - 4 * G, 0), 4):
                        bk = 4 + ACC_IDX[(c, qt)] // 3
                        n_ = pos[(idx, qt)]
                        mm(acc_ap(c, qt), et[:, qt * 128:(qt + 1) * 128], vv[pp][:, jj, 0:130], first[bk] == n_, last[bk] == n_)

                for idx in range(len(items)):
                    scores(idx)
                    if idx >= LAG:
                        pv(idx - LAG)
                    run_deferred(1)
                for idx in range(len(items) - LAG, len(items)):
                    pv(idx)
                run_deferred()
                if DA_STOP == 5:
                    return
                for k, (c, qt) in enumerate(ACC_ORDER):
                    src = acc_ap(c, qt)
                    dstv = accS[:, k, 0:130]
                    if (k // 3) % 2:
                        P.op("act", lambda e, src=src, dstv=dstv: e.copy(dstv, src), reads=[src], writes=[dstv])
                    else:
                        P.op("dve", lambda e, src=src, dstv=dstv: e.tensor_copy(dstv, src), reads=[src], writes=[dstv])
                dst = onT[:, h % 4, G * 512:(G + 1) * 512]

                def n1(qt):
                    k1, k2 = ACC_IDX[(0, qt)], ACC_IDX[(1, qt)]
                    o1, s1 = accS[:, k1, 0:128], accS[:, k1, 128:129]
                    o2, s2 = accS[:, k2, 0:128], accS[:, k2, 128:129]
                    rc1, rc2, ss, rs = (dsc[:, qt * 4 + n:qt * 4 + n + 1] for n in range(4))
                    dd = dds[qt % 2]
                    P.op("dve", lambda e: e.reciprocal(rc1, s1), reads=[s1], writes=[rc1])
                    P.op("dve", lambda e: e.reciprocal(rc2, s2), reads=[s2], writes=[rc2])
                    P.op("dve", lambda e: e.tensor_tensor(rc2, rc2, nlam, ALU.mult), reads=[rc2, nlam], writes=[rc2])
                    P.op("dve", lambda e: e.tensor_scalar(tt, o2, rc2, None, ALU.mult), reads=[o2, rc2], writes=[tt])
                    P.op("dve", lambda e: e.scalar_tensor_tensor(dd, o1, rc1, tt, ALU.mult, ALU.add),
                         reads=[o1, rc1, tt], writes=[dd])
                    P.op("act", lambda e: e.activation(junk, dd, AF.Square, accum_out=ss), reads=[dd], writes=[junk, ss])

                def n2(qt):
                    ss, rs = dsc[:, qt * 4 + 2:qt * 4 + 3], dsc[:, qt * 4 + 3:qt * 4 + 4]
                    P.op("dve", lambda e: e.tensor_scalar(rs, ss, 1.0 / 128.0, LN_EPS, ALU.mult, ALU.add), reads=[ss], writes=[rs])
                    P.op("pool", lambda e: e.tensor_tensor(rs, rs, mhalf[:, 0:1], ALU.pow), reads=[rs, mhalf[:, 0:1]], writes=[rs])

                def n3a(qt):
                    rs = dsc[:, qt * 4 + 3:qt * 4 + 4]
                    dd, dq = dds[qt % 2], dn[qt % 2]
                    P.op("dve", lambda e: e.scalar_tensor_tensor(dq, dd, rs, gtab, ALU.mult, ALU.mult),
                         reads=[dd, rs, gtab], writes=[dq])

                def n3b(qt):
                    dq = dn[qt % 2]
                    ptT = ps[:, 7, :].bitcast(BF16).rearrange("p (c n) -> p c n", n=128)
                    P.op("pe", lambda e: e.transpose(ptT[:, qt, :], dq, ident[:]),
                         reads=[dq, ident[:]], writes=[ps[:, 7, :]], signal=True)

                def n4():
                    ptT = ps[:, 7, :].bitcast(BF16).rearrange("p (c n) -> p c n", n=128)
                    P.op("act", lambda e: e.copy(dst.rearrange("p (c n) -> p c n", n=128), ptT[:, 0:4, :]),
                         reads=[ps[:, 7, :]], writes=[dst])

                order = [(n1, 0), (n1, 1), (n2, 0), (n2, 1), (n3a, 0), (n3a, 1), (n1, 2), (n1, 3), (n3b, 0), (n3b, 1),
                         (n2, 2), (n2, 3), (n3a, 2), (n3a, 3), None, None, (n3b, 2), (n3b, 3)]
                for it_ in order:
                    if it_ is None:
                        deferred.append(lambda: None)
                    else:
                        deferred.append(lambda fn=it_[0], qt=it_[1]: fn(qt))
                deferred.append(n4)

            def wo_pass(pi, final):
                run_deferred()
                wv_ = w_o[pi * 512:(pi + 1) * 512, :].rearrange("(c p) n -> p c n", p=128)
                P.dma("pool", "dawo", [(wo_sb[:, 0:2, :], wv_[:, 0:2, :]), (wo_sb[:, 2:4, :], wv_[:, 2:4, :])], writes=[wo_sb])
                pending = []
                for t in range(NT):
                    for hf in range(2):
                        b = bankA()
                        for hh in range(4):
                            mm(ps[:, b, :], onT[:, hh, t * 128:(t + 1) * 128], wo_sb[:, hh, hf * 512:(hf + 1) * 512], hh == 0, hh == 3)
                        accum(t, hf, b, 1.0 / ALPHA)
                    if final:
                        ln_tail(t)
                ln_flush()

            load_w(0)
            project(0)
            for h in range(8):
                if h + 1 < 8:
                    load_w(h + 1)
                for G in range(4):
                    attend(h, G)
                    if G == 1 and h + 1 < 8:
                        project(h + 1)
                if h == 3:
                    wo_pass(0, False)
            run_deferred()
            load_ln(i, 1)
            call_pre_hook()
            wo_pass(1, True)

        def retention(i):
            j = i // 3
            wall = W("ret_w_qkvg")[j]
            w_o = W("ret_w_o")[j]
            cosR = av(0, (S,), F32)
            sinR = av(8192, (S,), F32)
            qT = av(16384, (2, S), BF16)
            kT = av(24576, (2, S), BF16)
            ktok = av(32768, (NT, 256), BF16)
            vv = av(40960, (NT, 512), BF16)
            Rf = av(57344, (2, 512), F32)
            Rb = av(61440, (2, 512), BF16)
            wo_h = av(63488, (4, D), BF16)
            T1, T2, T3, T4 = (av(OFF_XN + q * 2048, (512,), F32) for q in range(4))
            T4s = [T2, T4]
            T2s = [av(OFF_LNP + q * 2048, (512,), F32) for q in range(3)]
            attb = attb_sb
            ybf = [av(OFF_XB + q * 1024, (512,), BF16) for q in range(2)]
            yTn = [av(OFF_XB + 2048 + q * 1024, (4, 128), BF16) for q in range(2)]
            rope_tables(cosR, sinR, retcst[:, 0:1], None,
                        av(16384, (S,), I32), av(24576, (S,), F32), av(32768, (S,), F32))
            loaders = []
            for h in range(4):
                loaders.append(col_loader(wall[:, h * 256:(h + 1) * 256]))
                loaders.append(col_loader(wall[:, 1024 + h * 256:1024 + (h + 1) * 256]))
                for cb in range(2):
                    loaders.append(col_loader(wall[:, 2048 + h * 512 + cb * 256:2048 + h * 512 + (cb + 1) * 256]))
                for cb in range(2):
                    loaders.append(col_loader(wall[:, 4096 + h * 512 + cb * 256:4096 + h * 512 + (cb + 1) * 256]))
            get = make_stream(loaders)
            sv = lambda sl: wst[:, sl, :].rearrange("p (c n) -> p c n", c=8)
            for h in range(4):
                gam = 1.0 - 2.0 ** (-5.0 - h)
                gC = gam ** 128
                qdec = retcst[:, 1 + h:2 + h]
                kdec = retcst[:, 5 + h:6 + h]
                wv_ = w_o[h * 512:(h + 1) * 512, :].rearrange("(c p) n -> p c n", p=128)
                P.dma("pool", "retwo", [(wo_h[:, 0:2, :], wv_[:, 0:2, :]), (wo_h[:, 2:4, :], wv_[:, 2:4, :])], writes=[wo_h])
                for n, dstT in ((0, qT), (1, kT)):
                    wv = sv(get(h * 6 + n))
                    for tg in range(4):
                        ba, bb = bankA(), bankA()
                        cols = slice(tg * 512, (tg + 1) * 512)
                        for kc in range(8):
                            mm(ps[:, ba, :], wv[:, kc, 0:128], xT[:, kc, cols], kc == 0, kc == 7)
                        for kc in range(8):
                            mm(ps[:, bb, :], wv[:, kc, 128:256], xT[:, kc, cols], kc == 0, kc == 7)
                        A_, B_ = ps[:, ba, :], ps[:, bb, :]
                        c_, s_ = cosR[:, cols], sinR[:, cols]
                        P.op("dve", lambda e, A_=A_, c_=c_: e.tensor_tensor(T1, A_, c_, ALU.mult), reads=[A_, c_], writes=[T1])
                        P.op("dve", lambda e, B_=B_, s_=s_: e.tensor_tensor(T2, B_, s_, ALU.mult), reads=[B_, s_], writes=[T2])
                        P.op("dve", lambda e, dstT=dstT, cols=cols: e.tensor_tensor(dstT[:, 0, cols], T1, T2, ALU.subtract),
                             reads=[T1, T2], writes=[dstT[:, 0, cols]])
                        P.op("dve", lambda e, B_=B_, c_=c_: e.tensor_tensor(T3, B_, c_, ALU.mult), reads=[B_, c_], writes=[T3])
                        P.op("dve", lambda e, A_=A_, s_=s_: e.tensor_tensor(T4, A_, s_, ALU.mult), reads=[A_, s_], writes=[T4])
                        P.op("dve", lambda e, dstT=dstT, cols=cols: e.tensor_tensor(dstT[:, 1, cols], T3, T4, ALU.add),
                             reads=[T3, T4], writes=[dstT[:, 1, cols]])
                for n4 in range(4):
                    b = bankA()
                    pt = ps[:, b, :].bitcast(BF16).rearrange("p (c n) -> p c n", n=128)
                    for nl in range(4):
                        n = n4 * 4 + nl
                        for ab in range(2):
                            src = kT[:, ab, n * 128:(n + 1) * 128]
                            P.op("pe", lambda e, src=src, pt=pt, idx=nl * 2 + ab: e.transpose(pt[:, idx, :], src, ident[:]),
                                 reads=[src, ident[:]], writes=[ps[:, b, :]], signal=(nl == 3 and ab == 1))
                    dst = ktok[:, n4 * 4:(n4 + 1) * 4, :].rearrange("p a (b n) -> p (a b) n", n=128)
                    P.op("dve", lambda e, dst=dst, pt=pt, kdec=kdec: e.tensor_scalar(dst, pt, kdec, None, ALU.mult),
                         reads=[ps[:, b, :], kdec], writes=[dst])
                v0 = sv(get(h * 6 + 2))
                v1 = sv(get.peek(h * 6 + 3))
                for n in range(NT):
                    b = bankA()
                    for cb, wv in enumerate((v0, v1)):
                        for kc in range(8):
                            mm(ps[:, b, cb * 256:(cb + 1) * 256], xT[:, kc, n * 128:(n + 1) * 128], wv[:, kc, :], kc == 0, kc == 7)
                    if n % 2:
                        P.op("act", lambda e, n=n, b=b: e.copy(vv[:, n, :], ps[:, b, :]), reads=[ps[:, b, :]], writes=[vv[:, n, :]])
                    else:
                        P.op("dve", lambda e, n=n, b=b: e.tensor_copy(vv[:, n, :], ps[:, b, :]), reads=[ps[:, b, :]], writes=[vv[:, n, :]])
                g0 = sv(get(h * 6 + 4))
                g1 = sv(get.peek(h * 6 + 5))
                DT = retdt[:, h, :]

                def sA(n, DT=DT):
                    tok = slice(n * 128, (n + 1) * 128)
                    ab_ = attb[:, n % 2, :]
                    b = bankA()
                    for ab in range(2):
                        mm(ps[:, b, 0:128], kT[:, ab, tok], qT[:, ab, tok], ab == 0, ab == 1)
                    P.op("dve", lambda e: e.tensor_tensor(ab_, ps[:, b, 0:128], DT, ALU.mult),
                         reads=[ps[:, b, 0:128], DT], writes=[ab_])

                def sB(n, qdec=qdec, gC=gC):
                    tok = slice(n * 128, (n + 1) * 128)
                    ab_ = attb[:, n % 2, :]
                    o_ = T2s[n % 3]
                    bx = bankB()
                    mm(ps[:, bx, :], ab_, vv[:, n, :], True, True)
                    if n > 0:
                        by = bankB()
                        for ab in range(2):
                            mm(ps[:, by, :], qT[:, ab, tok], Rb[:, ab, :], ab == 0, ab == 1)
                        P.op("act", lambda e: e.copy(T1, ps[:, bx, :]), reads=[ps[:, bx, :]], writes=[T1])
                        P.op("dve", lambda e: e.scalar_tensor_tensor(o_, ps[:, by, :], qdec, T1, ALU.mult, ALU.add),
                             reads=[ps[:, by, :], qdec, T1], writes=[o_])
                    else:
                        P.op("act", lambda e: e.copy(o_, ps[:, bx, :]), reads=[ps[:, bx, :]], writes=[o_])
                    if n < NT - 1:
                        for ab in range(2):
                            bz = bankB()
                            mm(ps[:, bz, :], ktok[:, n, ab * 128:(ab + 1) * 128], vv[:, n, :], True, True)
                            if n == 0:
                                P.op("dve", lambda e, ab=ab, bz=bz: e.tensor_copy(Rf[:, ab, :], ps[:, bz, :]),
                                     reads=[ps[:, bz, :]], writes=[Rf[:, ab, :]])
                            else:
                                P.op("dve", lambda e, ab=ab, bz=bz: e.scalar_tensor_tensor(Rf[:, ab, :], Rf[:, ab, :], gC, ps[:, bz, :], ALU.mult, ALU.add),
                                     reads=[Rf[:, ab, :], ps[:, bz, :]], writes=[Rf[:, ab, :]])
                        P.op("act", lambda e: e.copy(Rb, Rf), reads=[Rf], writes=[Rb])

                def sC(n, st_):
                    o_ = T2s[n % 3]
                    q = cnt["sm"] % 8
                    cnt["sm"] += 1
                    stf = small[:, q * 16:q * 16 + 6]
                    mv = small[:, q * 16 + 12:q * 16 + 14]
                    rs = small[:, q * 16 + 14:q * 16 + 15]
                    nmr = small[:, q * 16 + 15:q * 16 + 16]
                    st_[n] = (mv, rs, nmr)
                    P.op("dve", lambda e: e.bn_stats(stf, o_), reads=[o_], writes=[stf])
                    P.op("dve", lambda e: e.bn_aggr(mv, stf), reads=[stf], writes=[mv])
                    P.op("dve", lambda e: e.tensor_scalar(rs, mv[:, 1:2], LN_EPS, None, ALU.add), reads=[mv], writes=[rs])
                    P.op("pool", lambda e: e.tensor_tensor(rs, rs, mhalf[:, 0:1], ALU.pow), reads=[rs, mhalf[:, 0:1]], writes=[rs])

                def sD(n, st_, g0=g0, g1=g1):
                    tok = slice(n * 128, (n + 1) * 128)
                    o_ = T2s[n % 3]
                    sg = T4s[n % 2]
                    yb = ybf[n % 2]
                    mv, rs, nmr = st_.pop(n)
                    bg = bankA()
                    for cb, wv in enumerate((g0, g1)):
                        for kc in range(8):
                            mm(ps[:, bg, cb * 256:(cb + 1) * 256], xT[:, kc, tok], wv[:, kc, :], kc == 0, kc == 7)
                    P.op("act", lambda e: e.activation(sg, ps[:, bg, :], AF.Silu), reads=[ps[:, bg, :]], writes=[sg])
                    P.op("dve", lambda e: e.tensor_scalar(nmr, mv[:, 0:1], rs, -1.0, ALU.mult, ALU.mult),
                         reads=[mv, rs], writes=[nmr])
                    P.op("act", lambda e: e.activation(T3, o_, AF.Identity, bias=nmr, scale=rs),
                         reads=[o_, nmr, rs], writes=[T3])
                    P.op("dve", lambda e: e.tensor_tensor(yb, T3, sg, ALU.mult), reads=[T3, sg], writes=[yb])

                def sE(n):
                    yb = ybf[n % 2]
                    b = bankA()
                    pt = ps[:, b, :].bitcast(BF16).rearrange("p (c n) -> p c n", n=128)
                    for ec in range(4):
                        src = yb[:, ec * 128:(ec + 1) * 128]
                        P.op("pe", lambda e, ec=ec, src=src: e.transpose(pt[:, ec, :], src, ident[:]),
                             reads=[src, ident[:]], writes=[ps[:, b, :]], signal=(ec == 3))
                    yt = yTn[n % 2]
                    P.op("act", lambda e: e.copy(yt, pt[:, 0:4, :]), reads=[ps[:, b, :]], writes=[yt])

                def sF(n):
                    yt = yTn[n % 2]
                    for hf in range(2):
                        b = bankA()
                        for ec in range(4):
                            mm(ps[:, b, :], yt[:, ec, :], wo_h[:, ec, hf * 512:(hf + 1) * 512], ec == 0, ec == 3)
                        accum(n, hf, b, 1.0 / ALPHA)

                st_ = {}
                for it in range(NT + 5):
                    if it < NT:
                        sA(it)
                    if 0 <= it - 1 < NT:
                        sB(it - 1)
                    if 0 <= it - 2 < NT:
                        sC(it - 2, st_)
                    if 0 <= it - 3 < NT:
                        sD(it - 3, st_)
                    if 0 <= it - 4 < NT:
                        sE(it - 4)
                    if 0 <= it - 5 < NT:
                        sF(it - 5)
            load_ln(i, 1)
            call_pre_hook()
            if not DEBUG_NO_LN:
                for t in range(NT):
                    ln_tail(t)
                ln_flush()

        xv = dr["x"].rearrange("(t p) d -> p t d", p=128)
        for t4 in range(4):
            P.dma("sp", f"xin{t4}", [(xres[:, t4 * 4:(t4 + 1) * 4, :], xv[:, t4 * 4:(t4 + 1) * 4, :])],
                  writes=[xres[:, t4 * 4:(t4 + 1) * 4, :]])
        for t in range(NT):
            ln_tail(t, cast_only=True)
        ln_flush()
        xa_gen = {"g": None}
        for n_item, item in enumerate(plan):
            final_store["on"] = (n_item == len(plan) - 1) and item[0] in ("ffn", "xa")
            nxt = plan[n_item + 1] if n_item + 1 < len(plan) else None
            if item[0] == "mix" and nxt is not None and nxt[0] == "xa":
                xa_gen["g"] = xa(nxt[1])
                pre_hook["fn"] = lambda: next(xa_gen["g"])
            if item[0] == "ffn":
                ffn(item[1], item[2])
            elif item[0] == "xa":
                if xa_gen["g"] is None:
                    xa_gen["g"] = xa(item[1])
                    next(xa_gen["g"])
                for _ in xa_gen["g"]:
                    pass
                xa_gen["g"] = None
            elif item[0] == "mix" and item[1] % 3 == 1:
                poolmix(item[1])
            elif item[0] == "mix" and item[1] % 3 == 0:
                diffattn(item[1])
            elif item[0] == "mix" and item[1] % 3 == 2:
                retention(item[1])
            else:
                raise NotImplementedError(item)
            call_pre_hook()
        outs = list(final_store["idents"])
        if not outs:
            for t4 in range(4):
                outs.append(P.dma("sp", f"yout{t4}", [(yv[:, t4 * 4:(t4 + 1) * 4, :], xres[:, t4 * 4:(t4 + 1) * 4, :])],
                                  reads=[xres[:, t4 * 4:(t4 + 1) * 4, :]]))
        P.wait_all("sp", outs)
        P.emit()
    return nc, P, sorted(k for k in dr)


def const_inputs():
    c = {"ident": np.eye(128, dtype=np.float32)}
    pm = np.zeros((8, 128, 128), np.float32)
    pc = np.zeros((1, 4, 128), np.float32)
    sidx = np.arange(128)[:, None]
    tidx = np.arange(128)[None, :]
    for g, w in enumerate((2, 4, 8, 16)):
        pm[2 * g] = ((tidx - sidx >= 0) & (tidx - sidx < w)).astype(np.float32)
        pm[2 * g + 1] = ((tidx + 128 - sidx) < w).astype(np.float32)
        pc[0, g] = 1.0 / np.minimum(np.arange(128) + 1, w)
    kk = np.arange(128)
    c["trimask"] = (kk[:, None] <= kk[None, :]).astype(np.float32)
    inv = (1.0 / (np.float32(10000.0) ** (np.arange(0, 64, 2, dtype=np.float32) / np.float32(64)))).astype(np.float32)
    dac = np.zeros((128, 2), np.float32)
    dac[:, 0] = inv[kk % 32]
    dac[:, 1] = np.where((kk % 64) < 32, -1.0, 1.0)
    c["dacst"] = dac
    rc = np.zeros((128, 16), np.float32)
    rc[:, 0] = (1.0 / (np.float32(10000.0) ** np.linspace(0.0, 1.0, 128, dtype=np.float32))).astype(np.float32)
    rdt = np.zeros((4, 128, 128), np.float32)
    for h in range(4):
        gam = 1.0 - 2.0 ** (-5.0 - h)
        rc[:, 1 + h] = gam ** (kk + 1.0)
        rc[:, 5 + h] = gam ** (127.0 - kk) / 16.0
        dif = kk[None, :] - kk[:, None]
        rdt[h] = np.where(dif >= 0, gam ** np.maximum(dif, 0), 0.0) / 16.0
    c["retcst"] = rc
    c["retdt"] = rdt
    c["poolmat"] = pm
    c["poolcnt"] = pc
    return c


def layer_plan(i):
    return [("ffn", i, 0), ("mix", i), ("xa", i), ("ffn", i, 1)]


FUSED = True
_CACHE = {}


def _program(plan):
    key = tuple(plan)
    if key not in _CACHE:
        _CACHE[key] = build_program(list(plan))
    return _CACHE[key]


def kernel(**inputs):
    n = 8
    consts = const_inputs()
    x = np.asarray(inputs["x"], dtype=np.float32)
    cur = [np.ascontiguousarray(x[b]) for b in range(n)]
    if FUSED:
        plans = [sum((layer_plan(i) for i in range(DEPTH)), [])]
    else:
        plans = [layer_plan(i) for i in range(DEPTH)]
    for plan in plans:
        nc, _, names = _program(plan)
        in_maps = []
        for b in range(n):
            m = {"x": cur[b],
                 "mem": np.ascontiguousarray(inputs["mem"][b], dtype=np.float32),
                 "positions": np.ascontiguousarray(inputs["positions"][b:b + 1], dtype=np.int32)}
            m.update(consts)
            for name in names:
                if name not in m:
                    m[name] = np.ascontiguousarray(inputs[name], dtype=np.float32)
            in_maps.append({k: m[k] for k in names})
        res = run_bass_kernel_spmd(nc, in_maps, core_ids=list(range(n)))
        cur = [np.asarray(res.results[b]["y"], dtype=np.float32) for b in range(n)]
    return np.stack(cur, axis=0)
```

```python
import math
from contextlib import ExitStack

import numpy as np
import concourse.bass as bass
import concourse.mybir as mybir
from concourse.bass_utils import run_bass_kernel_spmd

F32 = mybir.dt.float32
BF16 = mybir.dt.bfloat16
I32 = mybir.dt.int32
AF = mybir.ActivationFunctionType
ALU = mybir.AluOpType
AX = mybir.AxisListType
DSZ = {F32: 4, BF16: 2, I32: 4}
ENGS = ("pe", "act", "dve", "pool", "sp")

S = 2048
D = 1024
NT = 16
DEPTH = 4
FF = 2816
NJ = 22
MEM = 256
ALPHA = (2 * DEPTH) ** 0.25
LN_EPS = 1e-5
EPS_S = LN_EPS / (ALPHA * ALPHA)


def ap_interval(ap):
    sz = DSZ[ap.dtype]
    pat = ap.ap
    pstep = pat[0][0]
    off = int(ap.offset)
    lo = off % pstep if pstep > 0 else off
    span = 0
    for st, cnt in pat[1:]:
        span += abs(st) * (cnt - 1)
    lo_b, hi_b = lo * sz, (lo + span + 1) * sz
    if ap.tensor.name == "ps":
        lo_b = (lo_b // 2048) * 2048
        hi_b = ((hi_b + 2047) // 2048) * 2048
    return ap.tensor.name, lo_b, hi_b


class Prog:
    def __init__(self, nc):
        self.nc = nc
        self.ops = {e: [] for e in ENGS}
        self.cnt = {e: 0 for e in ENGS}
        self.seen = {e: {} for e in ENGS}
        self.rec = {}
        self.dma_cnt = {}
        self.dma_sem_names = []

    def _overl(self, key):
        name, lo, hi = key
        return [r for r in self.rec.get(name, ()) if r[0] < hi and lo < r[1]]

    def _deps(self, eng, reads, writes):
        deps = {}

        def add(k, v):
            if k == "pe" and eng == "pe":
                return
            if deps.get(k, 0) < v:
                deps[k] = v

        for key in reads:
            for r in self._overl(key):
                if r[2] is not None:
                    add(*r[2])
                if key[0] == "ps":
                    for k, v in r[3].items():
                        if k != eng:
                            add(k, v)
        for key in writes:
            for r in self._overl(key):
                if r[2] is not None:
                    add(*r[2])
                for k, v in r[3].items():
                    add(k, v)
        out = []
        seen = self.seen[eng]
        for k, v in deps.items():
            if seen.get(k, 0) < v:
                seen[k] = v
                out.append((k, v))
        return out

    def _update(self, ident, reads, writes):
        for name, lo, hi in reads:
            lst = self.rec.setdefault(name, [])
            covered = False
            for r in lst:
                if r[0] < hi and lo < r[1]:
                    if r[3].get(ident[0], 0) < ident[1]:
                        r[3][ident[0]] = ident[1]
                    if r[0] <= lo and hi <= r[1]:
                        covered = True
            if not covered:
                lst.append([lo, hi, None, {ident[0]: ident[1]}])
        for name, lo, hi in writes:
            lst = self.rec.setdefault(name, [])
            lst[:] = [r for r in lst if not (lo <= r[0] and r[1] <= hi)]
            lst.append([lo, hi, ident, {}])

    def op(self, eng, fn, reads=(), writes=(), signal=True):
        rk = [ap_interval(a) for a in reads]
        wk = [ap_interval(a) for a in writes]
        waits = self._deps(eng, rk, wk)
        if signal:
            self.cnt[eng] += 1
            ident = (eng, self.cnt[eng])
        else:
            ident = (eng, self.cnt[eng] + 1)
        self._update(ident, rk, wk)
        self.ops[eng].append((waits, fn, eng if signal else None, 1))

    def dma(self, eng, slot, pairs, reads=(), writes=()):
        rk = [ap_interval(a) for a in reads]
        wk = [ap_interval(a) for a in writes]
        key = "dma:" + slot
        waits = self._deps(eng, rk, wk)
        if key not in self.dma_cnt:
            self.dma_cnt[key] = 0
            self.dma_sem_names.append(key)
        self.dma_cnt[key] += 16 * len(pairs)
        ident = (key, self.dma_cnt[key])
        self._update(ident, rk, wk)
        for n, (o, i) in enumerate(pairs):
            self.ops[eng].append((waits if n == 0 else [],
                                  lambda e, o=o, i=i: e.dma_start(out=o, in_=i), key, 16))
        return ident

    def wait_all(self, eng, idents):
        waits = []
        for k, v in idents:
            if self.seen[eng].get(k, 0) < v:
                self.seen[eng][k] = v
                waits.append((k, v))
        self.ops[eng].append((waits, None, None, 0))

    def check(self):
        pos = {e: 0 for e in ENGS}
        val = {}
        prog = True
        while prog:
            prog = False
            for e in ENGS:
                lst = self.ops[e]
                while pos[e] < len(lst):
                    waits, fn, sk, inc = lst[pos[e]]
                    if all(val.get(k, 0) >= v for k, v in waits):
                        if sk is not None:
                            val[sk] = val.get(sk, 0) + inc
                        pos[e] += 1
                        prog = True
                    else:
                        break
        for e in ENGS:
            if pos[e] != len(self.ops[e]):
                waits = self.ops[e][pos[e]][0]
                raise RuntimeError(f"deadlock: engine {e} stuck at op {pos[e]}/{len(self.ops[e])} waits={waits}")

    def emit(self):
        nc = self.nc
        self.check()
        with ExitStack() as es:
            sems = {}
            for e in ("pe", "act", "dve", "pool"):
                sems[e] = es.enter_context(nc.semaphore("s_" + e))
            for k in self.dma_sem_names:
                sems[k] = es.enter_context(nc.semaphore("s_" + k.replace(":", "_")))
            block = es.enter_context(nc.Block())

            def run(e, lst):
                for waits, fn, sk, inc in lst:
                    for k, v in waits:
                        e.wait_ge(sems[k], v)
                    if fn is None:
                        continue
                    ins = fn(e)
                    if sk is not None:
                        ins.then_inc(sems[sk], inc)

            @block.tensor
            def _(e):
                run(e, self.ops["pe"])

            @block.scalar
            def _(e):
                run(e, self.ops["act"])

            @block.vector
            def _(e):
                run(e, self.ops["dve"])

            @block.gpsimd
            def _(e):
                run(e, self.ops["pool"])

            @block.sync
            def _(e):
                run(e, self.ops["sp"])


WEIGHT_SPECS = [
    ("ffn_w_in", [4, 2, 1024, 5632]), ("ffn_w_out", [4, 2, 2816, 1024]),
    ("ln_g", [4, 4, 1024]), ("ln_b", [4, 4, 1024]),
    ("da_w_qkv", [2, 1024, 3072]), ("da_w_o", [2, 1024, 1024]),
    ("da_lam_q", [2, 2, 64]), ("da_lam_k", [2, 2, 64]), ("da_subln_g", [2, 128]),
    ("pool_w", [1, 4, 256, 256]), ("pool_b", [1, 4, 256]), ("pool_scale", [1, 1024]),
    ("ret_w_qkvg", [1, 1024, 6144]), ("ret_w_o", [1, 2048, 1024]),
    ("xa_wq", [4, 1024, 1024]), ("xa_wkv", [4, 1024, 2048]), ("xa_wo", [4, 1024, 1024]),
]

DEBUG_NO_LN = False
DA_STOP = 0
ARENA_BYTES = 109056
OFF_WST = 72192
OFF_LNP = 88576
OFF_XB = 96768
OFF_XN = 100864


def build_program(plan):
    nc = bass.Bass("TRN2", target_bir_lowering=False)
    dr = {}
    dr["x"] = nc.dram_tensor("x", [S, D], F32, kind="ExternalInput").ap()
    dr["mem"] = nc.dram_tensor("mem", [MEM, D], F32, kind="ExternalInput").ap()
    dr["positions"] = nc.dram_tensor("positions", [1, S], I32, kind="ExternalInput").ap()
    dr["ident"] = nc.dram_tensor("ident", [128, 128], F32, kind="ExternalInput").ap()
    dr["trimask"] = nc.dram_tensor("trimask", [128, 128], F32, kind="ExternalInput").ap()
    dr["dacst"] = nc.dram_tensor("dacst", [128, 2], F32, kind="ExternalInput").ap()
    dr["retcst"] = nc.dram_tensor("retcst", [128, 16], F32, kind="ExternalInput").ap()
    dr["retdt"] = nc.dram_tensor("retdt", [4, 128, 128], F32, kind="ExternalInput").ap()
    dr["poolmat"] = nc.dram_tensor("poolmat", [8, 128, 128], F32, kind="ExternalInput").ap()
    dr["poolcnt"] = nc.dram_tensor("poolcnt", [1, 4, 128], F32, kind="ExternalInput").ap()
    wshape = dict(WEIGHT_SPECS)

    def W(name):
        if name not in dr:
            dr[name] = nc.dram_tensor(name, wshape[name], F32, kind="ExternalInput").ap()
        return dr[name]
    y = nc.dram_tensor("y", [S, D], F32, kind="ExternalOutput").ap()
    P = Prog(nc)

    with ExitStack() as es:
        def T(name, shape, dt):
            return es.enter_context(nc.sbuf_tensor(name, shape, dt))

        xres = T("xres", [128, NT, D], F32)
        xT = T("xT", [128, 8, S], BF16)
        ident = T("ident_sb", [128, 128], BF16)
        small = T("small", [128, 128], F32)
        xst = T("xst", [128, 2, 16], F32)
        dsm = T("dsm", [128, 8], F32)
        dsc = T("dsc", [128, 16], F32)
        dacst = T("dacst_sb", [128, 2], F32)
        retcst = T("retcst_sb", [128, 16], F32)
        retdt = T("retdt_sb", [128, 4, 128], F32)
        attb_sb = T("attb_sb", [128, 2, 128], BF16)
        tri = T("tri_sb", [128, 128], BF16)
        ones = T("ones", [128, 128], BF16)
        mhalf = T("mhalf", [128, 4], F32)
        arena = T("arena", [128, ARENA_BYTES // 2], BF16)
        ps = es.enter_context(nc.psum_tensor("ps", [128, 8, 512], F32))

        def av(off, shape, dt):
            n = int(np.prod(shape)) * DSZ[dt]
            a = arena[:, off // 2:(off + n) // 2]
            if dt != BF16:
                a = a.bitcast(dt)
            if len(shape) == 2:
                a = a.rearrange("p (a b) -> p a b", b=shape[1])
            elif len(shape) == 3:
                a = a.rearrange("p (a b c) -> p a b c", b=shape[1], c=shape[2])
            return a

        wst = av(OFF_WST, (4, 2048), BF16)
        lnp = av(OFF_LNP, (2, D), F32)
        wres = av(49152, (11, D), BF16)
        st = {"wst": 0, "pa": 0, "pb": 0, "p8": 0}

        def next_wst():
            s = st["wst"]
            st["wst"] = (s + 1) % 4
            return s

        def bankA():
            b = st["pa"]
            st["pa"] = (b + 1) % 4
            return b

        def bankB():
            b = st["pb"]
            st["pb"] = (b + 1) % 4
            return 4 + b

        def mm(out, lhsT, rhs, start, stop):
            P.op("pe", lambda e: e.matmul(out, lhsT, rhs, start=start, stop=stop),
                 reads=[lhsT, rhs], writes=[out], signal=stop)

        P.dma("pool", "ident", [(ident[:], dr["ident"])], writes=[ident[:]])
        P.op("dve", lambda e: e.memset(mhalf[:], -0.5), writes=[mhalf[:]])
        P.op("dve", lambda e: e.memset(ones[:], 1.0), writes=[ones[:]])
        P.dma("pool", "tri", [(tri[:], dr["trimask"])], writes=[tri[:]])
        P.dma("sp", "dacst", [(dacst[:], dr["dacst"])], writes=[dacst[:]])
        P.dma("sp", "retcst", [(retcst[:], dr["retcst"]), (retdt[:], dr["retdt"].rearrange("h p n -> p h n"))],
              writes=[retcst[:], retdt[:]])

        xb_bufs = [av(OFF_XB + i * 2048, (D,), BF16) for i in range(2)]
        xn_bufs = [av(OFF_XN + i * 4096, (D,), F32) for i in range(2)]
        cnt = {"xb": 0, "xn": 0, "sm": 0}

        def load_ln(i, k):
            P.dma("sp", "lnp", [(lnp[:, 0, :], W("ln_g")[i, k:k + 1, :].partition_broadcast(128)),
                                (lnp[:, 1, :], W("ln_b")[i, k:k + 1, :].partition_broadcast(128))],
                  writes=[lnp[:]])

        ln_q = []
        final_store = {"on": False, "idents": []}
        yv = y.rearrange("(t p) d -> p t d", p=128)

        def _s1(en):
            t = en["t"]
            q = cnt["sm"] % 8
            cnt["sm"] += 1
            stf = small[:, q * 16:q * 16 + 12]
            stt = stf.rearrange("p (a b) -> p a b", b=6)
            mv = small[:, q * 16 + 12:q * 16 + 14]
            rs = small[:, q * 16 + 14:q * 16 + 15]
            nmr = small[:, q * 16 + 15:q * 16 + 16]
            en.update(mv=mv, rs=rs, nmr=nmr)
            for h in range(2):
                src = xres[:, t, h * 512:(h + 1) * 512]
                P.op("dve", lambda e, h=h, src=src: e.bn_stats(stt[:, h, :], src), reads=[src], writes=[stt[:, h, :]])
            P.op("dve", lambda e: e.bn_aggr(mv, stf), reads=[stf], writes=[mv])
            P.op("dve", lambda e: e.tensor_scalar(rs, mv[:, 1:2], EPS_S, None, ALU.add), reads=[mv], writes=[rs])
            P.op("pool", lambda e: e.tensor_tensor(rs, rs, mhalf[:, 0:1], ALU.pow), reads=[rs, mhalf[:, 0:1]], writes=[rs])

        def _s2(en):
            t, mv, rs, nmr = en["t"], en["mv"], en["rs"], en["nmr"]
            P.op("dve", lambda e: e.tensor_scalar(nmr, mv[:, 0:1], rs, -1.0, ALU.mult, ALU.mult), reads=[mv, rs], writes=[nmr])
            xn = xn_bufs[cnt["xn"] % 2]
            cnt["xn"] += 1
            en["xn"] = xn
            P.op("act", lambda e: e.activation(xn[:], xres[:, t, :], AF.Identity, bias=nmr, scale=rs),
                 reads=[xres[:, t, :], nmr, rs], writes=[xn[:]])

        def _s3(en):
            t = en["t"]
            if "xn" in en:
                xn = en["xn"]
                P.op("dve", lambda e: e.tensor_tensor(xn[:], xn[:], lnp[:, 0, :], ALU.mult), reads=[xn[:], lnp[:, 0, :]], writes=[xn[:]])
                P.op("dve", lambda e: e.tensor_tensor(xres[:, t, :], xn[:], lnp[:, 1, :], ALU.add),
                     reads=[xn[:], lnp[:, 1, :]], writes=[xres[:, t, :]])
            if final_store["on"]:
                if t % 4 == 3:
                    t4 = t // 4
                    final_store["idents"].append(
                        P.dma("sp", f"yout{t4}", [(yv[:, t4 * 4:(t4 + 1) * 4, :], xres[:, t4 * 4:(t4 + 1) * 4, :])],
                              reads=[xres[:, t4 * 4:(t4 + 1) * 4, :]]))
                return "done"
            xb = xb_bufs[cnt["xb"] % 2]
            cnt["xb"] += 1
            en["xb"] = xb
            P.op("act", lambda e: e.copy(xb[:], xres[:, t, :]), reads=[xres[:, t, :]], writes=[xb[:]])

        def _s4(en):
            t, xb = en["t"], en["xb"]
            b = bankA()
            pt = ps[:, b, :].bitcast(BF16).rearrange("p (c n) -> p c n", n=128)
            for c in range(8):
                P.op("pe", lambda e, c=c: e.transpose(pt[:, c, :], xb[:, c * 128:(c + 1) * 128], ident[:]),
                     reads=[xb[:, c * 128:(c + 1) * 128], ident[:]], writes=[ps[:, b, :]], signal=(c == 7))
            dst = xT[:, :, t * 128:(t + 1) * 128]
            if en["n"] % 2:
                P.op("act", lambda e: e.copy(dst, pt), reads=[ps[:, b, :]], writes=[dst])
            else:
                P.op("dve", lambda e: e.tensor_copy(dst, pt), reads=[ps[:, b, :]], writes=[dst])

        STAGES = (_s1, _s2, _s3, _s4)

        def _ln_advance():
            for en in list(ln_q):
                r = STAGES[en["stage"]](en)
                en["stage"] = 4 if r == "done" else en["stage"] + 1
            ln_q[:] = [en for en in ln_q if en["stage"] < 4]

        def ln_tail(t, cast_only=False):
            cnt["ln"] = cnt.get("ln", 0) + 1
            ln_q.append({"t": t, "stage": 2 if cast_only else 0, "n": cnt["ln"]})
            _ln_advance()

        def ln_flush():
            while ln_q:
                _ln_advance()

        def accum(t, h, bank, scale):
            dst = xres[:, t, h * 512:(h + 1) * 512]
            P.op("dve", lambda e: e.scalar_tensor_tensor(dst, ps[:, bank, :], scale, dst, ALU.mult, ALU.add),
                 reads=[ps[:, bank, :], dst], writes=[dst])

        def make_stream(loaders, look=3):
            state = {"n": 0, "slots": []}

            def get(n):
                while state["n"] < min(len(loaders), n + 1 + look):
                    sl = next_wst()
                    loaders[state["n"]](sl)
                    state["slots"].append(sl)
                    state["n"] += 1
                return state["slots"][n]
            get.peek = lambda n: state["slots"][n]
            return get

        def col_loader(src):
            def f(sl):
                P.dma("pool", f"wst{sl}", [(wst[:, sl, :].rearrange("p (c n) -> p c n", c=8),
                                            src.rearrange("(c p) n -> p c n", p=128))], writes=[wst[:, sl, :]])
            return f

        def ffn(i, k):
            w_in = W("ffn_w_in")[i, k]
            w_out = W("ffn_w_out")[i, k]
            actT = av(0, (11, S), BF16)
            sil = [av(45056 + q * 2048, (512,), F32) for q in range(2)]
            load_ln(i, 0 if k == 0 else 3)
            pending = []

            def win_loader(j):
                def f(sl):
                    wv = wst[:, sl, :].rearrange("p (g c n) -> p g c n", g=2, c=8)
                    P.dma("pool", f"wst{sl}",
                          [(wv[:, 0], w_in[:, j * 128:(j + 1) * 128].rearrange("(c p) n -> p c n", p=128)),
                           (wv[:, 1], w_in[:, FF + j * 128:FF + (j + 1) * 128].rearrange("(c p) n -> p c n", p=128))],
                          writes=[wst[:, sl, :]])
                return f
            get = make_stream([win_loader(j) for j in range(NJ)])
            for bi, (j0, j1) in enumerate(((0, 11), (11, 22))):
                nj = j1 - j0
                wo_v = w_out[j0 * 128:j1 * 128, :].rearrange("(c p) n -> p c n", p=128)
                P.dma("pool", "wres", [(wres[:, 0:6, :], wo_v[:, 0:6, :]), (wres[:, 6:nj, :], wo_v[:, 6:nj, :])],
                      writes=[wres[:, 0:nj, :]])
                for j in range(j0, j1):
                    s = get(j)
                    wv = wst[:, s, :].rearrange("p (g c n) -> p g c n", g=2, c=8)
                    for tg in range(4):
                        bg, bu = st["p8"] % 8, (st["p8"] + 1) % 8
                        st["p8"] += 2
                        xs = lambda c: xT[:, c, tg * 512:(tg + 1) * 512]
                        for c in range(8):
                            mm(ps[:, bg, :], wv[:, 0, c, :], xs(c), c == 0, c == 7)
                        for c in range(8):
                            mm(ps[:, bu, :], wv[:, 1, c, :], xs(c), c == 0, c == 7)
                        sl = sil[(j * 4 + tg) % 2]
                        P.op("act", lambda e, sl=sl, bg=bg: e.activation(sl, ps[:, bg, :], AF.Silu),
                             reads=[ps[:, bg, :]], writes=[sl])
                        dst = actT[:, j - j0, tg * 512:(tg + 1) * 512]
                        P.op("dve", lambda e, dst=dst, bu=bu, sl=sl: e.tensor_tensor(dst, ps[:, bu, :], sl, ALU.mult),
                             reads=[ps[:, bu, :], sl], writes=[dst])
                for t in range(NT):
                    for h in range(2):
                        b = bankB()
                        for jj in range(nj):
                            mm(ps[:, b, :], actT[:, jj, t * 128:(t + 1) * 128], wres[:, jj, h * 512:(h + 1) * 512],
                               jj == 0, jj == nj - 1)
                        accum(t, h, b, 0.5 / ALPHA)
                    if bi == 1:
                        ln_tail(t)
            ln_flush()

        pre_hook = {"fn": None}

        def call_pre_hook():
            fn, pre_hook["fn"] = pre_hook["fn"], None
            if fn is not None:
                fn()

        def prep_mem(memT):
            mf = av(OFF_XN, (2, D), F32)
            mb = av(OFF_XB, (2, D), BF16)
            P.dma("sp", "memin", [(mf, dr["mem"].rearrange("(t p) d -> p t d", p=128))], writes=[mf])
            P.op("act", lambda e: e.copy(mb, mf), reads=[mf], writes=[mb])
            for mt in range(2):
                b = bankA()
                pt = ps[:, b, :].bitcast(BF16).rearrange("p (c n) -> p c n", n=128)
                for c in range(8):
                    src = mb[:, mt, c * 128:(c + 1) * 128]
                    P.op("pe", lambda e, c=c, src=src, pt=pt: e.transpose(pt[:, c, :], src, ident[:]),
                         reads=[src, ident[:]], writes=[ps[:, b, :]], signal=(c == 7))
                dst = memT[:, :, mt * 128:(mt + 1) * 128]
                P.op("act", lambda e, dst=dst, pt=pt: e.copy(dst, pt), reads=[ps[:, b, :]], writes=[dst])

        def xa(i):
            wq, wkv, wo = W("xa_wq")[i], W("xa_wkv")[i], W("xa_wo")[i]
            qT = av(0, (8, S), BF16)
            kT = av(32768, (8, MEM), BF16)
            vx = av(36864, (2, D), BF16)
            pf = [av(40960 + q * 4096, (4, 256), F32) for q in range(2)]
            pn = [av(49152 + q * 2048, (4, 256), BF16) for q in range(2)]
            pT = av(53248, (8, 512), BF16)
            oT = av(61440, (8, 512), BF16)
            wo_sb = av(OFF_WST, (8, D), BF16)
            memT = av(53248, (8, MEM), BF16)
            prep_mem(memT)
            loaders = [col_loader(wkv[:, cc * 256:(cc + 1) * 256]) for cc in range(8)]
            loaders += [col_loader(wq[:, cc * 256:(cc + 1) * 256]) for cc in range(4)]
            get = make_stream(loaders)
            ev = {"n": 0}

            def evac(dst, src):
                ev["n"] += 1
                if ev["n"] % 2:
                    P.op("act", lambda e: e.copy(dst, src), reads=[src], writes=[dst])
                else:
                    P.op("dve", lambda e: e.tensor_copy(dst, src), reads=[src], writes=[dst])

            for cc in range(4):
                wv = wst[:, get(cc), :].rearrange("p (c n) -> p c n", c=8)
                for sub in range(2):
                    b = bankA()
                    for kc in range(8):
                        mm(ps[:, b, 0:MEM], wv[:, kc, sub * 128:(sub + 1) * 128], memT[:, kc, :], kc == 0, kc == 7)
                    evac(kT[:, cc * 2 + sub, :], ps[:, b, 0:MEM])
            for cc in range(4):
                wv = wst[:, get(4 + cc), :].rearrange("p (c n) -> p c n", c=8)
                for mt in range(2):
                    b = bankA()
                    for kc in range(8):
                        mm(ps[:, b, 0:256], memT[:, kc, mt * 128:(mt + 1) * 128], wv[:, kc, :], kc == 0, kc == 7)
                    evac(vx[:, mt, cc * 256:(cc + 1) * 256], ps[:, b, 0:256])
            yield
            load_ln(i, 2)
            for cc in range(4):
                wv = wst[:, get(8 + cc), :].rearrange("p (c n) -> p c n", c=8)
                for sub in range(2):
                    for tg in range(4):
                        b = bankA()
                        for kc in range(8):
                            mm(ps[:, b, :], wv[:, kc, sub * 128:(sub + 1) * 128], xT[:, kc, tg * 512:(tg + 1) * 512],
                               kc == 0, kc == 7)
                        evac(qT[:, cc * 2 + sub, tg * 512:(tg + 1) * 512], ps[:, b, :])
            wo_v = wo.rearrange("(c p) n -> p c n", p=128)
            P.dma("pool", "wo", [(wo_sb[:, 0:4, :], wo_v[:, 0:4, :]), (wo_sb[:, 4:8, :], wo_v[:, 4:8, :])],
                  writes=[wo_sb])
            scale = 1.0 / 16.0

            def sc_view(t):
                b2 = 4 + 2 * (t % 2)
                return ps[:, b2:b2 + 2, :].rearrange("p b (h m) -> p (b h) m", m=MEM)

            def stage_scores(t):
                sc = sc_view(t)
                for h in range(4):
                    for dc in range(2):
                        mm(sc[:, h, :], qT[:, 2 * h + dc, t * 128:(t + 1) * 128], kT[:, 2 * h + dc, :], dc == 0, dc == 1)

            def stage_softmax(t):
                q = t % 2
                sc = sc_view(t)
                mx, nb, sm, rs = (xst[:, q, 4 * n:4 * n + 4] for n in range(4))
                P.op("dve", lambda e: e.tensor_reduce(mx, sc, AX.X, ALU.max), reads=[sc], writes=[mx])
                P.op("dve", lambda e: e.tensor_scalar(nb, mx, -scale, None, ALU.mult), reads=[mx], writes=[nb])
                for h in range(4):
                    P.op("act", lambda e, h=h: e.activation(
                        pf[q][:, h, :], sc[:, h, :], AF.Exp, bias=nb[:, h:h + 1], scale=scale, accum_out=sm[:, h:h + 1]),
                        reads=[sc[:, h, :], nb], writes=[pf[q][:, h, :], sm[:, h:h + 1]])
                P.op("dve", lambda e: e.reciprocal(rs, sm), reads=[sm], writes=[rs])
                for h in range(4):
                    P.op("act", lambda e, h=h: e.activation(pn[q][:, h, :], pf[q][:, h, :], AF.Identity, scale=rs[:, h:h + 1]),
                         reads=[pf[q][:, h, :], rs], writes=[pn[q][:, h, :]])

            def stage_transpose(t):
                q = t % 2
                tl = t % 4
                b = bankA()
                pt = ps[:, b, :].bitcast(BF16).rearrange("p (c n) -> p c n", n=128)
                for hm in range(8):
                    src = pn[q][:, hm // 2, (hm % 2) * 128:(hm % 2 + 1) * 128]
                    P.op("pe", lambda e, hm=hm, src=src: e.transpose(pt[:, hm, :], src, ident[:]),
                         reads=[src, ident[:]], writes=[ps[:, b, :]], signal=(hm == 7))
                evac(pT[:, :, tl * 128:(tl + 1) * 128], pt)

            def group_tail(tg):
                for c in range(8):
                    b = bankA()
                    for mc in range(2):
                        mm(ps[:, b, :], vx[:, mc, c * 128:(c + 1) * 128], pT[:, (c // 2) * 2 + mc, :], mc == 0, mc == 1)
                    evac(oT[:, c, :], ps[:, b, :])
                for tl in range(4):
                    tt = tg * 4 + tl
                    for h in range(2):
                        b = bankA()
                        for c in range(8):
                            mm(ps[:, b, :], oT[:, c, tl * 128:(tl + 1) * 128], wo_sb[:, c, h * 512:(h + 1) * 512], c == 0, c == 7)
                        accum(tt, h, b, 1.0 / ALPHA)
                    ln_tail(tt)

            stage_scores(0)
            for t in range(NT + 1):
                if t < NT:
                    stage_softmax(t)
                if t + 1 < NT:
                    stage_scores(t + 1)
                if t >= 1:
                    stage_transpose(t - 1)
                    if (t - 1) % 4 == 3:
                        group_tail((t - 1) // 4)
            ln_flush()

        def poolmix(i):
            j = i // 3
            pmat = av(0, (8, 128), BF16)
            btab = av(2048, (D,), F32)
            stab = av(6144, (D,), F32)
            icnt = av(10240, (4, 128), F32)
            xbp = [av(12288 + q * 2048, (D,), BF16) for q in range(3)]
            poT = [av(18432 + q * 2048, (8, 128), BF16) for q in range(2)]
            tmp = av(22528, (D,), F32)
            tms = av(26624, (128,), F32)
            load_ln(i, 1)
            sl = next_wst()
            pw = wst[:, sl, :].rearrange("p (g c n) -> p g c n", g=4, c=2)
            P.dma("pool", f"wst{sl}", [(pw, W("pool_w")[j].rearrange("g (c p) n -> p g c n", p=128))], writes=[wst[:, sl, :]])
            P.dma("pool", "pmat", [(pmat, dr["poolmat"].rearrange("m p n -> p m n"))], writes=[pmat])
            P.dma("sp", "ptab", [(btab, W("pool_b")[j:j + 1].rearrange("o g d -> o (g d)").partition_broadcast(128)),
                                 (stab, W("pool_scale")[j:j + 1, :].partition_broadcast(128)),
                                 (av(10240, (512,), F32), dr["poolcnt"].rearrange("o g n -> o (g n)").partition_broadcast(128))],
                  writes=[btab, stab, icnt])
            P.op("dve", lambda e: e.tensor_scalar(stab, stab, 1.0 / ALPHA, None, ALU.mult), reads=[stab], writes=[stab])
            P.op("dve", lambda e: e.tensor_tensor(btab, btab, stab, ALU.mult), reads=[btab, stab], writes=[btab])
            wins = (2, 4, 8, 16)
            pending = []
            def win_stage(t):
                xc = xbp[t % 3]
                xp = xbp[(t - 1) % 3]
                P.op("act", lambda e: e.copy(xc, xres[:, t, :]), reads=[xres[:, t, :]], writes=[xc])
                po = poT[t % 2]
                for half in range(2):
                    b = bankA()
                    for cl in range(4):
                        c = half * 4 + cl
                        g = c // 2
                        out = ps[:, b, cl * 128:(cl + 1) * 128]
                        mm(out, xc[:, c * 128:(c + 1) * 128], pmat[:, 2 * g, :], True, t == 0)
                        if t > 0:
                            mm(out, xp[:, c * 128:(c + 1) * 128], pmat[:, 2 * g + 1, :], False, True)
                    for cl in range(4):
                        c = half * 4 + cl
                        g = c // 2
                        out = ps[:, b, cl * 128:(cl + 1) * 128]
                        xt = xT[:, c, t * 128:(t + 1) * 128]
                        if t == 0:
                            P.op("dve", lambda e, out=out, g=g: e.tensor_tensor(tms, out, icnt[:, g, :], ALU.mult),
                                 reads=[out, icnt[:, g, :]], writes=[tms])
                            P.op("dve", lambda e, c=c, xt=xt: e.tensor_tensor(po[:, c, :], tms, xt, ALU.subtract),
                                 reads=[tms, xt], writes=[po[:, c, :]])
                        else:
                            P.op("dve", lambda e, c=c, xt=xt, out=out, g=g: e.scalar_tensor_tensor(
                                po[:, c, :], out, 1.0 / wins[g], xt, ALU.mult, ALU.subtract),
                                reads=[out, xt], writes=[po[:, c, :]])

            def proj_stage(t):
                po = poT[t % 2]
                q = t % 2
                b2 = 4 + 2 * q
                yv2 = ps[:, b2:b2 + 2, :].rearrange("p b (g n) -> p (b g) n", n=256)
                for g in range(4):
                    for kc in range(2):
                        mm(yv2[:, g, :], po[:, 2 * g + kc, :], pw[:, g, kc, :], kc == 0, kc == 1)
                yflat = ps[:, b2:b2 + 2, :].rearrange("p b n -> p (b n)")
                P.op("dve", lambda e: e.tensor_tensor(tmp, yflat, stab, ALU.mult), reads=[yflat, stab], writes=[tmp])
                P.op("dve", lambda e: e.tensor_tensor(tmp, tmp, btab, ALU.add), reads=[tmp, btab], writes=[tmp])
                P.op("dve", lambda e: e.tensor_tensor(xres[:, t, :], xres[:, t, :], tmp, ALU.add),
                     reads=[xres[:, t, :], tmp], writes=[xres[:, t, :]])
                ln_tail(t)

            for t in range(NT + 1):
                if t < NT:
                    win_stage(t)
                if t >= 1:
                    proj_stage(t - 1)
            ln_flush()

        TWO_PI = 2.0 * math.pi
        C1 = 6.28125
        C2 = TWO_PI - C1

        def rope_tables(cosT, sinT, invf, sgn, tmp_i, tmp_a, tmp_b):
            P.dma("sp", "posin", [(tmp_i, dr["positions"].partition_broadcast(128))], writes=[tmp_i])
            ang = tmp_a
            P.op("dve", lambda e: e.tensor_copy(ang, tmp_i), reads=[tmp_i], writes=[ang])
            P.op("dve", lambda e: e.tensor_scalar(ang, ang, invf, None, ALU.mult), reads=[ang, invf], writes=[ang])
            kf = tmp_b
            ki = tmp_i
            P.op("dve", lambda e: e.tensor_scalar(kf, ang, 1.0 / TWO_PI, None, ALU.mult), reads=[ang], writes=[kf])
            P.op("dve", lambda e: e.tensor_copy(ki, kf), reads=[kf], writes=[ki])
            P.op("dve", lambda e: e.tensor_copy(kf, ki), reads=[ki], writes=[kf])
            P.op("dve", lambda e: e.scalar_tensor_tensor(ang, kf, -C1, ang, ALU.mult, ALU.add), reads=[kf, ang], writes=[ang])
            P.op("dve", lambda e: e.scalar_tensor_tensor(ang, kf, -C2, ang, ALU.mult, ALU.add), reads=[kf, ang], writes=[ang])

            def wrap(r):
                P.op("dve", lambda e: e.tensor_scalar(kf, r, math.pi, None, ALU.is_gt), reads=[r], writes=[kf])
                P.op("dve", lambda e: e.scalar_tensor_tensor(r, kf, -TWO_PI, r, ALU.mult, ALU.add), reads=[kf, r], writes=[r])
                P.op("dve", lambda e: e.tensor_scalar(kf, r, -math.pi, None, ALU.is_lt), reads=[r], writes=[kf])
                P.op("dve", lambda e: e.scalar_tensor_tensor(r, kf, TWO_PI, r, ALU.mult, ALU.add), reads=[kf, r], writes=[r])
            wrap(ang)
            P.op("act", lambda e: e.activation(sinT, ang, AF.Sin), reads=[ang], writes=[sinT])
            if sgn is not None:
                P.op("dve", lambda e: e.tensor_scalar(sinT, sinT, sgn, None, ALU.mult), reads=[sinT, sgn], writes=[sinT])
            P.op("dve", lambda e: e.tensor_scalar(ang, ang, 0.5 * math.pi, None, ALU.add), reads=[ang], writes=[ang])
            wrap(ang)
            P.op("act", lambda e: e.activation(cosT, ang, AF.Sin), reads=[ang], writes=[cosT])

        def diffattn(i):
            j = i // 3
            lam_init = 0.8 - 0.6 * math.exp(-0.3 * i)
            wqkv = W("da_w_qkv")[j]
            w_o = W("da_w_o")[j]
            onT = av(0, (4, S), BF16)
            qT = [av(16384 + q * 4096, (S,), BF16) for q in range(2)]
            kz = [[av(24576 + c * 4096, (S,), BF16) for c in range(2)],
                  [av(OFF_LNP + c * 4096, (S,), BF16) for c in range(2)]]
            vv = [av(32768 + q * 4160, (NT, 130), BF16) for q in range(2)]
            cosT = av(41088, (S,), F32)
            sinT = av(49280, (S,), F32)
            tA = av(57472, (512,), F32)
            tB = av(59520, (512,), F32)
            wo_sb = av(61568, (4, D), BF16)
            gtab = av(OFF_XB, (128,), F32)
            tt = av(OFF_XB + 512, (128,), F32)
            dds = [av(OFF_XB + 1024 + q * 512, (128,), F32) for q in range(2)]
            dn = [av(OFF_XB + 2048 + q * 256, (128,), BF16) for q in range(2)]
            junk = av(OFF_XB + 2560, (128,), BF16)
            accS = av(OFF_XN, (8, 130), F32)
            daw = av(OFF_WST, (5, 8, 128), BF16)
            eT = [av(OFF_WST + 10240 + q * 1024, (512,), BF16) for q in range(5)]
            rope_tables(cosT, sinT, dacst[:, 0:1], dacst[:, 1:2],
                        av(16384, (S,), I32), av(24576, (S,), F32), av(32768, (S,), F32))
            if DA_STOP:
                P.op("dve", lambda e: e.memset(onT, 0.0), writes=[onT])
            for q in range(2):
                for c in range(2):
                    P.op("dve", lambda e, q=q, c=c: e.memset(kz[q][c], 0.0), writes=[kz[q][c]])
            for q in range(2):
                P.op("dve", lambda e, q=q: e.memset(vv[q][:, :, 128:130], 1.0), writes=[vv[q]])
            lq = av(57472, (128,), F32)
            lk = av(57472 + 512, (128,), F32)
            P.dma("sp", "dalam", [(lq, W("da_lam_q")[j:j + 1].rearrange("o c d -> o (c d)").partition_broadcast(128)),
                                  (lk, W("da_lam_k")[j:j + 1].rearrange("o c d -> o (c d)").partition_broadcast(128)),
                                  (gtab, W("da_subln_g")[j:j + 1, :].partition_broadcast(128))],
                  writes=[lq, lk, gtab])
            P.op("dve", lambda e: e.tensor_tensor(lq, lq, lk, ALU.mult), reads=[lq, lk], writes=[lq])
            P.op("dve", lambda e: e.tensor_reduce(dsm[:, 0:2], lq.rearrange("p (c d) -> p c d", d=64), AX.X, ALU.add),
                 reads=[lq], writes=[dsm[:, 0:2]])
            P.op("act", lambda e: e.activation(dsm[:, 0:2], dsm[:, 0:2], AF.Exp), reads=[dsm[:, 0:2]], writes=[dsm[:, 0:2]])
            P.op("dve", lambda e: e.tensor_tensor(dsm[:, 2:3], dsm[:, 1:2], dsm[:, 0:1], ALU.subtract),
                 reads=[dsm[:, 0:2]], writes=[dsm[:, 2:3]])
            P.op("dve", lambda e: e.tensor_scalar(dsm[:, 2:3], dsm[:, 2:3], -lam_init, None, ALU.add),
                 reads=[dsm[:, 2:3]], writes=[dsm[:, 2:3]])
            P.op("dve", lambda e: e.tensor_scalar(gtab, gtab, 1.0 - lam_init, None, ALU.mult), reads=[gtab], writes=[gtab])
            nlam = dsm[:, 2:3]
            epsc = dsm[:, 5:6]
            P.op("dve", lambda e: e.memset(epsc, LN_EPS), writes=[epsc])

            def load_w(h):
                srcs = [wqkv[:, n * 1024 + h * 128:n * 1024 + (h + 1) * 128].rearrange("(c p) n -> p c n", p=128) for n in range(3)]
                P.dma("pool", "daw", [(daw[:, n], srcs[n]) for n in range(3)], writes=[daw[:, 0:3]])
                for n in range(2):
                    src = daw[:, n].rearrange("p c (g f d) -> p c g f d", g=2, f=2)
                    dst = daw[:, 3 + n].rearrange("p c (g f d) -> p c g f d", g=2, f=2)
                    for f in range(2):
                        sv = src[:, :, :, 1 - f, :].rearrange("p c g d -> p (c g) d")
                        dv = dst[:, :, :, f, :].rearrange("p c g d -> p (c g) d")
                        if f == 0:
                            P.op("act", lambda e, sv=sv, dv=dv: e.copy(dv, sv), reads=[daw[:, n]], writes=[daw[:, 3 + n]])
                        else:
                            P.op("dve", lambda e, sv=sv, dv=dv: e.tensor_copy(dv, sv), reads=[daw[:, n]], writes=[daw[:, 3 + n]])

            def project(h):
                pp = h % 2
                for n, dstT in ((0, qT[pp]), (1, None)):
                    for tg in range(4):
                        ba, bb = bankA(), bankA()
                        cols = slice(tg * 512, (tg + 1) * 512)
                        for kc in range(8):
                            mm(ps[:, ba, :], daw[:, n, kc, :], xT[:, kc, cols], kc == 0, kc == 7)
                        for kc in range(8):
                            mm(ps[:, bb, :], daw[:, 3 + n, kc, :], xT[:, kc, cols], kc == 0, kc == 7)
                        P.op("dve", lambda e, ba=ba, cols=cols: e.tensor_tensor(tA, ps[:, ba, :], cosT[:, cols], ALU.mult),
                             reads=[ps[:, ba, :], cosT[:, cols]], writes=[tA])
                        P.op("dve", lambda e, bb=bb, cols=cols: e.tensor_tensor(tB, ps[:, bb, :], sinT[:, cols], ALU.mult),
                             reads=[ps[:, bb, :], sinT[:, cols]], writes=[tB])
                        if dstT is not None:
                            P.op("dve", lambda e, dstT=dstT, cols=cols: e.tensor_tensor(dstT[:, cols], tA, tB, ALU.add),
                                 reads=[tA, tB], writes=[dstT[:, cols]])
                        else:
                            for c in range(2):
                                rows = slice(c * 64, (c + 1) * 64)
                                dk = kz[pp][c]
                                P.op("dve", lambda e, dk=dk, rows=rows, cols=cols: e.tensor_tensor(dk[rows, cols], tA[rows, :], tB[rows, :], ALU.add),
                                     reads=[tA, tB], writes=[dk[:, cols]])
                for t4 in range(4):
                    b = bankA()
                    for tl in range(4):
                        t = t4 * 4 + tl
                        for kc in range(8):
                            mm(ps[:, b, tl * 128:(tl + 1) * 128], xT[:, kc, t * 128:(t + 1) * 128], daw[:, 2, kc, :], kc == 0, kc == 7)
                    dst = vv[pp][:, t4 * 4:(t4 + 1) * 4, 0:128]
                    P.op("act", lambda e, dst=dst, b=b: e.copy(dst, ps[:, b, :].rearrange("p (a n) -> p a n", n=128)),
                         reads=[ps[:, b, :]], writes=[dst])

            ACC_ORDER = [(0, 0), (1, 0), (0, 1), (1, 1), (0, 2), (1, 2), (0, 3), (1, 3)]
            ACC_IDX = {cq: k for k, cq in enumerate(ACC_ORDER)}

            def acc_ap(c, qt):
                k = ACC_IDX[(c, qt)]
                return ps[:, 4 + k // 3, (k % 3) * 130:(k % 3) * 130 + 130]

            deferred = []

            def run_deferred(n=None):
                while deferred and (n is None or n > 0):
                    deferred.pop(0)()
                    if n is not None:
                        n -= 1

            def attend(h, G):
                pp = h % 2
                items = [(jj, c) for jj in range(4 * G + 4) for c in range(2)]
                LAG = 3
                info = {}

                def scores(idx):
                    jj, c = items[idx]
                    q0 = max(jj - 4 * G, 0) * 128
                    sb = bankA()
                    mm(ps[:, sb, q0:512], kz[pp][c][:, jj * 128:(jj + 1) * 128], qT[pp][:, G * 512 + q0:(G + 1) * 512], True, True)
                    et = eT[(idx + 5 * (h * 4 + G)) % 5]
                    P.op("act", lambda e: e.activation(et[:, q0:512], ps[:, sb, q0:512], AF.Exp, scale=0.125),
                         reads=[ps[:, sb, q0:512]], writes=[et[:, q0:512]])
                    if jj >= 4 * G:
                        P.op("dve", lambda e: e.tensor_tensor(et[:, q0:q0 + 128], et[:, q0:q0 + 128], tri[:], ALU.mult),
                             reads=[et[:, q0:q0 + 128], tri[:]], writes=[et[:, q0:q0 + 128]])
                    info[idx] = et

                seq = [(idx, qt) for idx, (jj, c) in enumerate(items) for qt in range(max(jj - 4 * G, 0), 4)]
                first, last = {}, {}
                for n_, (idx, qt) in enumerate(seq):
                    bk = 4 + ACC_IDX[(items[idx][1], qt)] // 3
                    first.setdefault(bk, n_)
                    last[bk] = n_
                pos = {iq: n_ for n_, iq in enumerate(seq)}

                def pv(idx):
                    jj, c = items[idx]
                    et = info.pop(idx)
                    if DA_STOP == 5:
                        return
                    for qt in range(max(jj - 4 * G, 0), 4):
                        bk = 4 + ACC_IDX[(c, qt)] // 3
                        n_ = pos[(idx, qt)]
                        mm(acc_ap(c, qt), et[:, qt * 128:(qt + 1) * 128], vv[pp][:, jj, 0:130], first[bk] == n_, last[bk] == n_)

                for idx in range(len(items)):
                    scores(idx)
                    if idx >= LAG:
                        pv(idx - LAG)
                    run_deferred(1)
                for idx in range(len(items) - LAG, len(items)):
                    pv(idx)
                run_deferred()
                if DA_STOP == 5:
                    return
                for k, (c, qt) in enumerate(ACC_ORDER):
                    src = acc_ap(c, qt)
                    dstv = accS[:, k, 0:130]
                    if (k // 3) % 2:
                        P.op("act", lambda e, src=src, dstv=dstv: e.copy(dstv, src), reads=[src], writes=[dstv])
                    else:
                        P.op("dve", lambda e, src=src, dstv=dstv: e.tensor_copy(dstv, src), reads=[src], writes=[dstv])
                dst = onT[:, h % 4, G * 512:(G + 1) * 512]

                def n1(qt):
                    k1, k2 = ACC_IDX[(0, qt)], ACC_IDX[(1, qt)]
                    o1, s1 = accS[:, k1, 0:128], accS[:, k1, 128:129]
                    o2, s2 = accS[:, k2, 0:128], accS[:, k2, 128:129]
                    rc1, rc2, ss, rs = (dsc[:, qt * 4 + n:qt * 4 + n + 1] for n in range(4))
                    dd = dds[qt % 2]
                    P.op("dve", lambda e: e.reciprocal(rc1, s1), reads=[s1], writes=[rc1])
                    P.op("dve", lambda e: e.reciprocal(rc2, s2), reads=[s2], writes=[rc2])
                    P.op("dve", lambda e: e.tensor_tensor(rc2, rc2, nlam, ALU.mult), reads=[rc2, nlam], writes=[rc2])
                    P.op("dve", lambda e: e.tensor_scalar(tt, o2, rc2, None, ALU.mult), reads=[o2, rc2], writes=[tt])
                    P.op("dve", lambda e: e.scalar_tensor_tensor(dd, o1, rc1, tt, ALU.mult, ALU.add),
                         reads=[o1, rc1, tt], writes=[dd])
                    P.op("act", lambda e: e.activation(junk, dd, AF.Square, accum_out=ss), reads=[dd], writes=[junk, ss])

                def n2(qt):
                    ss, rs = dsc[:, qt * 4 + 2:qt * 4 + 3], dsc[:, qt * 4 + 3:qt * 4 + 4]
                    P.op("dve", lambda e: e.tensor_scalar(rs, ss, 1.0 / 128.0, LN_EPS, ALU.mult, ALU.add), reads=[ss], writes=[rs])
                    P.op("pool", lambda e: e.tensor_tensor(rs, rs, mhalf[:, 0:1], ALU.pow), reads=[rs, mhalf[:, 0:1]], writes=[rs])

                def n3a(qt):
                    rs = dsc[:, qt * 4 + 3:qt * 4 + 4]
                    dd, dq = dds[qt % 2], dn[qt % 2]
                    P.op("dve", lambda e: e.scalar_tensor_tensor(dq, dd, rs, gtab, ALU.mult, ALU.mult),
                         reads=[dd, rs, gtab], writes=[dq])

                def n3b(qt):
                    dq = dn[qt % 2]
                    ptT = ps[:, 7, :].bitcast(BF16).rearrange("p (c n) -> p c n", n=128)
                    P.op("pe", lambda e: e.transpose(ptT[:, qt, :], dq, ident[:]),
                         reads=[dq, ident[:]], writes=[ps[:, 7, :]], signal=True)

                def n4():
                    ptT = ps[:, 7, :].bitcast(BF16).rearrange("p (c n) -> p c n", n=128)
                    P.op("act", lambda e: e.copy(dst.rearrange("p (c n) -> p c n", n=128), ptT[:, 0:4, :]),
                         reads=[ps[:, 7, :]], writes=[dst])

                order = [(n1, 0), (n1, 1), (n2, 0), (n2, 1), (n3a, 0), (n3a, 1), (n1, 2), (n1, 3), (n3b, 0), (n3b, 1),
                         (n2, 2), (n2, 3), (n3a, 2), (n3a, 3), None, None, (n3b, 2), (n3b, 3)]
                for it_ in order:
                    if it_ is None:
                        deferred.append(lambda: None)
                    else:
                        deferred.append(lambda fn=it_[0], qt=it_[1]: fn(qt))
                deferred.append(n4)

            def wo_pass(pi, final):
                run_deferred()
                wv_ = w_o[pi * 512:(pi + 1) * 512, :].rearrange("(c p) n -> p c n", p=128)
                P.dma("pool", "dawo", [(wo_sb[:, 0:2, :], wv_[:, 0:2, :]), (wo_sb[:, 2:4, :], wv_[:, 2:4, :])], writes=[wo_sb])
                pending = []
                for t in range(NT):
                    for hf in range(2):
                        b = bankA()
                        for hh in range(4):
                            mm(ps[:, b, :], onT[:, hh, t * 128:(t + 1) * 128], wo_sb[:, hh, hf * 512:(hf + 1) * 512], hh == 0, hh == 3)
                        accum(t, hf, b, 1.0 / ALPHA)
                    if final:
                        ln_tail(t)
                ln_flush()

            load_w(0)
            project(0)
            for h in range(8):
                if h + 1 < 8:
                    load_w(h + 1)
                for G in range(4):
                    attend(h, G)
                    if G == 1 and h + 1 < 8:
                        project(h + 1)
                if h == 3:
                    wo_pass(0, False)
            run_deferred()
            load_ln(i, 1)
            call_pre_hook()
            wo_pass(1, True)

        def retention(i):
            j = i // 3
            wall = W("ret_w_qkvg")[j]
            w_o = W("ret_w_o")[j]
            cosR = av(0, (S,), F32)
            sinR = av(8192, (S,), F32)
            qT = av(16384, (2, S), BF16)
            kT = av(24576, (2, S), BF16)
            ktok = av(32768, (NT, 256), BF16)
            vv = av(40960, (NT, 512), BF16)
            Rf = av(57344, (2, 512), F32)
            Rb = av(61440, (2, 512), BF16)
            wo_h = av(63488, (4, D), BF16)
            T1, T2, T3, T4 = (av(OFF_XN + q * 2048, (512,), F32) for q in range(4))
            T4s = [T2, T4]
            T2s = [av(OFF_LNP + q * 2048, (512,), F32) for q in range(3)]
            attb = attb_sb
            ybf = [av(OFF_XB + q * 1024, (512,), BF16) for q in range(2)]
            yTn = [av(OFF_XB + 2048 + q * 1024, (4, 128), BF16) for q in range(2)]
            rope_tables(cosR, sinR, retcst[:, 0:1], None,
                        av(16384, (S,), I32), av(24576, (S,), F32), av(32768, (S,), F32))
            loaders = []
            for h in range(4):
                loaders.append(col_loader(wall[:, h * 256:(h + 1) * 256]))
                loaders.append(col_loader(wall[:, 1024 + h * 256:1024 + (h + 1) * 256]))
                for cb in range(2):
                    loaders.append(col_loader(wall[:, 2048 + h * 512 + cb * 256:2048 + h * 512 + (cb + 1) * 256]))
                for cb in range(2):
                    loaders.append(col_loader(wall[:, 4096 + h * 512 + cb * 256:4096 + h * 512 + (cb + 1) * 256]))
            get = make_stream(loaders)
            sv = lambda sl: wst[:, sl, :].rearrange("p (c n) -> p c n", c=8)
            for h in range(4):
                gam = 1.0 - 2.0 ** (-5.0 - h)
                gC = gam ** 128
                qdec = retcst[:, 1 + h:2 + h]
                kdec = retcst[:, 5 + h:6 + h]
                wv_ = w_o[h * 512:(h + 1) * 512, :].rearrange("(c p) n -> p c n", p=128)
                P.dma("pool", "retwo", [(wo_h[:, 0:2, :], wv_[:, 0:2, :]), (wo_h[:, 2:4, :], wv_[:, 2:4, :])], writes=[wo_h])
                for n, dstT in ((0, qT), (1, kT)):
                    wv = sv(get(h * 6 + n))
                    for tg in range(4):
                        ba, bb = bankA(), bankA()
                        cols = slice(tg * 512, (tg + 1) * 512)
                        for kc in range(8):
                            mm(ps[:, ba, :], wv[:, kc, 0:128], xT[:, kc, cols], kc == 0, kc == 7)
                        for kc in range(8):
                            mm(ps[:, bb, :], wv[:, kc, 128:256], xT[:, kc, cols], kc == 0, kc == 7)
                        A_, B_ = ps[:, ba, :], ps[:, bb, :]
                        c_, s_ = cosR[:, cols], sinR[:, cols]
                        P.op("dve", lambda e, A_=A_, c_=c_: e.tensor_tensor(T1, A_, c_, ALU.mult), reads=[A_, c_], writes=[T1])
                        P.op("dve", lambda e, B_=B_, s_=s_: e.tensor_tensor(T2, B_, s_, ALU.mult), reads=[B_, s_], writes=[T2])
                        P.op("dve", lambda e, dstT=dstT, cols=cols: e.tensor_tensor(dstT[:, 0, cols], T1, T2, ALU.subtract),
                             reads=[T1, T2], writes=[dstT[:, 0, cols]])
                        P.op("dve", lambda e, B_=B_, c_=c_: e.tensor_tensor(T3, B_, c_, ALU.mult), reads=[B_, c_], writes=[T3])
                        P.op("dve", lambda e, A_=A_, s_=s_: e.tensor_tensor(T4, A_, s_, ALU.mult), reads=[A_, s_], writes=[T4])
                        P.op("dve", lambda e, dstT=dstT, cols=cols: e.tensor_tensor(dstT[:, 1, cols], T3, T4, ALU.add),
                             reads=[T3, T4], writes=[dstT[:, 1, cols]])
                for n4 in range(4):
                    b = bankA()
                    pt = ps[:, b, :].bitcast(BF16).rearrange("p (c n) -> p c n", n=128)
                    for nl in range(4):
                        n = n4 * 4 + nl
                        for ab in range(2):
                            src = kT[:, ab, n * 128:(n + 1) * 128]
                            P.op("pe", lambda e, src=src, pt=pt, idx=nl * 2 + ab: e.transpose(pt[:, idx, :], src, ident[:]),
                                 reads=[src, ident[:]], writes=[ps[:, b, :]], signal=(nl == 3 and ab == 1))
                    dst = ktok[:, n4 * 4:(n4 + 1) * 4, :].rearrange("p a (b n) -> p (a b) n", n=128)
                    P.op("dve", lambda e, dst=dst, pt=pt, kdec=kdec: e.tensor_scalar(dst, pt, kdec, None, ALU.mult),
                         reads=[ps[:, b, :], kdec], writes=[dst])
                v0 = sv(get(h * 6 + 2))
                v1 = sv(get.peek(h * 6 + 3))
                for n in range(NT):
                    b = bankA()
                    for cb, wv in enumerate((v0, v1)):
                        for kc in range(8):
                            mm(ps[:, b, cb * 256:(cb + 1) * 256], xT[:, kc, n * 128:(n + 1) * 128], wv[:, kc, :], kc == 0, kc == 7)
                    if n % 2:
                        P.op("act", lambda e, n=n, b=b: e.copy(vv[:, n, :], ps[:, b, :]), reads=[ps[:, b, :]], writes=[vv[:, n, :]])
                    else:
                        P.op("dve", lambda e, n=n, b=b: e.tensor_copy(vv[:, n, :], ps[:, b, :]), reads=[ps[:, b, :]], writes=[vv[:, n, :]])
                g0 = sv(get(h * 6 + 4))
                g1 = sv(get.peek(h * 6 + 5))
                DT = retdt[:, h, :]

                def sA(n, DT=DT):
                    tok = slice(n * 128, (n + 1) * 128)
                    ab_ = attb[:, n % 2, :]
                    b = bankA()
                    for ab in range(2):
                        mm(ps[:, b, 0:128], kT[:, ab, tok], qT[:, ab, tok], ab == 0, ab == 1)
                    P.op("dve", lambda e: e.tensor_tensor(ab_, ps[:, b, 0:128], DT, ALU.mult),
                         reads=[ps[:, b, 0:128], DT], writes=[ab_])

                def sB(n, qdec=qdec, gC=gC):
                    tok = slice(n * 128, (n + 1) * 128)
                    ab_ = attb[:, n % 2, :]
                    o_ = T2s[n % 3]
                    bx = bankB()
                    mm(ps[:, bx, :], ab_, vv[:, n, :], True, True)
                    if n > 0:
                        by = bankB()
                        for ab in range(2):
                            mm(ps[:, by, :], qT[:, ab, tok], Rb[:, ab, :], ab == 0, ab == 1)
                        P.op("act", lambda e: e.copy(T1, ps[:, bx, :]), reads=[ps[:, bx, :]], writes=[T1])
                        P.op("dve", lambda e: e.scalar_tensor_tensor(o_, ps[:, by, :], qdec, T1, ALU.mult, ALU.add),
                             reads=[ps[:, by, :], qdec, T1], writes=[o_])
                    else:
                        P.op("act", lambda e: e.copy(o_, ps[:, bx, :]), reads=[ps[:, bx, :]], writes=[o_])
                    if n < NT - 1:
                        for ab in range(2):
                            bz = bankB()
                            mm(ps[:, bz, :], ktok[:, n, ab * 128:(ab + 1) * 128], vv[:, n, :], True, True)
                            if n == 0:
                                P.op("dve", lambda e, ab=ab, bz=bz: e.tensor_copy(Rf[:, ab, :], ps[:, bz, :]),
                                     reads=[ps[:, bz, :]], writes=[Rf[:, ab, :]])
                            else:
                                P.op("dve", lambda e, ab=ab, bz=bz: e.scalar_tensor_tensor(Rf[:, ab, :], Rf[:, ab, :], gC, ps[:, bz, :], ALU.mult, ALU.add),
                                     reads=[Rf[:, ab, :], ps[:, bz, :]], writes=[Rf[:, ab, :]])
                        P.op("act", lambda e: e.copy(Rb, Rf), reads=[Rf], writes=[Rb])

                def sC(n, st_):
                    o_ = T2s[n % 3]
                    q = cnt["sm"] % 8
                    cnt["sm"] += 1
                    stf = small[:, q * 16:q * 16 + 6]
                    mv = small[:, q * 16 + 12:q * 16 + 14]
                    rs = small[:, q * 16 + 14:q * 16 + 15]
                    nmr = small[:, q * 16 + 15:q * 16 + 16]
                    st_[n] = (mv, rs, nmr)
                    P.op("dve", lambda e: e.bn_stats(stf, o_), reads=[o_], writes=[stf])
                    P.op("dve", lambda e: e.bn_aggr(mv, stf), reads=[stf], writes=[mv])
                    P.op("dve", lambda e: e.tensor_scalar(rs, mv[:, 1:2], LN_EPS, None, ALU.add), reads=[mv], writes=[rs])
                    P.op("pool", lambda e: e.tensor_tensor(rs, rs, mhalf[:, 0:1], ALU.pow), reads=[rs, mhalf[:, 0:1]], writes=[rs])

                def sD(n, st_, g0=g0, g1=g1):
                    tok = slice(n * 128, (n + 1) * 128)
                    o_ = T2s[n % 3]
                    sg = T4s[n % 2]
                    yb = ybf[n % 2]
                    mv, rs, nmr = st_.pop(n)
                    bg = bankA()
                    for cb, wv in enumerate((g0, g1)):
                        for kc in range(8):
                            mm(ps[:, bg, cb * 256:(cb + 1) * 256], xT[:, kc, tok], wv[:, kc, :], kc == 0, kc == 7)
                    P.op("act", lambda e: e.activation(sg, ps[:, bg, :], AF.Silu), reads=[ps[:, bg, :]], writes=[sg])
                    P.op("dve", lambda e: e.tensor_scalar(nmr, mv[:, 0:1], rs, -1.0, ALU.mult, ALU.mult),
                         reads=[mv, rs], writes=[nmr])
                    P.op("act", lambda e: e.activation(T3, o_, AF.Identity, bias=nmr, scale=rs),
                         reads=[o_, nmr, rs], writes=[T3])
                    P.op("dve", lambda e: e.tensor_tensor(yb, T3, sg, ALU.mult), reads=[T3, sg], writes=[yb])

                def sE(n):
                    yb = ybf[n % 2]
                    b = bankA()
                    pt = ps[:, b, :].bitcast(BF16).rearrange("p (c n) -> p c n", n=128)
                    for ec in range(4):
                        src = yb[:, ec * 128:(ec + 1) * 128]
                        P.op("pe", lambda e, ec=ec, src=src: e.transpose(pt[:, ec, :], src, ident[:]),
                             reads=[src, ident[:]], writes=[ps[:, b, :]], signal=(ec == 3))
                    yt = yTn[n % 2]
                    P.op("act", lambda e: e.copy(yt, pt[:, 0:4, :]), reads=[ps[:, b, :]], writes=[yt])

                def sF(n):
                    yt = yTn[n % 2]
                    for hf in range(2):
                        b = bankA()
                        for ec in range(4):
                            mm(ps[:, b, :], yt[:, ec, :], wo_h[:, ec, hf * 512:(hf + 1) * 512], ec == 0, ec == 3)
                        accum(n, hf, b, 1.0 / ALPHA)

                st_ = {}
                for it in range(NT + 5):
                    if it < NT:
                        sA(it)
                    if 0 <= it - 1 < NT:
                        sB(it - 1)
                    if 0 <= it - 2 < NT:
                        sC(it - 2, st_)
                    if 0 <= it - 3 < NT:
                        sD(it - 3, st_)
                    if 0 <= it - 4 < NT:
                        sE(it - 4)
                    if 0 <= it - 5 < NT:
                        sF(it - 5)
            load_ln(i, 1)
            call_pre_hook()
            if not DEBUG_NO_LN:
                for t in range(NT):
                    ln_tail(t)
                ln_flush()

        xv = dr["x"].rearrange("(t p) d -> p t d", p=128)
        for t4 in range(4):
            P.dma("sp", f"xin{t4}", [(xres[:, t4 * 4:(t4 + 1) * 4, :], xv[:, t4 * 4:(t4 + 1) * 4, :])],
                  writes=[xres[:, t4 * 4:(t4 + 1) * 4, :]])
        for t in range(NT):
            ln_tail(t, cast_only=True)
        ln_flush()
        xa_gen = {"g": None}
        for n_item, item in enumerate(plan):
            final_store["on"] = (n_item == len(plan) - 1) and item[0] in ("ffn", "xa")
            nxt = plan[n_item + 1] if n_item + 1 < len(plan) else None
            if item[0] == "mix" and nxt is not None and nxt[0] == "xa":
                xa_gen["g"] = xa(nxt[1])
                pre_hook["fn"] = lambda: next(xa_gen["g"])
            if item[0] == "ffn":
                ffn(item[1], item[2])
            elif item[0] == "xa":
                if xa_gen["g"] is None:
                    xa_gen["g"] = xa(item[1])
                    next(xa_gen["g"])
                for _ in xa_gen["g"]:
                    pass
                xa_gen["g"] = None
            elif item[0] == "mix" and item[1] % 3 == 1:
                poolmix(item[1])
            elif item[0] == "mix" and item[1] % 3 == 0:
                diffattn(item[1])
            elif item[0] == "mix" and item[1] % 3 == 2:
                retention(item[1])
            else:
                raise NotImplementedError(item)
            call_pre_hook()
        outs = list(final_store["idents"])
        if not outs:
            for t4 in range(4):
                outs.append(P.dma("sp", f"yout{t4}", [(yv[:, t4 * 4:(t4 + 1) * 4, :], xres[:, t4 * 4:(t4 + 1) * 4, :])],
                                  reads=[xres[:, t4 * 4:(t4 + 1) * 4, :]]))
        P.wait_all("sp", outs)
        P.emit()
    return nc, P, sorted(k for k in dr)


def const_inputs():
    c = {"ident": np.eye(128, dtype=np.float32)}
    pm = np.zeros((8, 128, 128), np.float32)
    pc = np.zeros((1, 4, 128), np.float32)
    sidx = np.arange(128)[:, None]
    tidx = np.arange(128)[None, :]
    for g, w in enumerate((2, 4, 8, 16)):
        pm[2 * g] = ((tidx - sidx >= 0) & (tidx - sidx < w)).astype(np.float32)
        pm[2 * g + 1] = ((tidx + 128 - sidx) < w).astype(np.float32)
        pc[0, g] = 1.0 / np.minimum(np.arange(128) + 1, w)
    kk = np.arange(128)
    c["trimask"] = (kk[:, None] <= kk[None, :]).astype(np.float32)
    inv = (1.0 / (np.float32(10000.0) ** (np.arange(0, 64, 2, dtype=np.float32) / np.float32(64)))).astype(np.float32)
    dac = np.zeros((128, 2), np.float32)
    dac[:, 0] = inv[kk % 32]
    dac[:, 1] = np.where((kk % 64) < 32, -1.0, 1.0)
    c["dacst"] = dac
    rc = np.zeros((128, 16), np.float32)
    rc[:, 0] = (1.0 / (np.float32(10000.0) ** np.linspace(0.0, 1.0, 128, dtype=np.float32))).astype(np.float32)
    rdt = np.zeros((4, 128, 128), np.float32)
    for h in range(4):
        gam = 1.0 - 2.0 ** (-5.0 - h)
        rc[:, 1 + h] = gam ** (kk + 1.0)
        rc[:, 5 + h] = gam ** (127.0 - kk) / 16.0
        dif = kk[None, :] - kk[:, None]
        rdt[h] = np.where(dif >= 0, gam ** np.maximum(dif, 0), 0.0) / 16.0
    c["retcst"] = rc
    c["retdt"] = rdt
    c["poolmat"] = pm
    c["poolcnt"] = pc
    return c


def layer_plan(i):
    return [("ffn", i, 0), ("mix", i), ("xa", i), ("ffn", i, 1)]


FUSED = True
_CACHE = {}


def _program(plan):
    key = tuple(plan)
    if key not in _CACHE:
        _CACHE[key] = build_program(list(plan))
    return _CACHE[key]


def kernel(**inputs):
    n = 8
    consts = const_inputs()
    x = np.asarray(inputs["x"], dtype=np.float32)
    cur = [np.ascontiguousarray(x[b]) for b in range(n)]
    if FUSED:
        plans = [sum((layer_plan(i) for i in range(DEPTH)), [])]
    else:
        plans = [layer_plan(i) for i in range(DEPTH)]
    for plan in plans:
        nc, _, names = _program(plan)
        in_maps = []
        for b in range(n):
            m = {"x": cur[b],
                 "mem": np.ascontiguousarray(inputs["mem"][b], dtype=np.float32),
                 "positions": np.ascontiguousarray(inputs["positions"][b:b + 1], dtype=np.int32)}
            m.update(consts)
            for name in names:
                if name not in m:
                    m[name] = np.ascontiguousarray(inputs[name], dtype=np.float32)
            in_maps.append({k: m[k] for k in names})
        res = run_bass_kernel_spmd(nc, in_maps, core_ids=list(range(n)))
        cur = [np.asarray(res.results[b]["y"], dtype=np.float32) for b in range(n)]
    return np.stack(cur, axis=0)
```

```python
import math
from contextlib import ExitStack

import numpy as np
import concourse.bass as bass
import concourse.mybir as mybir
from concourse.bass_utils import run_bass_kernel_spmd

F32 = mybir.dt.float32
BF16 = mybir.dt.bfloat16
I32 = mybir.dt.int32
AF = mybir.ActivationFunctionType
ALU = mybir.AluOpType
AX = mybir.AxisListType
DSZ = {F32: 4, BF16: 2, I32: 4}
ENGS = ("pe", "act", "dve", "pool", "sp")

S = 2048
D = 1024
NT = 16
DEPTH = 4
FF = 2816
NJ = 22
MEM = 256
ALPHA = (2 * DEPTH) ** 0.25
LN_EPS = 1e-5
EPS_S = LN_EPS / (ALPHA * ALPHA)


def ap_interval(ap):
    sz = DSZ[ap.dtype]
    pat = ap.ap
    pstep = pat[0][0]
    off = int(ap.offset)
    lo = off % pstep if pstep > 0 else off
    span = 0
    for st, cnt in pat[1:]:
        span += abs(st) * (cnt - 1)
    lo_b, hi_b = lo * sz, (lo + span + 1) * sz
    if ap.tensor.name == "ps":
        lo_b = (lo_b // 2048) * 2048
        hi_b = ((hi_b + 2047) // 2048) * 2048
    return ap.tensor.name, lo_b, hi_b


class Prog:
    def __init__(self, nc):
        self.nc = nc
        self.ops = {e: [] for e in ENGS}
        self.cnt = {e: 0 for e in ENGS}
        self.seen = {e: {} for e in ENGS}
        self.rec = {}
        self.dma_cnt = {}
        self.dma_sem_names = []

    def _overl(self, key):
        name, lo, hi = key
        return [r for r in self.rec.get(name, ()) if r[0] < hi and lo < r[1]]

    def _deps(self, eng, reads, writes):
        deps = {}

        def add(k, v):
            if k == "pe" and eng == "pe":
                return
            if deps.get(k, 0) < v:
                deps[k] = v

        for key in reads:
            for r in self._overl(key):
                if r[2] is not None:
                    add(*r[2])
                if key[0] == "ps":
                    for k, v in r[3].items():
                        if k != eng:
                            add(k, v)
        for key in writes:
            for r in self._overl(key):
                if r[2] is not None:
                    add(*r[2])
                for k, v in r[3].items():
                    add(k, v)
        out = []
        seen = self.seen[eng]
        for k, v in deps.items():
            if seen.get(k, 0) < v:
                seen[k] = v
                out.append((k, v))
        return out

    def _update(self, ident, reads, writes):
        for name, lo, hi in reads:
            lst = self.rec.setdefault(name, [])
            covered = False
            for r in lst:
                if r[0] < hi and lo < r[1]:
                    if r[3].get(ident[0], 0) < ident[1]:
                        r[3][ident[0]] = ident[1]
                    if r[0] <= lo and hi <= r[1]:
                        covered = True
            if not covered:
                lst.append([lo, hi, None, {ident[0]: ident[1]}])
        for name, lo, hi in writes:
            lst = self.rec.setdefault(name, [])
            lst[:] = [r for r in lst if not (lo <= r[0] and r[1] <= hi)]
            lst.append([lo, hi, ident, {}])

    def op(self, eng, fn, reads=(), writes=(), signal=True):
        rk = [ap_interval(a) for a in reads]
        wk = [ap_interval(a) for a in writes]
        waits = self._deps(eng, rk, wk)
        if signal:
            self.cnt[eng] += 1
            ident = (eng, self.cnt[eng])
        else:
            ident = (eng, self.cnt[eng] + 1)
        self._update(ident, rk, wk)
        self.ops[eng].append((waits, fn, eng if signal else None, 1))

    def dma(self, eng, slot, pairs, reads=(), writes=()):
        rk = [ap_interval(a) for a in reads]
        wk = [ap_interval(a) for a in writes]
        key = "dma:" + slot
        waits = self._deps(eng, rk, wk)
        if key not in self.dma_cnt:
            self.dma_cnt[key] = 0
            self.dma_sem_names.append(key)
        self.dma_cnt[key] += 16 * len(pairs)
        ident = (key, self.dma_cnt[key])
        self._update(ident, rk, wk)
        for n, (o, i) in enumerate(pairs):
            self.ops[eng].append((waits if n == 0 else [],
                                  lambda e, o=o, i=i: e.dma_start(out=o, in_=i), key, 16))
        return ident

    def wait_all(self, eng, idents):
        waits = []
        for k, v in idents:
            if self.seen[eng].get(k, 0) < v:
                self.seen[eng][k] = v
                waits.append((k, v))
        self.ops[eng].append((waits, None, None, 0))

    def check(self):
        pos = {e: 0 for e in ENGS}
        val = {}
        prog = True
        while prog:
            prog = False
            for e in ENGS:
                lst = self.ops[e]
                while pos[e] < len(lst):
                    waits, fn, sk, inc = lst[pos[e]]
                    if all(val.get(k, 0) >= v for k, v in waits):
                        if sk is not None:
                            val[sk] = val.get(sk, 0) + inc
                        pos[e] += 1
                        prog = True
                    else:
                        break
        for e in ENGS:
            if pos[e] != len(self.ops[e]):
                waits = self.ops[e][pos[e]][0]
                raise RuntimeError(f"deadlock: engine {e} stuck at op {pos[e]}/{len(self.ops[e])} waits={waits}")

    def emit(self):
        nc = self.nc
        self.check()
        with ExitStack() as es:
            sems = {}
            for e in ("pe", "act", "dve", "pool"):
                sems[e] = es.enter_context(nc.semaphore("s_" + e))
            for k in self.dma_sem_names:
                sems[k] = es.enter_context(nc.semaphore("s_" + k.replace(":", "_")))
            block = es.enter_context(nc.Block())

            def run(e, lst):
                for waits, fn, sk, inc in lst:
                    for k, v in waits:
                        e.wait_ge(sems[k], v)
                    if fn is None:
                        continue
                    ins = fn(e)
                    if sk is not None:
                        ins.then_inc(sems[sk], inc)

            @block.tensor
            def _(e):
                run(e, self.ops["pe"])

            @block.scalar
            def _(e):
                run(e, self.ops["act"])

            @block.vector
            def _(e):
                run(e, self.ops["dve"])

            @block.gpsimd
            def _(e):
                run(e, self.ops["pool"])

            @block.sync
            def _(e):
                run(e, self.ops["sp"])


WEIGHT_SPECS = [
    ("ffn_w_in", [4, 2, 1024, 5632]), ("ffn_w_out", [4, 2, 2816, 1024]),
    ("ln_g", [4, 4, 1024]), ("ln_b", [4, 4, 1024]),
    ("da_w_qkv", [2, 1024, 3072]), ("da_w_o", [2, 1024, 1024]),
    ("da_lam_q", [2, 2, 64]), ("da_lam_k", [2, 2, 64]), ("da_subln_g", [2, 128]),
    ("pool_w", [1, 4, 256, 256]), ("pool_b", [1, 4, 256]), ("pool_scale", [1, 1024]),
    ("ret_w_qkvg", [1, 1024, 6144]), ("ret_w_o", [1, 2048, 1024]),
    ("xa_wq", [4, 1024, 1024]), ("xa_wkv", [4, 1024, 2048]), ("xa_wo", [4, 1024, 1024]),
]

DEBUG_NO_LN = False
DA_STOP = 0
ARENA_BYTES = 109056
OFF_WST = 72192
OFF_LNP = 88576
OFF_XB = 96768
OFF_XN = 100864


def build_program(plan):
    nc = bass.Bass("TRN2", target_bir_lowering=False)
    dr = {}
    dr["x"] = nc.dram_tensor("x", [S, D], F32, kind="ExternalInput").ap()
    dr["mem"] = nc.dram_tensor("mem", [MEM, D], F32, kind="ExternalInput").ap()
    dr["positions"] = nc.dram_tensor("positions", [1, S], I32, kind="ExternalInput").ap()
    dr["ident"] = nc.dram_tensor("ident", [128, 128], F32, kind="ExternalInput").ap()
    dr["trimask"] = nc.dram_tensor("trimask", [128, 128], F32, kind="ExternalInput").ap()
    dr["dacst"] = nc.dram_tensor("dacst", [128, 2], F32, kind="ExternalInput").ap()
    dr["retcst"] = nc.dram_tensor("retcst", [128, 16], F32, kind="ExternalInput").ap()
    dr["retdt"] = nc.dram_tensor("retdt", [4, 128, 128], F32, kind="ExternalInput").ap()
    dr["poolmat"] = nc.dram_tensor("poolmat", [8, 128, 128], F32, kind="ExternalInput").ap()
    dr["poolcnt"] = nc.dram_tensor("poolcnt", [1, 4, 128], F32, kind="ExternalInput").ap()
    wshape = dict(WEIGHT_SPECS)

    def W(name):
        if name not in dr:
            dr[name] = nc.dram_tensor(name, wshape[name], F32, kind="ExternalInput").ap()
        return dr[name]
    y = nc.dram_tensor("y", [S, D], F32, kind="ExternalOutput").ap()
    P = Prog(nc)

    with ExitStack() as es:
        def T(name, shape, dt):
            return es.enter_context(nc.sbuf_tensor(name, shape, dt))

        xres = T("xres", [128, NT, D], F32)
        xT = T("xT", [128, 8, S], BF16)
        ident = T("ident_sb", [128, 128], BF16)
        small = T("small", [128, 128], F32)
        xst = T("xst", [128, 2, 16], F32)
        dsm = T("dsm", [128, 8], F32)
        dsc = T("dsc", [128, 16], F32)
        dacst = T("dacst_sb", [128, 2], F32)
        retcst = T("retcst_sb", [128, 16], F32)
        retdt = T("retdt_sb", [128, 4, 128], F32)
        attb_sb = T("attb_sb", [128, 2, 128], BF16)
        tri = T("tri_sb", [128, 128], BF16)
        ones = T("ones", [128, 128], BF16)
        mhalf = T("mhalf", [128, 4], F32)
        arena = T("arena", [128, ARENA_BYTES // 2], BF16)
        ps = es.enter_context(nc.psum_tensor("ps", [128, 8, 512], F32))

        def av(off, shape, dt):
            n = int(np.prod(shape)) * DSZ[dt]
            a = arena[:, off // 2:(off + n) // 2]
            if dt != BF16:
                a = a.bitcast(dt)
            if len(shape) == 2:
                a = a.rearrange("p (a b) -> p a b", b=shape[1])
            elif len(shape) == 3:
                a = a.rearrange("p (a b c) -> p a b c", b=shape[1], c=shape[2])
            return a

        wst = av(OFF_WST, (4, 2048), BF16)
        lnp = av(OFF_LNP, (2, D), F32)
        wres = av(49152, (11, D), BF16)
        st = {"wst": 0, "pa": 0, "pb": 0}

        def next_wst():
            s = st["wst"]
            st["wst"] = (s + 1) % 4
            return s

        def bankA():
            b = st["pa"]
            st["pa"] = (b + 1) % 4
            return b

        def bankB():
            b = st["pb"]
            st["pb"] = (b + 1) % 4
            return 4 + b

        def mm(out, lhsT, rhs, start, stop):
            P.op("pe", lambda e: e.matmul(out, lhsT, rhs, start=start, stop=stop),
                 reads=[lhsT, rhs], writes=[out], signal=stop)

        P.dma("pool", "ident", [(ident[:], dr["ident"])], writes=[ident[:]])
        P.op("dve", lambda e: e.memset(mhalf[:], -0.5), writes=[mhalf[:]])
        P.op("dve", lambda e: e.memset(ones[:], 1.0), writes=[ones[:]])
        P.dma("pool", "tri", [(tri[:], dr["trimask"])], writes=[tri[:]])
        P.dma("sp", "dacst", [(dacst[:], dr["dacst"])], writes=[dacst[:]])
        P.dma("sp", "retcst", [(retcst[:], dr["retcst"]), (retdt[:], dr["retdt"].rearrange("h p n -> p h n"))],
              writes=[retcst[:], retdt[:]])

        xb_bufs = [av(OFF_XB + i * 2048, (D,), BF16) for i in range(2)]
        xn_bufs = [av(OFF_XN + i * 4096, (D,), F32) for i in range(2)]
        cnt = {"xb": 0, "xn": 0, "sm": 0}

        def load_ln(i, k):
            P.dma("sp", "lnp", [(lnp[:, 0, :], W("ln_g")[i, k:k + 1, :].partition_broadcast(128)),
                                (lnp[:, 1, :], W("ln_b")[i, k:k + 1, :].partition_broadcast(128))],
                  writes=[lnp[:]])

        ln_q = []
        final_store = {"on": False, "idents": []}
        yv = y.rearrange("(t p) d -> p t d", p=128)

        def _s1(en):
            t = en["t"]
            q = cnt["sm"] % 8
            cnt["sm"] += 1
            stf = small[:, q * 16:q * 16 + 12]
            stt = stf.rearrange("p (a b) -> p a b", b=6)
            mv = small[:, q * 16 + 12:q * 16 + 14]
            rs = small[:, q * 16 + 14:q * 16 + 15]
            nmr = small[:, q * 16 + 15:q * 16 + 16]
            en.update(mv=mv, rs=rs, nmr=nmr)
            for h in range(2):
                src = xres[:, t, h * 512:(h + 1) * 512]
                P.op("dve", lambda e, h=h, src=src: e.bn_stats(stt[:, h, :], src), reads=[src], writes=[stt[:, h, :]])
            P.op("dve", lambda e: e.bn_aggr(mv, stf), reads=[stf], writes=[mv])
            P.op("dve", lambda e: e.tensor_scalar(rs, mv[:, 1:2], EPS_S, None, ALU.add), reads=[mv], writes=[rs])
            P.op("pool", lambda e: e.tensor_tensor(rs, rs, mhalf[:, 0:1], ALU.pow), reads=[rs, mhalf[:, 0:1]], writes=[rs])

        def _s2(en):
            t, mv, rs, nmr = en["t"], en["mv"], en["rs"], en["nmr"]
            P.op("dve", lambda e: e.tensor_scalar(nmr, mv[:, 0:1], rs, -1.0, ALU.mult, ALU.mult), reads=[mv, rs], writes=[nmr])
            xn = xn_bufs[cnt["xn"] % 2]
            cnt["xn"] += 1
            en["xn"] = xn
            P.op("act", lambda e: e.activation(xn[:], xres[:, t, :], AF.Identity, bias=nmr, scale=rs),
                 reads=[xres[:, t, :], nmr, rs], writes=[xn[:]])

        def _s3(en):
            t = en["t"]
            if "xn" in en:
                xn = en["xn"]
                P.op("dve", lambda e: e.tensor_tensor(xn[:], xn[:], lnp[:, 0, :], ALU.mult), reads=[xn[:], lnp[:, 0, :]], writes=[xn[:]])
                P.op("dve", lambda e: e.tensor_tensor(xres[:, t, :], xn[:], lnp[:, 1, :], ALU.add),
                     reads=[xn[:], lnp[:, 1, :]], writes=[xres[:, t, :]])
            if final_store["on"]:
                if t % 4 == 3:
                    t4 = t // 4
                    final_store["idents"].append(
                        P.dma("sp", f"yout{t4}", [(yv[:, t4 * 4:(t4 + 1) * 4, :], xres[:, t4 * 4:(t4 + 1) * 4, :])],
                              reads=[xres[:, t4 * 4:(t4 + 1) * 4, :]]))
                return "done"
            xb = xb_bufs[cnt["xb"] % 2]
            cnt["xb"] += 1
            en["xb"] = xb
            P.op("act", lambda e: e.copy(xb[:], xres[:, t, :]), reads=[xres[:, t, :]], writes=[xb[:]])

        def _s4(en):
            t, xb = en["t"], en["xb"]
            b = bankA()
            pt = ps[:, b, :].bitcast(BF16).rearrange("p (c n) -> p c n", n=128)
            for c in range(8):
                P.op("pe", lambda e, c=c: e.transpose(pt[:, c, :], xb[:, c * 128:(c + 1) * 128], ident[:]),
                     reads=[xb[:, c * 128:(c + 1) * 128], ident[:]], writes=[ps[:, b, :]], signal=(c == 7))
            dst = xT[:, :, t * 128:(t + 1) * 128]
            if en["n"] % 2:
                P.op("act", lambda e: e.copy(dst, pt), reads=[ps[:, b, :]], writes=[dst])
            else:
                P.op("dve", lambda e: e.tensor_copy(dst, pt), reads=[ps[:, b, :]], writes=[dst])

        STAGES = (_s1, _s2, _s3, _s4)

        def _ln_advance():
            for en in list(ln_q):
                r = STAGES[en["stage"]](en)
                en["stage"] = 4 if r == "done" else en["stage"] + 1
            ln_q[:] = [en for en in ln_q if en["stage"] < 4]

        def ln_tail(t, cast_only=False):
            cnt["ln"] = cnt.get("ln", 0) + 1
            ln_q.append({"t": t, "stage": 2 if cast_only else 0, "n": cnt["ln"]})
            _ln_advance()

        def ln_flush():
            while ln_q:
                _ln_advance()

        def accum(t, h, bank, scale):
            dst = xres[:, t, h * 512:(h + 1) * 512]
            P.op("dve", lambda e: e.scalar_tensor_tensor(dst, ps[:, bank, :], scale, dst, ALU.mult, ALU.add),
                 reads=[ps[:, bank, :], dst], writes=[dst])

        def make_stream(loaders, look=3):
            state = {"n": 0, "slots": []}

            def get(n):
                while state["n"] < min(len(loaders), n + 1 + look):
                    sl = next_wst()
                    loaders[state["n"]](sl)
                    state["slots"].append(sl)
                    state["n"] += 1
                return state["slots"][n]
            get.peek = lambda n: state["slots"][n]
            return get

        def col_loader(src):
            def f(sl):
                P.dma("pool", f"wst{sl}", [(wst[:, sl, :].rearrange("p (c n) -> p c n", c=8),
                                            src.rearrange("(c p) n -> p c n", p=128))], writes=[wst[:, sl, :]])
            return f

        def ffn(i, k):
            w_in = W("ffn_w_in")[i, k]
            w_out = W("ffn_w_out")[i, k]
            actT = av(0, (11, S), BF16)
            sil = [av(45056 + q * 2048, (512,), F32) for q in range(2)]
            load_ln(i, 0 if k == 0 else 3)
            pending = []

            def win_loader(j):
                def f(sl):
                    wv = wst[:, sl, :].rearrange("p (g c n) -> p g c n", g=2, c=8)
                    P.dma("pool", f"wst{sl}",
                          [(wv[:, 0], w_in[:, j * 128:(j + 1) * 128].rearrange("(c p) n -> p c n", p=128)),
                           (wv[:, 1], w_in[:, FF + j * 128:FF + (j + 1) * 128].rearrange("(c p) n -> p c n", p=128))],
                          writes=[wst[:, sl, :]])
                return f
            get = make_stream([win_loader(j) for j in range(NJ)])
            for bi, (j0, j1) in enumerate(((0, 11), (11, 22))):
                nj = j1 - j0
                wo_v = w_out[j0 * 128:j1 * 128, :].rearrange("(c p) n -> p c n", p=128)
                P.dma("pool", "wres", [(wres[:, 0:6, :], wo_v[:, 0:6, :]), (wres[:, 6:nj, :], wo_v[:, 6:nj, :])],
                      writes=[wres[:, 0:nj, :]])
                for j in range(j0, j1):
                    s = get(j)
                    wv = wst[:, s, :].rearrange("p (g c n) -> p g c n", g=2, c=8)
                    for tg in range(4):
                        bg, bu = bankA(), bankA()
                        xs = lambda c: xT[:, c, tg * 512:(tg + 1) * 512]
                        for c in range(8):
                            mm(ps[:, bg, :], wv[:, 0, c, :], xs(c), c == 0, c == 7)
                        for c in range(8):
                            mm(ps[:, bu, :], wv[:, 1, c, :], xs(c), c == 0, c == 7)
                        sl = sil[(j * 4 + tg) % 2]
                        P.op("act", lambda e, sl=sl, bg=bg: e.activation(sl, ps[:, bg, :], AF.Silu),
                             reads=[ps[:, bg, :]], writes=[sl])
                        dst = actT[:, j - j0, tg * 512:(tg + 1) * 512]
                        P.op("dve", lambda e, dst=dst, bu=bu, sl=sl: e.tensor_tensor(dst, ps[:, bu, :], sl, ALU.mult),
                             reads=[ps[:, bu, :], sl], writes=[dst])
                for t in range(NT):
                    for h in range(2):
                        b = bankB()
                        for jj in range(nj):
                            mm(ps[:, b, :], actT[:, jj, t * 128:(t + 1) * 128], wres[:, jj, h * 512:(h + 1) * 512],
                               jj == 0, jj == nj - 1)
                        accum(t, h, b, 0.5 / ALPHA)
                    if bi == 1:
                        ln_tail(t)
            ln_flush()

        pre_hook = {"fn": None}

        def call_pre_hook():
            fn, pre_hook["fn"] = pre_hook["fn"], None
            if fn is not None:
                fn()

        def prep_mem(memT):
            mf = av(OFF_XN, (2, D), F32)
            mb = av(OFF_XB, (2, D), BF16)
            P.dma("sp", "memin", [(mf, dr["mem"].rearrange("(t p) d -> p t d", p=128))], writes=[mf])
            P.op("act", lambda e: e.copy(mb, mf), reads=[mf], writes=[mb])
            for mt in range(2):
                b = bankA()
                pt = ps[:, b, :].bitcast(BF16).rearrange("p (c n) -> p c n", n=128)
                for c in range(8):
                    src = mb[:, mt, c * 128:(c + 1) * 128]
                    P.op("pe", lambda e, c=c, src=src, pt=pt: e.transpose(pt[:, c, :], src, ident[:]),
                         reads=[src, ident[:]], writes=[ps[:, b, :]], signal=(c == 7))
                dst = memT[:, :, mt * 128:(mt + 1) * 128]
                P.op("act", lambda e, dst=dst, pt=pt: e.copy(dst, pt), reads=[ps[:, b, :]], writes=[dst])

        def xa(i):
            wq, wkv, wo = W("xa_wq")[i], W("xa_wkv")[i], W("xa_wo")[i]
            qT = av(0, (8, S), BF16)
            kT = av(32768, (8, MEM), BF16)
            vx = av(36864, (2, D), BF16)
            pf = [av(40960 + q * 4096, (4, 256), F32) for q in range(2)]
            pn = [av(49152 + q * 2048, (4, 256), BF16) for q in range(2)]
            pT = av(53248, (8, 512), BF16)
            oT = av(61440, (8, 512), BF16)
            wo_sb = av(OFF_WST, (8, D), BF16)
            memT = av(53248, (8, MEM), BF16)
            prep_mem(memT)
            loaders = [col_loader(wkv[:, cc * 256:(cc + 1) * 256]) for cc in range(8)]
            loaders += [col_loader(wq[:, cc * 256:(cc + 1) * 256]) for cc in range(4)]
            get = make_stream(loaders)
            ev = {"n": 0}

            def evac(dst, src):
                ev["n"] += 1
                if ev["n"] % 2:
                    P.op("act", lambda e: e.copy(dst, src), reads=[src], writes=[dst])
                else:
                    P.op("dve", lambda e: e.tensor_copy(dst, src), reads=[src], writes=[dst])

            for cc in range(4):
                wv = wst[:, get(cc), :].rearrange("p (c n) -> p c n", c=8)
                for sub in range(2):
                    b = bankA()
                    for kc in range(8):
                        mm(ps[:, b, 0:MEM], wv[:, kc, sub * 128:(sub + 1) * 128], memT[:, kc, :], kc == 0, kc == 7)
                    evac(kT[:, cc * 2 + sub, :], ps[:, b, 0:MEM])
            for cc in range(4):
                wv = wst[:, get(4 + cc), :].rearrange("p (c n) -> p c n", c=8)
                for mt in range(2):
                    b = bankA()
                    for kc in range(8):
                        mm(ps[:, b, 0:256], memT[:, kc, mt * 128:(mt + 1) * 128], wv[:, kc, :], kc == 0, kc == 7)
                    evac(vx[:, mt, cc * 256:(cc + 1) * 256], ps[:, b, 0:256])
            yield
            load_ln(i, 2)
            for cc in range(4):
                wv = wst[:, get(8 + cc), :].rearrange("p (c n) -> p c n", c=8)
                for sub in range(2):
                    for tg in range(4):
                        b = bankA()
                        for kc in range(8):
                            mm(ps[:, b, :], wv[:, kc, sub * 128:(sub + 1) * 128], xT[:, kc, tg * 512:(tg + 1) * 512],
                               kc == 0, kc == 7)
                        evac(qT[:, cc * 2 + sub, tg * 512:(tg + 1) * 512], ps[:, b, :])
            wo_v = wo.rearrange("(c p) n -> p c n", p=128)
            P.dma("pool", "wo", [(wo_sb[:, 0:4, :], wo_v[:, 0:4, :]), (wo_sb[:, 4:8, :], wo_v[:, 4:8, :])],
                  writes=[wo_sb])
            scale = 1.0 / 16.0

            def sc_view(t):
                b2 = 4 + 2 * (t % 2)
                return ps[:, b2:b2 + 2, :].rearrange("p b (h m) -> p (b h) m", m=MEM)

            def stage_scores(t):
                sc = sc_view(t)
                for h in range(4):
                    for dc in range(2):
                        mm(sc[:, h, :], qT[:, 2 * h + dc, t * 128:(t + 1) * 128], kT[:, 2 * h + dc, :], dc == 0, dc == 1)

            def stage_softmax(t):
                q = t % 2
                sc = sc_view(t)
                mx, nb, sm, rs = (xst[:, q, 4 * n:4 * n + 4] for n in range(4))
                P.op("dve", lambda e: e.tensor_reduce(mx, sc, AX.X, ALU.max), reads=[sc], writes=[mx])
                P.op("dve", lambda e: e.tensor_scalar(nb, mx, -scale, None, ALU.mult), reads=[mx], writes=[nb])
                for h in range(4):
                    P.op("act", lambda e, h=h: e.activation(
                        pf[q][:, h, :], sc[:, h, :], AF.Exp, bias=nb[:, h:h + 1], scale=scale, accum_out=sm[:, h:h + 1]),
                        reads=[sc[:, h, :], nb], writes=[pf[q][:, h, :], sm[:, h:h + 1]])
                P.op("dve", lambda e: e.reciprocal(rs, sm), reads=[sm], writes=[rs])
                for h in range(4):
                    P.op("act", lambda e, h=h: e.activation(pn[q][:, h, :], pf[q][:, h, :], AF.Identity, scale=rs[:, h:h + 1]),
                         reads=[pf[q][:, h, :], rs], writes=[pn[q][:, h, :]])

            def stage_transpose(t):
                q = t % 2
                tl = t % 4
                b = bankA()
                pt = ps[:, b, :].bitcast(BF16).rearrange("p (c n) -> p c n", n=128)
                for hm in range(8):
                    src = pn[q][:, hm // 2, (hm % 2) * 128:(hm % 2 + 1) * 128]
                    P.op("pe", lambda e, hm=hm, src=src: e.transpose(pt[:, hm, :], src, ident[:]),
                         reads=[src, ident[:]], writes=[ps[:, b, :]], signal=(hm == 7))
                evac(pT[:, :, tl * 128:(tl + 1) * 128], pt)

            def group_tail(tg):
                for c in range(8):
                    b = bankA()
                    for mc in range(2):
                        mm(ps[:, b, :], vx[:, mc, c * 128:(c + 1) * 128], pT[:, (c // 2) * 2 + mc, :], mc == 0, mc == 1)
                    evac(oT[:, c, :], ps[:, b, :])
                for tl in range(4):
                    tt = tg * 4 + tl
                    for h in range(2):
                        b = bankA()
                        for c in range(8):
                            mm(ps[:, b, :], oT[:, c, tl * 128:(tl + 1) * 128], wo_sb[:, c, h * 512:(h + 1) * 512], c == 0, c == 7)
                        accum(tt, h, b, 1.0 / ALPHA)
                    ln_tail(tt)

            stage_scores(0)
            for t in range(NT + 1):
                if t < NT:
                    stage_softmax(t)
                if t + 1 < NT:
                    stage_scores(t + 1)
                if t >= 1:
                    stage_transpose(t - 1)
                    if (t - 1) % 4 == 3:
                        group_tail((t - 1) // 4)
            ln_flush()

        def poolmix(i):
            j = i // 3
            pmat = av(0, (8, 128), BF16)
            btab = av(2048, (D,), F32)
            stab = av(6144, (D,), F32)
            icnt = av(10240, (4, 128), F32)
            xbp = [av(12288 + q * 2048, (D,), BF16) for q in range(3)]
            poT = [av(18432 + q * 2048, (8, 128), BF16) for q in range(2)]
            tmp = av(22528, (D,), F32)
            tms = av(26624, (128,), F32)
            load_ln(i, 1)
            sl = next_wst()
            pw = wst[:, sl, :].rearrange("p (g c n) -> p g c n", g=4, c=2)
            P.dma("pool", f"wst{sl}", [(pw, W("pool_w")[j].rearrange("g (c p) n -> p g c n", p=128))], writes=[wst[:, sl, :]])
            P.dma("pool", "pmat", [(pmat, dr["poolmat"].rearrange("m p n -> p m n"))], writes=[pmat])
            P.dma("sp", "ptab", [(btab, W("pool_b")[j:j + 1].rearrange("o g d -> o (g d)").partition_broadcast(128)),
                                 (stab, W("pool_scale")[j:j + 1, :].partition_broadcast(128)),
                                 (av(10240, (512,), F32), dr["poolcnt"].rearrange("o g n -> o (g n)").partition_broadcast(128))],
                  writes=[btab, stab, icnt])
            P.op("dve", lambda e: e.tensor_scalar(stab, stab, 1.0 / ALPHA, None, ALU.mult), reads=[stab], writes=[stab])
            P.op("dve", lambda e: e.tensor_tensor(btab, btab, stab, ALU.mult), reads=[btab, stab], writes=[btab])
            wins = (2, 4, 8, 16)
            pending = []
            def win_stage(t):
                xc = xbp[t % 3]
                xp = xbp[(t - 1) % 3]
                P.op("act", lambda e: e.copy(xc, xres[:, t, :]), reads=[xres[:, t, :]], writes=[xc])
                po = poT[t % 2]
                for half in range(2):
                    b = bankA()
                    for cl in range(4):
                        c = half * 4 + cl
                        g = c // 2
                        out = ps[:, b, cl * 128:(cl + 1) * 128]
                        mm(out, xc[:, c * 128:(c + 1) * 128], pmat[:, 2 * g, :], True, t == 0)
                        if t > 0:
                            mm(out, xp[:, c * 128:(c + 1) * 128], pmat[:, 2 * g + 1, :], False, True)
                    for cl in range(4):
                        c = half * 4 + cl
                        g = c // 2
                        out = ps[:, b, cl * 128:(cl + 1) * 128]
                        xt = xT[:, c, t * 128:(t + 1) * 128]
                        if t == 0:
                            P.op("dve", lambda e, out=out, g=g: e.tensor_tensor(tms, out, icnt[:, g, :], ALU.mult),
                                 reads=[out, icnt[:, g, :]], writes=[tms])
                            P.op("dve", lambda e, c=c, xt=xt: e.tensor_tensor(po[:, c, :], tms, xt, ALU.subtract),
                                 reads=[tms, xt], writes=[po[:, c, :]])
                        else:
                            P.op("dve", lambda e, c=c, xt=xt, out=out, g=g: e.scalar_tensor_tensor(
                                po[:, c, :], out, 1.0 / wins[g], xt, ALU.mult, ALU.subtract),
                                reads=[out, xt], writes=[po[:, c, :]])

            def proj_stage(t):
                po = poT[t % 2]
                q = t % 2
                b2 = 4 + 2 * q
                yv2 = ps[:, b2:b2 + 2, :].rearrange("p b (g n) -> p (b g) n", n=256)
                for g in range(4):
                    for kc in range(2):
                        mm(yv2[:, g, :], po[:, 2 * g + kc, :], pw[:, g, kc, :], kc == 0, kc == 1)
                yflat = ps[:, b2:b2 + 2, :].rearrange("p b n -> p (b n)")
                P.op("dve", lambda e: e.tensor_tensor(tmp, yflat, stab, ALU.mult), reads=[yflat, stab], writes=[tmp])
                P.op("dve", lambda e: e.tensor_tensor(tmp, tmp, btab, ALU.add), reads=[tmp, btab], writes=[tmp])
                P.op("dve", lambda e: e.tensor_tensor(xres[:, t, :], xres[:, t, :], tmp, ALU.add),
                     reads=[xres[:, t, :], tmp], writes=[xres[:, t, :]])
                ln_tail(t)

            for t in range(NT + 1):
                if t < NT:
                    win_stage(t)
                if t >= 1:
                    proj_stage(t - 1)
            ln_flush()

        TWO_PI = 2.0 * math.pi
        C1 = 6.28125
        C2 = TWO_PI - C1

        def rope_tables(cosT, sinT, invf, sgn, tmp_i, tmp_a, tmp_b):
            P.dma("sp", "posin", [(tmp_i, dr["positions"].partition_broadcast(128))], writes=[tmp_i])
            ang = tmp_a
            P.op("dve", lambda e: e.tensor_copy(ang, tmp_i), reads=[tmp_i], writes=[ang])
            P.op("dve", lambda e: e.tensor_scalar(ang, ang, invf, None, ALU.mult), reads=[ang, invf], writes=[ang])
            kf = tmp_b
            ki = tmp_i
            P.op("dve", lambda e: e.tensor_scalar(kf, ang, 1.0 / TWO_PI, None, ALU.mult), reads=[ang], writes=[kf])
            P.op("dve", lambda e: e.tensor_copy(ki, kf), reads=[kf], writes=[ki])
            P.op("dve", lambda e: e.tensor_copy(kf, ki), reads=[ki], writes=[kf])
            P.op("dve", lambda e: e.scalar_tensor_tensor(ang, kf, -C1, ang, ALU.mult, ALU.add), reads=[kf, ang], writes=[ang])
            P.op("dve", lambda e: e.scalar_tensor_tensor(ang, kf, -C2, ang, ALU.mult, ALU.add), reads=[kf, ang], writes=[ang])

            def wrap(r):
                P.op("dve", lambda e: e.tensor_scalar(kf, r, math.pi, None, ALU.is_gt), reads=[r], writes=[kf])
                P.op("dve", lambda e: e.scalar_tensor_tensor(r, kf, -TWO_PI, r, ALU.mult, ALU.add), reads=[kf, r], writes=[r])
                P.op("dve", lambda e: e.tensor_scalar(kf, r, -math.pi, None, ALU.is_lt), reads=[r], writes=[kf])
                P.op("dve", lambda e: e.scalar_tensor_tensor(r, kf, TWO_PI, r, ALU.mult, ALU.add), reads=[kf, r], writes=[r])
            wrap(ang)
            P.op("act", lambda e: e.activation(sinT, ang, AF.Sin), reads=[ang], writes=[sinT])
            if sgn is not None:
                P.op("dve", lambda e: e.tensor_scalar(sinT, sinT, sgn, None, ALU.mult), reads=[sinT, sgn], writes=[sinT])
            P.op("dve", lambda e: e.tensor_scalar(ang, ang, 0.5 * math.pi, None, ALU.add), reads=[ang], writes=[ang])
            wrap(ang)
            P.op("act", lambda e: e.activation(cosT, ang, AF.Sin), reads=[ang], writes=[cosT])

        def diffattn(i):
            j = i // 3
            lam_init = 0.8 - 0.6 * math.exp(-0.3 * i)
            wqkv = W("da_w_qkv")[j]
            w_o = W("da_w_o")[j]
            onT = av(0, (4, S), BF16)
            qT = [av(16384 + q * 4096, (S,), BF16) for q in range(2)]
            kz = [[av(24576 + c * 4096, (S,), BF16) for c in range(2)],
                  [av(OFF_LNP + c * 4096, (S,), BF16) for c in range(2)]]
            vv = [av(32768 + q * 4160, (NT, 130), BF16) for q in range(2)]
            cosT = av(41088, (S,), F32)
            sinT = av(49280, (S,), F32)
            tA = av(57472, (512,), F32)
            tB = av(59520, (512,), F32)
            wo_sb = av(61568, (4, D), BF16)
            gtab = av(OFF_XB, (128,), F32)
            tt = av(OFF_XB + 512, (128,), F32)
            dds = [av(OFF_XB + 1024 + q * 512, (128,), F32) for q in range(2)]
            dn = [av(OFF_XB + 2048 + q * 256, (128,), BF16) for q in range(2)]
            junk = av(OFF_XB + 2560, (128,), BF16)
            accS = av(OFF_XN, (8, 130), F32)
            daw = av(OFF_WST, (5, 8, 128), BF16)
            eT = [av(OFF_WST + 10240 + q * 1024, (512,), BF16) for q in range(5)]
            rope_tables(cosT, sinT, dacst[:, 0:1], dacst[:, 1:2],
                        av(16384, (S,), I32), av(24576, (S,), F32), av(32768, (S,), F32))
            if DA_STOP:
                P.op("dve", lambda e: e.memset(onT, 0.0), writes=[onT])
            for q in range(2):
                for c in range(2):
                    P.op("dve", lambda e, q=q, c=c: e.memset(kz[q][c], 0.0), writes=[kz[q][c]])
            for q in range(2):
                P.op("dve", lambda e, q=q: e.memset(vv[q][:, :, 128:130], 1.0), writes=[vv[q]])
            lq = av(57472, (128,), F32)
            lk = av(57472 + 512, (128,), F32)
            P.dma("sp", "dalam", [(lq, W("da_lam_q")[j:j + 1].rearrange("o c d -> o (c d)").partition_broadcast(128)),
                                  (lk, W("da_lam_k")[j:j + 1].rearrange("o c d -> o (c d)").partition_broadcast(128)),
                                  (gtab, W("da_subln_g")[j:j + 1, :].partition_broadcast(128))],
                  writes=[lq, lk, gtab])
            P.op("dve", lambda e: e.tensor_tensor(lq, lq, lk, ALU.mult), reads=[lq, lk], writes=[lq])
            P.op("dve", lambda e: e.tensor_reduce(dsm[:, 0:2], lq.rearrange("p (c d) -> p c d", d=64), AX.X, ALU.add),
                 reads=[lq], writes=[dsm[:, 0:2]])
            P.op("act", lambda e: e.activation(dsm[:, 0:2], dsm[:, 0:2], AF.Exp), reads=[dsm[:, 0:2]], writes=[dsm[:, 0:2]])
            P.op("dve", lambda e: e.tensor_tensor(dsm[:, 2:3], dsm[:, 1:2], dsm[:, 0:1], ALU.subtract),
                 reads=[dsm[:, 0:2]], writes=[dsm[:, 2:3]])
            P.op("dve", lambda e: e.tensor_scalar(dsm[:, 2:3], dsm[:, 2:3], -lam_init, None, ALU.add),
                 reads=[dsm[:, 2:3]], writes=[dsm[:, 2:3]])
            P.op("dve", lambda e: e.tensor_scalar(gtab, gtab, 1.0 - lam_init, None, ALU.mult), reads=[gtab], writes=[gtab])
            nlam = dsm[:, 2:3]
            epsc = dsm[:, 5:6]
            P.op("dve", lambda e: e.memset(epsc, LN_EPS), writes=[epsc])

            def load_w(h):
                srcs = [wqkv[:, n * 1024 + h * 128:n * 1024 + (h + 1) * 128].rearrange("(c p) n -> p c n", p=128) for n in range(3)]
                P.dma("pool", "daw", [(daw[:, n], srcs[n]) for n in range(3)], writes=[daw[:, 0:3]])
                for n in range(2):
                    src = daw[:, n].rearrange("p c (g f d) -> p c g f d", g=2, f=2)
                    dst = daw[:, 3 + n].rearrange("p c (g f d) -> p c g f d", g=2, f=2)
                    for f in range(2):
                        sv = src[:, :, :, 1 - f, :].rearrange("p c g d -> p (c g) d")
                        dv = dst[:, :, :, f, :].rearrange("p c g d -> p (c g) d")
                        if f == 0:
                            P.op("act", lambda e, sv=sv, dv=dv: e.copy(dv, sv), reads=[daw[:, n]], writes=[daw[:, 3 + n]])
                        else:
                            P.op("dve", lambda e, sv=sv, dv=dv: e.tensor_copy(dv, sv), reads=[daw[:, n]], writes=[daw[:, 3 + n]])

            def project(h):
                pp = h % 2
                for n, dstT in ((0, qT[pp]), (1, None)):
                    for tg in range(4):
                        ba, bb = bankA(), bankA()
                        cols = slice(tg * 512, (tg + 1) * 512)
                        for kc in range(8):
                            mm(ps[:, ba, :], daw[:, n, kc, :], xT[:, kc, cols], kc == 0, kc == 7)
                        for kc in range(8):
                            mm(ps[:, bb, :], daw[:, 3 + n, kc, :], xT[:, kc, cols], kc == 0, kc == 7)
                        P.op("dve", lambda e, ba=ba, cols=cols: e.tensor_tensor(tA, ps[:, ba, :], cosT[:, cols], ALU.mult),
                             reads=[ps[:, ba, :], cosT[:, cols]], writes=[tA])
                        P.op("dve", lambda e, bb=bb, cols=cols: e.tensor_tensor(tB, ps[:, bb, :], sinT[:, cols], ALU.mult),
                             reads=[ps[:, bb, :], sinT[:, cols]], writes=[tB])
                        if dstT is not None:
                            P.op("dve", lambda e, dstT=dstT, cols=cols: e.tensor_tensor(dstT[:, cols], tA, tB, ALU.add),
                                 reads=[tA, tB], writes=[dstT[:, cols]])
                        else:
                            for c in range(2):
                                rows = slice(c * 64, (c + 1) * 64)
                                dk = kz[pp][c]
                                P.op("dve", lambda e, dk=dk, rows=rows, cols=cols: e.tensor_tensor(dk[rows, cols], tA[rows, :], tB[rows, :], ALU.add),
                                     reads=[tA, tB], writes=[dk[:, cols]])
                for t4 in range(4):
                    b = bankA()
                    for tl in range(4):
                        t = t4 * 4 + tl
                        for kc in range(8):
                            mm(ps[:, b, tl * 128:(tl + 1) * 128], xT[:, kc, t * 128:(t + 1) * 128], daw[:, 2, kc, :], kc == 0, kc == 7)
                    dst = vv[pp][:, t4 * 4:(t4 + 1) * 4, 0:128]
                    P.op("act", lambda e, dst=dst, b=b: e.copy(dst, ps[:, b, :].rearrange("p (a n) -> p a n", n=128)),
                         reads=[ps[:, b, :]], writes=[dst])

            ACC_ORDER = [(0, 0), (1, 0), (0, 1), (1, 1), (0, 2), (1, 2), (0, 3), (1, 3)]
            ACC_IDX = {cq: k for k, cq in enumerate(ACC_ORDER)}

            def acc_ap(c, qt):
                k = ACC_IDX[(c, qt)]
                return ps[:, 4 + k // 3, (k % 3) * 130:(k % 3) * 130 + 130]

            deferred = []

            def run_deferred(n=None):
                while deferred and (n is None or n > 0):
                    deferred.pop(0)()
                    if n is not None:
                        n -= 1

            def attend(h, G):
                pp = h % 2
                items = [(jj, c) for jj in range(4 * G + 4) for c in range(2)]
                LAG = 3
                info = {}

                def scores(idx):
                    jj, c = items[idx]
                    q0 = max(jj - 4 * G, 0) * 128
                    sb = bankA()
                    mm(ps[:, sb, q0:512], kz[pp][c][:, jj * 128:(jj + 1) * 128], qT[pp][:, G * 512 + q0:(G + 1) * 512], True, True)
                    et = eT[(idx + 5 * (h * 4 + G)) % 5]
                    P.op("act", lambda e: e.activation(et[:, q0:512], ps[:, sb, q0:512], AF.Exp, scale=0.125),
                         reads=[ps[:, sb, q0:512]], writes=[et[:, q0:512]])
                    if jj >= 4 * G:
                        P.op("dve", lambda e: e.tensor_tensor(et[:, q0:q0 + 128], et[:, q0:q0 + 128], tri[:], ALU.mult),
                             reads=[et[:, q0:q0 + 128], tri[:]], writes=[et[:, q0:q0 + 128]])
                    info[idx] = et

                seq = [(idx, qt) for idx, (jj, c) in enumerate(items) for qt in range(max(jj - 4 * G, 0), 4)]
                first, last = {}, {}
                for n_, (idx, qt) in enumerate(seq):
                    bk = 4 + ACC_IDX[(items[idx][1], qt)] // 3
                    first.setdefault(bk, n_)
                    last[bk] = n_
                pos = {iq: n_ for n_, iq in enumerate(seq)}

                def pv(idx):
                    jj, c = items[idx]
                    et = info.pop(idx)
                    if DA_STOP == 5:
                        return
                    for qt in range(max(jj - 4 * G, 0), 4):
                        bk = 4 + ACC_IDX[(c, qt)] // 3
                        n_ = pos[(idx, qt)]
                        mm(acc_ap(c, qt), et[:, qt * 128:(qt + 1) * 128], vv[pp][:, jj, 0:130], first[bk] == n_, last[bk] == n_)

                for idx in range(len(items)):
                    scores(idx)
                    if idx >= LAG:
                        pv(idx - LAG)
                    run_deferred(1)
                for idx in range(len(items) - LAG, len(items)):
                    pv(idx)
                run_deferred()
                if DA_STOP == 5:
                    return
                for k, (c, qt) in enumerate(ACC_ORDER):
                    src = acc_ap(c, qt)
                    dstv = accS[:, k, 0:130]
                    if (k // 3) % 2:
                        P.op("act", lambda e, src=src, dstv=dstv: e.copy(dstv, src), reads=[src], writes=[dstv])
                    else:
                        P.op("dve", lambda e, src=src, dstv=dstv: e.tensor_copy(dstv, src), reads=[src], writes=[dstv])
                dst = onT[:, h % 4, G * 512:(G + 1) * 512]

                def n1(qt):
                    k1, k2 = ACC_IDX[(0, qt)], ACC_IDX[(1, qt)]
                    o1, s1 = accS[:, k1, 0:128], accS[:, k1, 128:129]
                    o2, s2 = accS[:, k2, 0:128], accS[:, k2, 128:129]
                    rc1, rc2, ss, rs = (dsc[:, qt * 4 + n:qt * 4 + n + 1] for n in range(4))
                    dd = dds[qt % 2]
                    P.op("dve", lambda e: e.reciprocal(rc1, s1), reads=[s1], writes=[rc1])
                    P.op("dve", lambda e: e.reciprocal(rc2, s2), reads=[s2], writes=[rc2])
                    P.op("dve", lambda e: e.tensor_tensor(rc2, rc2, nlam, ALU.mult), reads=[rc2, nlam], writes=[rc2])
                    P.op("dve", lambda e: e.tensor_scalar(tt, o2, rc2, None, ALU.mult), reads=[o2, rc2], writes=[tt])
                    P.op("dve", lambda e: e.scalar_tensor_tensor(dd, o1, rc1, tt, ALU.mult, ALU.add),
                         reads=[o1, rc1, tt], writes=[dd])
                    P.op("act", lambda e: e.activation(junk, dd, AF.Square, accum_out=ss), reads=[dd], writes=[junk, ss])

                def n2(qt):
                    ss, rs = dsc[:, qt * 4 + 2:qt * 4 + 3], dsc[:, qt * 4 + 3:qt * 4 + 4]
                    P.op("dve", lambda e: e.tensor_scalar(rs, ss, 1.0 / 128.0, LN_EPS, ALU.mult, ALU.add), reads=[ss], writes=[rs])
                    P.op("pool", lambda e: e.tensor_tensor(rs, rs, mhalf[:, 0:1], ALU.pow), reads=[rs, mhalf[:, 0:1]], writes=[rs])

                def n3a(qt):
                    rs = dsc[:, qt * 4 + 3:qt * 4 + 4]
                    dd, dq = dds[qt % 2], dn[qt % 2]
                    P.op("dve", lambda e: e.scalar_tensor_tensor(dq, dd, rs, gtab, ALU.mult, ALU.mult),
                         reads=[dd, rs, gtab], writes=[dq])

                def n3b(qt):
                    dq = dn[qt % 2]
                    ptT = ps[:, 7, :].bitcast(BF16).rearrange("p (c n) -> p c n", n=128)
                    P.op("pe", lambda e: e.transpose(ptT[:, qt, :], dq, ident[:]),
                         reads=[dq, ident[:]], writes=[ps[:, 7, :]], signal=True)

                def n4():
                    ptT = ps[:, 7, :].bitcast(BF16).rearrange("p (c n) -> p c n", n=128)
                    P.op("act", lambda e: e.copy(dst.rearrange("p (c n) -> p c n", n=128), ptT[:, 0:4, :]),
                         reads=[ps[:, 7, :]], writes=[dst])

                order = [(n1, 0), (n1, 1), (n2, 0), (n2, 1), (n3a, 0), (n3a, 1), (n1, 2), (n1, 3), (n3b, 0), (n3b, 1),
                         (n2, 2), (n2, 3), (n3a, 2), (n3a, 3), None, None, (n3b, 2), (n3b, 3)]
                for it_ in order:
                    if it_ is None:
                        deferred.append(lambda: None)
                    else:
                        deferred.append(lambda fn=it_[0], qt=it_[1]: fn(qt))
                deferred.append(n4)

            def wo_pass(pi, final):
                run_deferred()
                wv_ = w_o[pi * 512:(pi + 1) * 512, :].rearrange("(c p) n -> p c n", p=128)
                P.dma("pool", "dawo", [(wo_sb[:, 0:2, :], wv_[:, 0:2, :]), (wo_sb[:, 2:4, :], wv_[:, 2:4, :])], writes=[wo_sb])
                pending = []
                for t in range(NT):
                    for hf in range(2):
                        b = bankA()
                        for hh in range(4):
                            mm(ps[:, b, :], onT[:, hh, t * 128:(t + 1) * 128], wo_sb[:, hh, hf * 512:(hf + 1) * 512], hh == 0, hh == 3)
                        accum(t, hf, b, 1.0 / ALPHA)
                    if final:
                        ln_tail(t)
                ln_flush()

            load_w(0)
            project(0)
            for h in range(8):
                if h + 1 < 8:
                    load_w(h + 1)
                for G in range(4):
                    attend(h, G)
                    if G == 1 and h + 1 < 8:
                        project(h + 1)
                if h == 3:
                    wo_pass(0, False)
            run_deferred()
            load_ln(i, 1)
            call_pre_hook()
            wo_pass(1, True)

        def retention(i):
            j = i // 3
            wall = W("ret_w_qkvg")[j]
            w_o = W("ret_w_o")[j]
            cosR = av(0, (S,), F32)
            sinR = av(8192, (S,), F32)
            qT = av(16384, (2, S), BF16)
            kT = av(24576, (2, S), BF16)
            ktok = av(32768, (NT, 256), BF16)
            vv = av(40960, (NT, 512), BF16)
            Rf = av(57344, (2, 512), F32)
            Rb = av(61440, (2, 512), BF16)
            wo_h = av(63488, (4, D), BF16)
            T1, T2, T3, T4 = (av(OFF_XN + q * 2048, (512,), F32) for q in range(4))
            T4s = [T2, T4]
            T2s = [av(OFF_LNP + q * 2048, (512,), F32) for q in range(3)]
            attb = attb_sb
            ybf = [av(OFF_XB + q * 1024, (512,), BF16) for q in range(2)]
            yTn = [av(OFF_XB + 2048 + q * 1024, (4, 128), BF16) for q in range(2)]
            rope_tables(cosR, sinR, retcst[:, 0:1], None,
                        av(16384, (S,), I32), av(24576, (S,), F32), av(32768, (S,), F32))
            loaders = []
            for h in range(4):
                loaders.append(col_loader(wall[:, h * 256:(h + 1) * 256]))
                loaders.append(col_loader(wall[:, 1024 + h * 256:1024 + (h + 1) * 256]))
                for cb in range(2):
                    loaders.append(col_loader(wall[:, 2048 + h * 512 + cb * 256:2048 + h * 512 + (cb + 1) * 256]))
                for cb in range(2):
                    loaders.append(col_loader(wall[:, 4096 + h * 512 + cb * 256:4096 + h * 512 + (cb + 1) * 256]))
            get = make_stream(loaders)
            sv = lambda sl: wst[:, sl, :].rearrange("p (c n) -> p c n", c=8)
            for h in range(4):
                gam = 1.0 - 2.0 ** (-5.0 - h)
                gC = gam ** 128
                qdec = retcst[:, 1 + h:2 + h]
                kdec = retcst[:, 5 + h:6 + h]
                wv_ = w_o[h * 512:(h + 1) * 512, :].rearrange("(c p) n -> p c n", p=128)
                P.dma("pool", "retwo", [(wo_h[:, 0:2, :], wv_[:, 0:2, :]), (wo_h[:, 2:4, :], wv_[:, 2:4, :])], writes=[wo_h])
                for n, dstT in ((0, qT), (1, kT)):
                    wv = sv(get(h * 6 + n))
                    for tg in range(4):
                        ba, bb = bankA(), bankA()
                        cols = slice(tg * 512, (tg + 1) * 512)
                        for kc in range(8):
                            mm(ps[:, ba, :], wv[:, kc, 0:128], xT[:, kc, cols], kc == 0, kc == 7)
                        for kc in range(8):
                            mm(ps[:, bb, :], wv[:, kc, 128:256], xT[:, kc, cols], kc == 0, kc == 7)
                        A_, B_ = ps[:, ba, :], ps[:, bb, :]
                        c_, s_ = cosR[:, cols], sinR[:, cols]
                        P.op("dve", lambda e, A_=A_, c_=c_: e.tensor_tensor(T1, A_, c_, ALU.mult), reads=[A_, c_], writes=[T1])
                        P.op("dve", lambda e, B_=B_, s_=s_: e.tensor_tensor(T2, B_, s_, ALU.mult), reads=[B_, s_], writes=[T2])
                        P.op("dve", lambda e, dstT=dstT, cols=cols: e.tensor_tensor(dstT[:, 0, cols], T1, T2, ALU.subtract),
                             reads=[T1, T2], writes=[dstT[:, 0, cols]])
                        P.op("dve", lambda e, B_=B_, c_=c_: e.tensor_tensor(T3, B_, c_, ALU.mult), reads=[B_, c_], writes=[T3])
                        P.op("dve", lambda e, A_=A_, s_=s_: e.tensor_tensor(T4, A_, s_, ALU.mult), reads=[A_, s_], writes=[T4])
                        P.op("dve", lambda e, dstT=dstT, cols=cols: e.tensor_tensor(dstT[:, 1, cols], T3, T4, ALU.add),
                             reads=[T3, T4], writes=[dstT[:, 1, cols]])
                for n4 in range(4):
                    b = bankA()
                    pt = ps[:, b, :].bitcast(BF16).rearrange("p (c n) -> p c n", n=128)
                    for nl in range(4):
                        n = n4 * 4 + nl
                        for ab in range(2):
                            src = kT[:, ab, n * 128:(n + 1) * 128]
                            P.op("pe", lambda e, src=src, pt=pt, idx=nl * 2 + ab: e.transpose(pt[:, idx, :], src, ident[:]),
                                 reads=[src, ident[:]], writes=[ps[:, b, :]], signal=(nl == 3 and ab == 1))
                    dst = ktok[:, n4 * 4:(n4 + 1) * 4, :].rearrange("p a (b n) -> p (a b) n", n=128)
                    P.op("dve", lambda e, dst=dst, pt=pt, kdec=kdec: e.tensor_scalar(dst, pt, kdec, None, ALU.mult),
                         reads=[ps[:, b, :], kdec], writes=[dst])
                v0 = sv(get(h * 6 + 2))
                v1 = sv(get.peek(h * 6 + 3))
                for n in range(NT):
                    b = bankA()
                    for cb, wv in enumerate((v0, v1)):
                        for kc in range(8):
                            mm(ps[:, b, cb * 256:(cb + 1) * 256], xT[:, kc, n * 128:(n + 1) * 128], wv[:, kc, :], kc == 0, kc == 7)
                    if n % 2:
                        P.op("act", lambda e, n=n, b=b: e.copy(vv[:, n, :], ps[:, b, :]), reads=[ps[:, b, :]], writes=[vv[:, n, :]])
                    else:
                        P.op("dve", lambda e, n=n, b=b: e.tensor_copy(vv[:, n, :], ps[:, b, :]), reads=[ps[:, b, :]], writes=[vv[:, n, :]])
                g0 = sv(get(h * 6 + 4))
                g1 = sv(get.peek(h * 6 + 5))
                DT = retdt[:, h, :]

                def sA(n, DT=DT):
                    tok = slice(n * 128, (n + 1) * 128)
                    ab_ = attb[:, n % 2, :]
                    b = bankA()
                    for ab in range(2):
                        mm(ps[:, b, 0:128], kT[:, ab, tok], qT[:, ab, tok], ab == 0, ab == 1)
                    P.op("dve", lambda e: e.tensor_tensor(ab_, ps[:, b, 0:128], DT, ALU.mult),
                         reads=[ps[:, b, 0:128], DT], writes=[ab_])

                def sB(n, qdec=qdec, gC=gC):
                    tok = slice(n * 128, (n + 1) * 128)
                    ab_ = attb[:, n % 2, :]
                    o_ = T2s[n % 3]
                    bx = bankB()
                    mm(ps[:, bx, :], ab_, vv[:, n, :], True, True)
                    if n > 0:
                        by = bankB()
                        for ab in range(2):
                            mm(ps[:, by, :], qT[:, ab, tok], Rb[:, ab, :], ab == 0, ab == 1)
                        P.op("act", lambda e: e.copy(T1, ps[:, bx, :]), reads=[ps[:, bx, :]], writes=[T1])
                        P.op("dve", lambda e: e.scalar_tensor_tensor(o_, ps[:, by, :], qdec, T1, ALU.mult, ALU.add),
                             reads=[ps[:, by, :], qdec, T1], writes=[o_])
                    else:
                        P.op("act", lambda e: e.copy(o_, ps[:, bx, :]), reads=[ps[:, bx, :]], writes=[o_])
                    if n < NT - 1:
                        for ab in range(2):
                            bz = bankB()
                            mm(ps[:, bz, :], ktok[:, n, ab * 128:(ab + 1) * 128], vv[:, n, :], True, True)
                            if n == 0:
                                P.op("dve", lambda e, ab=ab, bz=bz: e.tensor_copy(Rf[:, ab, :], ps[:, bz, :]),
                                     reads=[ps[:, bz, :]], writes=[Rf[:, ab, :]])
                            else:
                                P.op("dve", lambda e, ab=ab, bz=bz: e.scalar_tensor_tensor(Rf[:, ab, :], Rf[:, ab, :], gC, ps[:, bz, :], ALU.mult, ALU.add),
                                     reads=[Rf[:, ab, :], ps[:, bz, :]], writes=[Rf[:, ab, :]])
                        P.op("act", lambda e: e.copy(Rb, Rf), reads=[Rf], writes=[Rb])

                def sC(n, st_):
                    o_ = T2s[n % 3]
                    q = cnt["sm"] % 8
                    cnt["sm"] += 1
                    stf = small[:, q * 16:q * 16 + 6]
                    mv = small[:, q * 16 + 12:q * 16 + 14]
                    rs = small[:, q * 16 + 14:q * 16 + 15]
                    nmr = small[:, q * 16 + 15:q * 16 + 16]
                    st_[n] = (mv, rs, nmr)
                    P.op("dve", lambda e: e.bn_stats(stf, o_), reads=[o_], writes=[stf])
                    P.op("dve", lambda e: e.bn_aggr(mv, stf), reads=[stf], writes=[mv])
                    P.op("dve", lambda e: e.tensor_scalar(rs, mv[:, 1:2], LN_EPS, None, ALU.add), reads=[mv], writes=[rs])
                    P.op("pool", lambda e: e.tensor_tensor(rs, rs, mhalf[:, 0:1], ALU.pow), reads=[rs, mhalf[:, 0:1]], writes=[rs])

                def sD(n, st_, g0=g0, g1=g1):
                    tok = slice(n * 128, (n + 1) * 128)
                    o_ = T2s[n % 3]
                    sg = T4s[n % 2]
                    yb = ybf[n % 2]
                    mv, rs, nmr = st_.pop(n)
                    bg = bankA()
                    for cb, wv in enumerate((g0, g1)):
                        for kc in range(8):
                            mm(ps[:, bg, cb * 256:(cb + 1) * 256], xT[:, kc, tok], wv[:, kc, :], kc == 0, kc == 7)
                    P.op("act", lambda e: e.activation(sg, ps[:, bg, :], AF.Silu), reads=[ps[:, bg, :]], writes=[sg])
                    P.op("dve", lambda e: e.tensor_scalar(nmr, mv[:, 0:1], rs, -1.0, ALU.mult, ALU.mult),
                         reads=[mv, rs], writes=[nmr])
                    P.op("act", lambda e: e.activation(T3, o_, AF.Identity, bias=nmr, scale=rs),
                         reads=[o_, nmr, rs], writes=[T3])
                    P.op("dve", lambda e: e.tensor_tensor(yb, T3, sg, ALU.mult), reads=[T3, sg], writes=[yb])

                def sE(n):
                    yb = ybf[n % 2]
                    b = bankA()
                    pt = ps[:, b, :].bitcast(BF16).rearrange("p (c n) -> p c n", n=128)
                    for ec in range(4):
                        src = yb[:, ec * 128:(ec + 1) * 128]
                        P.op("pe", lambda e, ec=ec, src=src: e.transpose(pt[:, ec, :], src, ident[:]),
                             reads=[src, ident[:]], writes=[ps[:, b, :]], signal=(ec == 3))
                    yt = yTn[n % 2]
                    P.op("act", lambda e: e.copy(yt, pt[:, 0:4, :]), reads=[ps[:, b, :]], writes=[yt])

                def sF(n):
                    yt = yTn[n % 2]
                    for hf in range(2):
                        b = bankA()
                        for ec in range(4):
                            mm(ps[:, b, :], yt[:, ec, :], wo_h[:, ec, hf * 512:(hf + 1) * 512], ec == 0, ec == 3)
                        accum(n, hf, b, 1.0 / ALPHA)

                st_ = {}
                for it in range(NT + 5):
                    if it < NT:
                        sA(it)
                    if 0 <= it - 1 < NT:
                        sB(it - 1)
                    if 0 <= it - 2 < NT:
                        sC(it - 2, st_)
                    if 0 <= it - 3 < NT:
                        sD(it - 3, st_)
                    if 0 <= it - 4 < NT:
                        sE(it - 4)
                    if 0 <= it - 5 < NT:
                        sF(it - 5)
            load_ln(i, 1)
            call_pre_hook()
            if not DEBUG_NO_LN:
                for t in range(NT):
                    ln_tail(t)
                ln_flush()

        xv = dr["x"].rearrange("(t p) d -> p t d", p=128)
        for t in range(NT):
            P.dma("sp", f"xin{t}", [(xres[:, t, :], xv[:, t, :])], writes=[xres[:, t, :]])
        for t in range(NT):
            ln_tail(t, cast_only=True)
        ln_flush()
        xa_gen = {"g": None}
        for n_item, item in enumerate(plan):
            final_store["on"] = (n_item == len(plan) - 1) and item[0] in ("ffn", "xa")
            nxt = plan[n_item + 1] if n_item + 1 < len(plan) else None
            if item[0] == "mix" and nxt is not None and nxt[0] == "xa":
                xa_gen["g"] = xa(nxt[1])
                pre_hook["fn"] = lambda: next(xa_gen["g"])
            if item[0] == "ffn":
                ffn(item[1], item[2])
            elif item[0] == "xa":
                if xa_gen["g"] is None:
                    xa_gen["g"] = xa(item[1])
                    next(xa_gen["g"])
                for _ in xa_gen["g"]:
                    pass
                xa_gen["g"] = None
            elif item[0] == "mix" and item[1] % 3 == 1:
                poolmix(item[1])
            elif item[0] == "mix" and item[1] % 3 == 0:
                diffattn(item[1])
            elif item[0] == "mix" and item[1] % 3 == 2:
                retention(item[1])
            else:
                raise NotImplementedError(item)
            call_pre_hook()
        outs = list(final_store["idents"])
        if not outs:
            for t4 in range(4):
                outs.append(P.dma("sp", f"yout{t4}", [(yv[:, t4 * 4:(t4 + 1) * 4, :], xres[:, t4 * 4:(t4 + 1) * 4, :])],
                                  reads=[xres[:, t4 * 4:(t4 + 1) * 4, :]]))
        P.wait_all("sp", outs)
        P.emit()
    return nc, P, sorted(k for k in dr)


def const_inputs():
    c = {"ident": np.eye(128, dtype=np.float32)}
    pm = np.zeros((8, 128, 128), np.float32)
    pc = np.zeros((1, 4, 128), np.float32)
    sidx = np.arange(128)[:, None]
    tidx = np.arange(128)[None, :]
    for g, w in enumerate((2, 4, 8, 16)):
        pm[2 * g] = ((tidx - sidx >= 0) & (tidx - sidx < w)).astype(np.float32)
        pm[2 * g + 1] = ((tidx + 128 - sidx) < w).astype(np.float32)
        pc[0, g] = 1.0 / np.minimum(np.arange(128) + 1, w)
    kk = np.arange(128)
    c["trimask"] = (kk[:, None] <= kk[None, :]).astype(np.float32)
    inv = (1.0 / (np.float32(10000.0) ** (np.arange(0, 64, 2, dtype=np.float32) / np.float32(64)))).astype(np.float32)
    dac = np.zeros((128, 2), np.float32)
    dac[:, 0] = inv[kk % 32]
    dac[:, 1] = np.where((kk % 64) < 32, -1.0, 1.0)
    c["dacst"] = dac
    rc = np.zeros((128, 16), np.float32)
    rc[:, 0] = (1.0 / (np.float32(10000.0) ** np.linspace(0.0, 1.0, 128, dtype=np.float32))).astype(np.float32)
    rdt = np.zeros((4, 128, 128), np.float32)
    for h in range(4):
        gam = 1.0 - 2.0 ** (-5.0 - h)
        rc[:, 1 + h] = gam ** (kk + 1.0)
        rc[:, 5 + h] = gam ** (127.0 - kk) / 16.0
        dif = kk[None, :] - kk[:, None]
        rdt[h] = np.where(dif >= 0, gam ** np.maximum(dif, 0), 0.0) / 16.0
    c["retcst"] = rc
    c["retdt"] = rdt
    c["poolmat"] = pm
    c["poolcnt"] = pc
    return c


def layer_plan(i):
    return [("ffn", i, 0), ("mix", i), ("xa", i), ("ffn", i, 1)]


FUSED = True
_CACHE = {}


def _program(plan):
    key = tuple(plan)
    if key not in _CACHE:
        _CACHE[key] = build_program(list(plan))
    return _CACHE[key]


def kernel(**inputs):
    n = 8
    consts = const_inputs()
    x = np.asarray(inputs["x"], dtype=np.float32)
    cur = [np.ascontiguousarray(x[b]) for b in range(n)]
    if FUSED:
        plans = [sum((layer_plan(i) for i in range(DEPTH)), [])]
    else:
        plans = [layer_plan(i) for i in range(DEPTH)]
    for plan in plans:
        nc, _, names = _program(plan)
        in_maps = []
        for b in range(n):
            m = {"x": cur[b],
                 "mem": np.ascontiguousarray(inputs["mem"][b], dtype=np.float32),
                 "positions": np.ascontiguousarray(inputs["positions"][b:b + 1], dtype=np.int32)}
            m.update(consts)
            for name in names:
                if name not in m:
                    m[name] = np.ascontiguousarray(inputs[name], dtype=np.float32)
            in_maps.append({k: m[k] for k in names})
        res = run_bass_kernel_spmd(nc, in_maps, core_ids=list(range(n)))
        cur = [np.asarray(res.results[b]["y"], dtype=np.float32) for b in range(n)]
    return np.stack(cur, axis=0)
```

```python
import math
from contextlib import ExitStack

import numpy as np
import concourse.bass as bass
import concourse.mybir as mybir
from concourse.bass_utils import run_bass_kernel_spmd

F32 = mybir.dt.float32
BF16 = mybir.dt.bfloat16
I32 = mybir.dt.int32
AF = mybir.ActivationFunctionType
ALU = mybir.AluOpType
AX = mybir.AxisListType
DSZ = {F32: 4, BF16: 2, I32: 4}
ENGS = ("pe", "act", "dve", "pool", "sp")

S = 2048
D = 1024
NT = 16
DEPTH = 4
FF = 2816
NJ = 22
MEM = 256
ALPHA = (2 * DEPTH) ** 0.25
LN_EPS = 1e-5
EPS_S = LN_EPS / (ALPHA * ALPHA)


def ap_interval(ap):
    sz = DSZ[ap.dtype]
    pat = ap.ap
    pstep = pat[0][0]
    off = int(ap.offset)
    lo = off % pstep if pstep > 0 else off
    span = 0
    for st, cnt in pat[1:]:
        span += abs(st) * (cnt - 1)
    lo_b, hi_b = lo * sz, (lo + span + 1) * sz
    if ap.tensor.name == "ps":
        lo_b = (lo_b // 2048) * 2048
        hi_b = ((hi_b + 2047) // 2048) * 2048
    return ap.tensor.name, lo_b, hi_b


class Prog:
    def __init__(self, nc):
        self.nc = nc
        self.ops = {e: [] for e in ENGS}
        self.cnt = {e: 0 for e in ENGS}
        self.seen = {e: {} for e in ENGS}
        self.rec = {}
        self.dma_cnt = {}
        self.dma_sem_names = []

    def _overl(self, key):
        name, lo, hi = key
        return [r for r in self.rec.get(name, ()) if r[0] < hi and lo < r[1]]

    def _deps(self, eng, reads, writes):
        deps = {}

        def add(k, v):
            if k == "pe" and eng == "pe":
                return
            if deps.get(k, 0) < v:
                deps[k] = v

        for key in reads:
            for r in self._overl(key):
                if r[2] is not None:
                    add(*r[2])
                if key[0] == "ps":
                    for k, v in r[3].items():
                        if k != eng:
                            add(k, v)
        for key in writes:
            for r in self._overl(key):
                if r[2] is not None:
                    add(*r[2])
                for k, v in r[3].items():
                    add(k, v)
        out = []
        seen = self.seen[eng]
        for k, v in deps.items():
            if seen.get(k, 0) < v:
                seen[k] = v
                out.append((k, v))
        return out

    def _update(self, ident, reads, writes):
        for name, lo, hi in reads:
            lst = self.rec.setdefault(name, [])
            covered = False
            for r in lst:
                if r[0] < hi and lo < r[1]:
                    if r[3].get(ident[0], 0) < ident[1]:
                        r[3][ident[0]] = ident[1]
                    if r[0] <= lo and hi <= r[1]:
                        covered = True
            if not covered:
                lst.append([lo, hi, None, {ident[0]: ident[1]}])
        for name, lo, hi in writes:
            lst = self.rec.setdefault(name, [])
            lst[:] = [r for r in lst if not (lo <= r[0] and r[1] <= hi)]
            lst.append([lo, hi, ident, {}])

    def op(self, eng, fn, reads=(), writes=(), signal=True):
        rk = [ap_interval(a) for a in reads]
        wk = [ap_interval(a) for a in writes]
        waits = self._deps(eng, rk, wk)
        if signal:
            self.cnt[eng] += 1
            ident = (eng, self.cnt[eng])
        else:
            ident = (eng, self.cnt[eng] + 1)
        self._update(ident, rk, wk)
        self.ops[eng].append((waits, fn, eng if signal else None, 1))

    def dma(self, eng, slot, pairs, reads=(), writes=()):
        rk = [ap_interval(a) for a in reads]
        wk = [ap_interval(a) for a in writes]
        key = "dma:" + slot
        waits = self._deps(eng, rk, wk)
        if key not in self.dma_cnt:
            self.dma_cnt[key] = 0
            self.dma_sem_names.append(key)
        self.dma_cnt[key] += 16 * len(pairs)
        ident = (key, self.dma_cnt[key])
        self._update(ident, rk, wk)
        for n, (o, i) in enumerate(pairs):
            self.ops[eng].append((waits if n == 0 else [],
                                  lambda e, o=o, i=i: e.dma_start(out=o, in_=i), key, 16))
        return ident

    def wait_all(self, eng, idents):
        waits = []
        for k, v in idents:
            if self.seen[eng].get(k, 0) < v:
                self.seen[eng][k] = v
                waits.append((k, v))
        self.ops[eng].append((waits, None, None, 0))

    def check(self):
        pos = {e: 0 for e in ENGS}
        val = {}
        prog = True
        while prog:
            prog = False
            for e in ENGS:
                lst = self.ops[e]
                while pos[e] < len(lst):
                    waits, fn, sk, inc = lst[pos[e]]
                    if all(val.get(k, 0) >= v for k, v in waits):
                        if sk is not None:
                            val[sk] = val.get(sk, 0) + inc
                        pos[e] += 1
                        prog = True
                    else:
                        break
        for e in ENGS:
            if pos[e] != len(self.ops[e]):
                waits = self.ops[e][pos[e]][0]
                raise RuntimeError(f"deadlock: engine {e} stuck at op {pos[e]}/{len(self.ops[e])} waits={waits}")

    def emit(self):
        nc = self.nc
        self.check()
        with ExitStack() as es:
            sems = {}
            for e in ("pe", "act", "dve", "pool"):
                sems[e] = es.enter_context(nc.semaphore("s_" + e))
            for k in self.dma_sem_names:
                sems[k] = es.enter_context(nc.semaphore("s_" + k.replace(":", "_")))
            block = es.enter_context(nc.Block())

            def run(e, lst):
                for waits, fn, sk, inc in lst:
                    for k, v in waits:
                        e.wait_ge(sems[k], v)
                    if fn is None:
                        continue
                    ins = fn(e)
                    if sk is not None:
                        ins.then_inc(sems[sk], inc)

            @block.tensor
            def _(e):
                run(e, self.ops["pe"])

            @block.scalar
            def _(e):
                run(e, self.ops["act"])

            @block.vector
            def _(e):
                run(e, self.ops["dve"])

            @block.gpsimd
            def _(e):
                run(e, self.ops["pool"])

            @block.sync
            def _(e):
                run(e, self.ops["sp"])


WEIGHT_SPECS = [
    ("ffn_w_in", [4, 2, 1024, 5632]), ("ffn_w_out", [4, 2, 2816, 1024]),
    ("ln_g", [4, 4, 1024]), ("ln_b", [4, 4, 1024]),
    ("da_w_qkv", [2, 1024, 3072]), ("da_w_o", [2, 1024, 1024]),
    ("da_lam_q", [2, 2, 64]), ("da_lam_k", [2, 2, 64]), ("da_subln_g", [2, 128]),
    ("pool_w", [1, 4, 256, 256]), ("pool_b", [1, 4, 256]), ("pool_scale", [1, 1024]),
    ("ret_w_qkvg", [1, 1024, 6144]), ("ret_w_o", [1, 2048, 1024]),
    ("xa_wq", [4, 1024, 1024]), ("xa_wkv", [4, 1024, 2048]), ("xa_wo", [4, 1024, 1024]),
]

DEBUG_NO_LN = False
DA_STOP = 0
ARENA_BYTES = 109056
OFF_WST = 72192
OFF_LNP = 88576
OFF_XB = 96768
OFF_XN = 100864


def build_program(plan):
    nc = bass.Bass("TRN2", target_bir_lowering=False)
    dr = {}
    dr["x"] = nc.dram_tensor("x", [S, D], F32, kind="ExternalInput").ap()
    dr["mem"] = nc.dram_tensor("mem", [MEM, D], F32, kind="ExternalInput").ap()
    dr["positions"] = nc.dram_tensor("positions", [1, S], I32, kind="ExternalInput").ap()
    dr["ident"] = nc.dram_tensor("ident", [128, 128], F32, kind="ExternalInput").ap()
    dr["trimask"] = nc.dram_tensor("trimask", [128, 128], F32, kind="ExternalInput").ap()
    dr["dacst"] = nc.dram_tensor("dacst", [128, 2], F32, kind="ExternalInput").ap()
    dr["retcst"] = nc.dram_tensor("retcst", [128, 16], F32, kind="ExternalInput").ap()
    dr["retdt"] = nc.dram_tensor("retdt", [4, 128, 128], F32, kind="ExternalInput").ap()
    dr["poolmat"] = nc.dram_tensor("poolmat", [8, 128, 128], F32, kind="ExternalInput").ap()
    dr["poolcnt"] = nc.dram_tensor("poolcnt", [1, 4, 128], F32, kind="ExternalInput").ap()
    wshape = dict(WEIGHT_SPECS)

    def W(name):
        if name not in dr:
            dr[name] = nc.dram_tensor(name, wshape[name], F32, kind="ExternalInput").ap()
        return dr[name]
    y = nc.dram_tensor("y", [S, D], F32, kind="ExternalOutput").ap()
    P = Prog(nc)

    with ExitStack() as es:
        def T(name, shape, dt):
            return es.enter_context(nc.sbuf_tensor(name, shape, dt))

        xres = T("xres", [128, NT, D], F32)
        xT = T("xT", [128, 8, S], BF16)
        ident = T("ident_sb", [128, 128], BF16)
        small = T("small", [128, 128], F32)
        xst = T("xst", [128, 2, 16], F32)
        dsm = T("dsm", [128, 8], F32)
        dsc = T("dsc", [128, 16], F32)
        dacst = T("dacst_sb", [128, 2], F32)
        retcst = T("retcst_sb", [128, 16], F32)
        retdt = T("retdt_sb", [128, 4, 128], F32)
        attb_sb = T("attb_sb", [128, 2, 128], BF16)
        tri = T("tri_sb", [128, 128], BF16)
        ones = T("ones", [128, 128], BF16)
        mhalf = T("mhalf", [128, 4], F32)
        arena = T("arena", [128, ARENA_BYTES // 2], BF16)
        ps = es.enter_context(nc.psum_tensor("ps", [128, 8, 512], F32))

        def av(off, shape, dt):
            n = int(np.prod(shape)) * DSZ[dt]
            a = arena[:, off // 2:(off + n) // 2]
            if dt != BF16:
                a = a.bitcast(dt)
            if len(shape) == 2:
                a = a.rearrange("p (a b) -> p a b", b=shape[1])
            elif len(shape) == 3:
                a = a.rearrange("p (a b c) -> p a b c", b=shape[1], c=shape[2])
            return a

        wst = av(OFF_WST, (4, 2048), BF16)
        lnp = av(OFF_LNP, (2, D), F32)
        wres = av(49152, (11, D), BF16)
        st = {"wst": 0, "pa": 0, "pb": 0}

        def next_wst():
            s = st["wst"]
            st["wst"] = (s + 1) % 4
            return s

        def bankA():
            b = st["pa"]
            st["pa"] = (b + 1) % 4
            return b

        def bankB():
            b = st["pb"]
            st["pb"] = (b + 1) % 4
            return 4 + b

        def mm(out, lhsT, rhs, start, stop):
            P.op("pe", lambda e: e.matmul(out, lhsT, rhs, start=start, stop=stop),
                 reads=[lhsT, rhs], writes=[out], signal=stop)

        P.dma("pool", "ident", [(ident[:], dr["ident"])], writes=[ident[:]])
        P.op("dve", lambda e: e.memset(mhalf[:], -0.5), writes=[mhalf[:]])
        P.op("dve", lambda e: e.memset(ones[:], 1.0), writes=[ones[:]])
        P.dma("pool", "tri", [(tri[:], dr["trimask"])], writes=[tri[:]])
        P.dma("sp", "dacst", [(dacst[:], dr["dacst"])], writes=[dacst[:]])
        P.dma("sp", "retcst", [(retcst[:], dr["retcst"]), (retdt[:], dr["retdt"].rearrange("h p n -> p h n"))],
              writes=[retcst[:], retdt[:]])

        xb_bufs = [av(OFF_XB + i * 2048, (D,), BF16) for i in range(2)]
        xn_bufs = [av(OFF_XN + i * 4096, (D,), F32) for i in range(2)]
        cnt = {"xb": 0, "xn": 0, "sm": 0}

        def load_ln(i, k):
            P.dma("sp", "lnp", [(lnp[:, 0, :], W("ln_g")[i, k:k + 1, :].partition_broadcast(128)),
                                (lnp[:, 1, :], W("ln_b")[i, k:k + 1, :].partition_broadcast(128))],
                  writes=[lnp[:]])

        ln_q = []
        final_store = {"on": False, "idents": []}
        yv = y.rearrange("(t p) d -> p t d", p=128)

        def _s1(en):
            t = en["t"]
            q = cnt["sm"] % 8
            cnt["sm"] += 1
            stf = small[:, q * 16:q * 16 + 12]
            stt = stf.rearrange("p (a b) -> p a b", b=6)
            mv = small[:, q * 16 + 12:q * 16 + 14]
            rs = small[:, q * 16 + 14:q * 16 + 15]
            nmr = small[:, q * 16 + 15:q * 16 + 16]
            en.update(mv=mv, rs=rs, nmr=nmr)
            for h in range(2):
                src = xres[:, t, h * 512:(h + 1) * 512]
                P.op("dve", lambda e, h=h, src=src: e.bn_stats(stt[:, h, :], src), reads=[src], writes=[stt[:, h, :]])
            P.op("dve", lambda e: e.bn_aggr(mv, stf), reads=[stf], writes=[mv])
            P.op("dve", lambda e: e.tensor_scalar(rs, mv[:, 1:2], EPS_S, None, ALU.add), reads=[mv], writes=[rs])
            P.op("pool", lambda e: e.tensor_tensor(rs, rs, mhalf[:, 0:1], ALU.pow), reads=[rs, mhalf[:, 0:1]], writes=[rs])

        def _s2(en):
            t, mv, rs, nmr = en["t"], en["mv"], en["rs"], en["nmr"]
            P.op("dve", lambda e: e.tensor_scalar(nmr, mv[:, 0:1], rs, -1.0, ALU.mult, ALU.mult), reads=[mv, rs], writes=[nmr])
            xn = xn_bufs[cnt["xn"] % 2]
            cnt["xn"] += 1
            en["xn"] = xn
            P.op("act", lambda e: e.activation(xn[:], xres[:, t, :], AF.Identity, bias=nmr, scale=rs),
                 reads=[xres[:, t, :], nmr, rs], writes=[xn[:]])

        def _s3(en):
            t = en["t"]
            if "xn" in en:
                xn = en["xn"]
                P.op("dve", lambda e: e.tensor_tensor(xn[:], xn[:], lnp[:, 0, :], ALU.mult), reads=[xn[:], lnp[:, 0, :]], writes=[xn[:]])
                P.op("dve", lambda e: e.tensor_tensor(xres[:, t, :], xn[:], lnp[:, 1, :], ALU.add),
                     reads=[xn[:], lnp[:, 1, :]], writes=[xres[:, t, :]])
            if final_store["on"]:
                if t % 4 == 3:
                    t4 = t // 4
                    final_store["idents"].append(
                        P.dma("sp", f"yout{t4}", [(yv[:, t4 * 4:(t4 + 1) * 4, :], xres[:, t4 * 4:(t4 + 1) * 4, :])],
                              reads=[xres[:, t4 * 4:(t4 + 1) * 4, :]]))
                return "done"
            xb = xb_bufs[cnt["xb"] % 2]
            cnt["xb"] += 1
            en["xb"] = xb
            P.op("act", lambda e: e.copy(xb[:], xres[:, t, :]), reads=[xres[:, t, :]], writes=[xb[:]])

        def _s4(en):
            t, xb = en["t"], en["xb"]
            b = bankA()
            pt = ps[:, b, :].bitcast(BF16).rearrange("p (c n) -> p c n", n=128)
            for c in range(8):
                P.op("pe", lambda e, c=c: e.transpose(pt[:, c, :], xb[:, c * 128:(c + 1) * 128], ident[:]),
                     reads=[xb[:, c * 128:(c + 1) * 128], ident[:]], writes=[ps[:, b, :]], signal=(c == 7))
            dst = xT[:, :, t * 128:(t + 1) * 128]
            if en["n"] % 2:
                P.op("act", lambda e: e.copy(dst, pt), reads=[ps[:, b, :]], writes=[dst])
            else:
                P.op("dve", lambda e: e.tensor_copy(dst, pt), reads=[ps[:, b, :]], writes=[dst])

        STAGES = (_s1, _s2, _s3, _s4)

        def _ln_advance():
            for en in list(ln_q):
                r = STAGES[en["stage"]](en)
                en["stage"] = 4 if r == "done" else en["stage"] + 1
            ln_q[:] = [en for en in ln_q if en["stage"] < 4]

        def ln_tail(t, cast_only=False):
            cnt["ln"] = cnt.get("ln", 0) + 1
            ln_q.append({"t": t, "stage": 2 if cast_only else 0, "n": cnt["ln"]})
            _ln_advance()

        def ln_flush():
            while ln_q:
                _ln_advance()

        def accum(t, h, bank, scale):
            dst = xres[:, t, h * 512:(h + 1) * 512]
            P.op("dve", lambda e: e.scalar_tensor_tensor(dst, ps[:, bank, :], scale, dst, ALU.mult, ALU.add),
                 reads=[ps[:, bank, :], dst], writes=[dst])

        def make_stream(loaders, look=3):
            state = {"n": 0, "slots": []}

            def get(n):
                while state["n"] < min(len(loaders), n + 1 + look):
                    sl = next_wst()
                    loaders[state["n"]](sl)
                    state["slots"].append(sl)
                    state["n"] += 1
                return state["slots"][n]
            get.peek = lambda n: state["slots"][n]
            return get

        def col_loader(src):
            def f(sl):
                P.dma("pool", f"wst{sl}", [(wst[:, sl, :].rearrange("p (c n) -> p c n", c=8),
                                            src.rearrange("(c p) n -> p c n", p=128))], writes=[wst[:, sl, :]])
            return f

        def ffn(i, k):
            w_in = W("ffn_w_in")[i, k]
            w_out = W("ffn_w_out")[i, k]
            actT = av(0, (11, S), BF16)
            sil = [av(45056 + q * 2048, (512,), F32) for q in range(2)]
            load_ln(i, 0 if k == 0 else 3)
            pending = []

            def win_loader(j):
                def f(sl):
                    wv = wst[:, sl, :].rearrange("p (g c n) -> p g c n", g=2, c=8)
                    P.dma("pool", f"wst{sl}",
                          [(wv[:, 0], w_in[:, j * 128:(j + 1) * 128].rearrange("(c p) n -> p c n", p=128)),
                           (wv[:, 1], w_in[:, FF + j * 128:FF + (j + 1) * 128].rearrange("(c p) n -> p c n", p=128))],
                          writes=[wst[:, sl, :]])
                return f
            get = make_stream([win_loader(j) for j in range(NJ)])
            for bi, (j0, j1) in enumerate(((0, 11), (11, 22))):
                nj = j1 - j0
                wo_v = w_out[j0 * 128:j1 * 128, :].rearrange("(c p) n -> p c n", p=128)
                P.dma("pool", "wres", [(wres[:, 0:6, :], wo_v[:, 0:6, :]), (wres[:, 6:nj, :], wo_v[:, 6:nj, :])],
                      writes=[wres[:, 0:nj, :]])
                for j in range(j0, j1):
                    s = get(j)
                    wv = wst[:, s, :].rearrange("p (g c n) -> p g c n", g=2, c=8)
                    for tg in range(4):
                        bg, bu = bankA(), bankA()
                        xs = lambda c: xT[:, c, tg * 512:(tg + 1) * 512]
                        for c in range(8):
                            mm(ps[:, bg, :], wv[:, 0, c, :], xs(c), c == 0, c == 7)
                        for c in range(8):
                            mm(ps[:, bu, :], wv[:, 1, c, :], xs(c), c == 0, c == 7)
                        sl = sil[(j * 4 + tg) % 2]
                        P.op("act", lambda e, sl=sl, bg=bg: e.activation(sl, ps[:, bg, :], AF.Silu),
                             reads=[ps[:, bg, :]], writes=[sl])
                        dst = actT[:, j - j0, tg * 512:(tg + 1) * 512]
                        P.op("dve", lambda e, dst=dst, bu=bu, sl=sl: e.tensor_tensor(dst, ps[:, bu, :], sl, ALU.mult),
                             reads=[ps[:, bu, :], sl], writes=[dst])
                for t in range(NT):
                    pb = 4 + 2 * (t % 2)
                    for h in range(2):
                        for jj in range(nj):
                            mm(ps[:, pb + h, :], actT[:, jj, t * 128:(t + 1) * 128], wres[:, jj, h * 512:(h + 1) * 512],
                               jj == 0, jj == nj - 1)
                    src2 = ps[:, pb:pb + 2, :].rearrange("p b n -> p (b n)")
                    dst2 = xres[:, t, :]
                    P.op("dve", lambda e, src2=src2, dst2=dst2: e.scalar_tensor_tensor(dst2, src2, 0.5 / ALPHA, dst2, ALU.mult, ALU.add),
                         reads=[src2, dst2], writes=[dst2])
                    if bi == 1:
                        ln_tail(t)
            ln_flush()

        pre_hook = {"fn": None}

        def call_pre_hook():
            fn, pre_hook["fn"] = pre_hook["fn"], None
            if fn is not None:
                fn()

        def prep_mem(memT):
            mf = av(OFF_XN, (2, D), F32)
            mb = av(OFF_XB, (2, D), BF16)
            P.dma("sp", "memin", [(mf, dr["mem"].rearrange("(t p) d -> p t d", p=128))], writes=[mf])
            P.op("act", lambda e: e.copy(mb, mf), reads=[mf], writes=[mb])
            for mt in range(2):
                b = bankA()
                pt = ps[:, b, :].bitcast(BF16).rearrange("p (c n) -> p c n", n=128)
                for c in range(8):
                    src = mb[:, mt, c * 128:(c + 1) * 128]
                    P.op("pe", lambda e, c=c, src=src, pt=pt: e.transpose(pt[:, c, :], src, ident[:]),
                         reads=[src, ident[:]], writes=[ps[:, b, :]], signal=(c == 7))
                dst = memT[:, :, mt * 128:(mt + 1) * 128]
                P.op("act", lambda e, dst=dst, pt=pt: e.copy(dst, pt), reads=[ps[:, b, :]], writes=[dst])

        def xa(i):
            wq, wkv, wo = W("xa_wq")[i], W("xa_wkv")[i], W("xa_wo")[i]
            qT = av(0, (8, S), BF16)
            kT = av(32768, (8, MEM), BF16)
            vx = av(36864, (2, D), BF16)
            pf = [av(40960 + q * 4096, (4, 256), F32) for q in range(2)]
            pn = [av(49152 + q * 2048, (4, 256), BF16) for q in range(2)]
            pT = av(53248, (8, 512), BF16)
            oT = av(61440, (8, 512), BF16)
            wo_sb = av(OFF_WST, (8, D), BF16)
            memT = av(53248, (8, MEM), BF16)
            prep_mem(memT)
            loaders = [col_loader(wkv[:, cc * 256:(cc + 1) * 256]) for cc in range(8)]
            loaders += [col_loader(wq[:, cc * 256:(cc + 1) * 256]) for cc in range(4)]
            get = make_stream(loaders)
            ev = {"n": 0}

            def evac(dst, src):
                ev["n"] += 1
                if ev["n"] % 2:
                    P.op("act", lambda e: e.copy(dst, src), reads=[src], writes=[dst])
                else:
                    P.op("dve", lambda e: e.tensor_copy(dst, src), reads=[src], writes=[dst])

            for cc in range(4):
                wv = wst[:, get(cc), :].rearrange("p (c n) -> p c n", c=8)
                for sub in range(2):
                    b = bankA()
                    for kc in range(8):
                        mm(ps[:, b, 0:MEM], wv[:, kc, sub * 128:(sub + 1) * 128], memT[:, kc, :], kc == 0, kc == 7)
                    evac(kT[:, cc * 2 + sub, :], ps[:, b, 0:MEM])
            for cc in range(4):
                wv = wst[:, get(4 + cc), :].rearrange("p (c n) -> p c n", c=8)
                for mt in range(2):
                    b = bankA()
                    for kc in range(8):
                        mm(ps[:, b, 0:256], memT[:, kc, mt * 128:(mt + 1) * 128], wv[:, kc, :], kc == 0, kc == 7)
                    evac(vx[:, mt, cc * 256:(cc + 1) * 256], ps[:, b, 0:256])
            yield
            load_ln(i, 2)
            for cc in range(4):
                wv = wst[:, get(8 + cc), :].rearrange("p (c n) -> p c n", c=8)
                for sub in range(2):
                    for tg in range(4):
                        b = bankA()
                        for kc in range(8):
                            mm(ps[:, b, :], wv[:, kc, sub * 128:(sub + 1) * 128], xT[:, kc, tg * 512:(tg + 1) * 512],
                               kc == 0, kc == 7)
                        evac(qT[:, cc * 2 + sub, tg * 512:(tg + 1) * 512], ps[:, b, :])
            wo_v = wo.rearrange("(c p) n -> p c n", p=128)
            P.dma("pool", "wo", [(wo_sb[:, 0:4, :], wo_v[:, 0:4, :]), (wo_sb[:, 4:8, :], wo_v[:, 4:8, :])],
                  writes=[wo_sb])
            scale = 1.0 / 16.0

            def sc_view(t):
                b2 = 4 + 2 * (t % 2)
                return ps[:, b2:b2 + 2, :].rearrange("p b (h m) -> p (b h) m", m=MEM)

            def stage_scores(t):
                sc = sc_view(t)
                for h in range(4):
                    for dc in range(2):
                        mm(sc[:, h, :], qT[:, 2 * h + dc, t * 128:(t + 1) * 128], kT[:, 2 * h + dc, :], dc == 0, dc == 1)

            def stage_softmax(t):
                q = t % 2
                sc = sc_view(t)
                mx, nb, sm, rs = (xst[:, q, 4 * n:4 * n + 4] for n in range(4))
                P.op("dve", lambda e: e.tensor_reduce(mx, sc, AX.X, ALU.max), reads=[sc], writes=[mx])
                P.op("dve", lambda e: e.tensor_scalar(nb, mx, -scale, None, ALU.mult), reads=[mx], writes=[nb])
                for h in range(4):
                    P.op("act", lambda e, h=h: e.activation(
                        pf[q][:, h, :], sc[:, h, :], AF.Exp, bias=nb[:, h:h + 1], scale=scale, accum_out=sm[:, h:h + 1]),
                        reads=[sc[:, h, :], nb], writes=[pf[q][:, h, :], sm[:, h:h + 1]])
                P.op("dve", lambda e: e.reciprocal(rs, sm), reads=[sm], writes=[rs])
                for h in range(4):
                    P.op("act", lambda e, h=h: e.activation(pn[q][:, h, :], pf[q][:, h, :], AF.Identity, scale=rs[:, h:h + 1]),
                         reads=[pf[q][:, h, :], rs], writes=[pn[q][:, h, :]])

            def stage_transpose(t):
                q = t % 2
                tl = t % 4
                b = bankA()
                pt = ps[:, b, :].bitcast(BF16).rearrange("p (c n) -> p c n", n=128)
                for hm in range(8):
                    src = pn[q][:, hm // 2, (hm % 2) * 128:(hm % 2 + 1) * 128]
                    P.op("pe", lambda e, hm=hm, src=src: e.transpose(pt[:, hm, :], src, ident[:]),
                         reads=[src, ident[:]], writes=[ps[:, b, :]], signal=(hm == 7))
                evac(pT[:, :, tl * 128:(tl + 1) * 128], pt)

            def group_tail(tg):
                for c in range(8):
                    b = bankA()
                    for mc in range(2):
                        mm(ps[:, b, :], vx[:, mc, c * 128:(c + 1) * 128], pT[:, (c // 2) * 2 + mc, :], mc == 0, mc == 1)
                    evac(oT[:, c, :], ps[:, b, :])
                for tl in range(4):
                    tt = tg * 4 + tl
                    for h in range(2):
                        b = bankA()
                        for c in range(8):
                            mm(ps[:, b, :], oT[:, c, tl * 128:(tl + 1) * 128], wo_sb[:, c, h * 512:(h + 1) * 512], c == 0, c == 7)
                        accum(tt, h, b, 1.0 / ALPHA)
                    ln_tail(tt)

            stage_scores(0)
            for t in range(NT + 1):
                if t < NT:
                    stage_softmax(t)
                if t + 1 < NT:
                    stage_scores(t + 1)
                if t >= 1:
                    stage_transpose(t - 1)
                    if (t - 1) % 4 == 3:
                        group_tail((t - 1) // 4)
            ln_flush()

        def poolmix(i):
            j = i // 3
            pmat = av(0, (8, 128), BF16)
            btab = av(2048, (D,), F32)
            stab = av(6144, (D,), F32)
            icnt = av(10240, (4, 128), F32)
            xbp = [av(12288 + q * 2048, (D,), BF16) for q in range(3)]
            poT = [av(18432 + q * 2048, (8, 128), BF16) for q in range(2)]
            tmp = av(22528, (D,), F32)
            tms = av(26624, (128,), F32)
            load_ln(i, 1)
            sl = next_wst()
            pw = wst[:, sl, :].rearrange("p (g c n) -> p g c n", g=4, c=2)
            P.dma("pool", f"wst{sl}", [(pw, W("pool_w")[j].rearrange("g (c p) n -> p g c n", p=128))], writes=[wst[:, sl, :]])
            P.dma("pool", "pmat", [(pmat, dr["poolmat"].rearrange("m p n -> p m n"))], writes=[pmat])
            P.dma("sp", "ptab", [(btab, W("pool_b")[j:j + 1].rearrange("o g d -> o (g d)").partition_broadcast(128)),
                                 (stab, W("pool_scale")[j:j + 1, :].partition_broadcast(128)),
                                 (av(10240, (512,), F32), dr["poolcnt"].rearrange("o g n -> o (g n)").partition_broadcast(128))],
                  writes=[btab, stab, icnt])
            P.op("dve", lambda e: e.tensor_scalar(stab, stab, 1.0 / ALPHA, None, ALU.mult), reads=[stab], writes=[stab])
            P.op("dve", lambda e: e.tensor_tensor(btab, btab, stab, ALU.mult), reads=[btab, stab], writes=[btab])
            wins = (2, 4, 8, 16)
            pending = []
            def win_stage(t):
                xc = xbp[t % 3]
                xp = xbp[(t - 1) % 3]
                P.op("act", lambda e: e.copy(xc, xres[:, t, :]), reads=[xres[:, t, :]], writes=[xc])
                po = poT[t % 2]
                for half in range(2):
                    b = bankA()
                    for cl in range(4):
                        c = half * 4 + cl
                        g = c // 2
                        out = ps[:, b, cl * 128:(cl + 1) * 128]
                        mm(out, xc[:, c * 128:(c + 1) * 128], pmat[:, 2 * g, :], True, t == 0)
                        if t > 0:
                            mm(out, xp[:, c * 128:(c + 1) * 128], pmat[:, 2 * g + 1, :], False, True)
                    for cl in range(4):
                        c = half * 4 + cl
                        g = c // 2
                        out = ps[:, b, cl * 128:(cl + 1) * 128]
                        xt = xT[:, c, t * 128:(t + 1) * 128]
                        if t == 0:
                            P.op("dve", lambda e, out=out, g=g: e.tensor_tensor(tms, out, icnt[:, g, :], ALU.mult),
                                 reads=[out, icnt[:, g, :]], writes=[tms])
                            P.op("dve", lambda e, c=c, xt=xt: e.tensor_tensor(po[:, c, :], tms, xt, ALU.subtract),
                                 reads=[tms, xt], writes=[po[:, c, :]])
                        else:
                            P.op("dve", lambda e, c=c, xt=xt, out=out, g=g: e.scalar_tensor_tensor(
                                po[:, c, :], out, 1.0 / wins[g], xt, ALU.mult, ALU.subtract),
                                reads=[out, xt], writes=[po[:, c, :]])

            def proj_stage(t):
                po = poT[t % 2]
                q = t % 2
                b2 = 4 + 2 * q
                yv2 = ps[:, b2:b2 + 2, :].rearrange("p b (g n) -> p (b g) n", n=256)
                for g in range(4):
                    for kc in range(2):
                        mm(yv2[:, g, :], po[:, 2 * g + kc, :], pw[:, g, kc, :], kc == 0, kc == 1)
                yflat = ps[:, b2:b2 + 2, :].rearrange("p b n -> p (b n)")
                P.op("dve", lambda e: e.tensor_tensor(tmp, yflat, stab, ALU.mult), reads=[yflat, stab], writes=[tmp])
                P.op("dve", lambda e: e.tensor_tensor(tmp, tmp, btab, ALU.add), reads=[tmp, btab], writes=[tmp])
                P.op("dve", lambda e: e.tensor_tensor(xres[:, t, :], xres[:, t, :], tmp, ALU.add),
                     reads=[xres[:, t, :], tmp], writes=[xres[:, t, :]])
                ln_tail(t)

            for t in range(NT + 1):
                if t < NT:
                    win_stage(t)
                if t >= 1:
                    proj_stage(t - 1)
            ln_flush()

        TWO_PI = 2.0 * math.pi
        C1 = 6.28125
        C2 = TWO_PI - C1

        def rope_tables(cosT, sinT, invf, sgn, tmp_i, tmp_a, tmp_b):
            P.dma("sp", "posin", [(tmp_i, dr["positions"].partition_broadcast(128))], writes=[tmp_i])
            ang = tmp_a
            P.op("dve", lambda e: e.tensor_copy(ang, tmp_i), reads=[tmp_i], writes=[ang])
            P.op("dve", lambda e: e.tensor_scalar(ang, ang, invf, None, ALU.mult), reads=[ang, invf], writes=[ang])
            kf = tmp_b
            ki = tmp_i
            P.op("dve", lambda e: e.tensor_scalar(kf, ang, 1.0 / TWO_PI, None, ALU.mult), reads=[ang], writes=[kf])
            P.op("dve", lambda e: e.tensor_copy(ki, kf), reads=[kf], writes=[ki])
            P.op("dve", lambda e: e.tensor_copy(kf, ki), reads=[ki], writes=[kf])
            P.op("dve", lambda e: e.scalar_tensor_tensor(ang, kf, -C1, ang, ALU.mult, ALU.add), reads=[kf, ang], writes=[ang])
            P.op("dve", lambda e: e.scalar_tensor_tensor(ang, kf, -C2, ang, ALU.mult, ALU.add), reads=[kf, ang], writes=[ang])

            def wrap(r):
                P.op("dve", lambda e: e.tensor_scalar(kf, r, math.pi, None, ALU.is_gt), reads=[r], writes=[kf])
                P.op("dve", lambda e: e.scalar_tensor_tensor(r, kf, -TWO_PI, r, ALU.mult, ALU.add), reads=[kf, r], writes=[r])
                P.op("dve", lambda e: e.tensor_scalar(kf, r, -math.pi, None, ALU.is_lt), reads=[r], writes=[kf])
                P.op("dve", lambda e: e.scalar_tensor_tensor(r, kf, TWO_PI, r, ALU.mult, ALU.add), reads=[kf, r], writes=[r])
            wrap(ang)
            P.op("act", lambda e: e.activation(sinT, ang, AF.Sin), reads=[ang], writes=[sinT])
            if sgn is not None:
                P.op("dve", lambda e: e.tensor_scalar(sinT, sinT, sgn, None, ALU.mult), reads=[sinT, sgn], writes=[sinT])
            P.op("dve", lambda e: e.tensor_scalar(ang, ang, 0.5 * math.pi, None, ALU.add), reads=[ang], writes=[ang])
            wrap(ang)
            P.op("act", lambda e: e.activation(cosT, ang, AF.Sin), reads=[ang], writes=[cosT])

        def diffattn(i):
            j = i // 3
            lam_init = 0.8 - 0.6 * math.exp(-0.3 * i)
            wqkv = W("da_w_qkv")[j]
            w_o = W("da_w_o")[j]
            onT = av(0, (4, S), BF16)
            qT = [av(16384 + q * 4096, (S,), BF16) for q in range(2)]
            kz = [[av(24576 + c * 4096, (S,), BF16) for c in range(2)],
                  [av(OFF_LNP + c * 4096, (S,), BF16) for c in range(2)]]
            vv = [av(32768 + q * 4160, (NT, 130), BF16) for q in range(2)]
            cosT = av(41088, (S,), F32)
            sinT = av(49280, (S,), F32)
            tA = av(57472, (512,), F32)
            tB = av(59520, (512,), F32)
            wo_sb = av(61568, (4, D), BF16)
            gtab = av(OFF_XB, (128,), F32)
            tt = av(OFF_XB + 512, (128,), F32)
            dds = [av(OFF_XB + 1024 + q * 512, (128,), F32) for q in range(2)]
            dn = [av(OFF_XB + 2048 + q * 256, (128,), BF16) for q in range(2)]
            junk = av(OFF_XB + 2560, (128,), BF16)
            accS = av(OFF_XN, (8, 130), F32)
            daw = av(OFF_WST, (5, 8, 128), BF16)
            eT = [av(OFF_WST + 10240 + q * 1024, (512,), BF16) for q in range(5)]
            rope_tables(cosT, sinT, dacst[:, 0:1], dacst[:, 1:2],
                        av(16384, (S,), I32), av(24576, (S,), F32), av(32768, (S,), F32))
            if DA_STOP:
                P.op("dve", lambda e: e.memset(onT, 0.0), writes=[onT])
            for q in range(2):
                for c in range(2):
                    P.op("dve", lambda e, q=q, c=c: e.memset(kz[q][c], 0.0), writes=[kz[q][c]])
            for q in range(2):
                P.op("dve", lambda e, q=q: e.memset(vv[q][:, :, 128:130], 1.0), writes=[vv[q]])
            lq = av(57472, (128,), F32)
            lk = av(57472 + 512, (128,), F32)
            P.dma("sp", "dalam", [(lq, W("da_lam_q")[j:j + 1].rearrange("o c d -> o (c d)").partition_broadcast(128)),
                                  (lk, W("da_lam_k")[j:j + 1].rearrange("o c d -> o (c d)").partition_broadcast(128)),
                                  (gtab, W("da_subln_g")[j:j + 1, :].partition_broadcast(128))],
                  writes=[lq, lk, gtab])
            P.op("dve", lambda e: e.tensor_tensor(lq, lq, lk, ALU.mult), reads=[lq, lk], writes=[lq])
            P.op("dve", lambda e: e.tensor_reduce(dsm[:, 0:2], lq.rearrange("p (c d) -> p c d", d=64), AX.X, ALU.add),
                 reads=[lq], writes=[dsm[:, 0:2]])
            P.op("act", lambda e: e.activation(dsm[:, 0:2], dsm[:, 0:2], AF.Exp), reads=[dsm[:, 0:2]], writes=[dsm[:, 0:2]])
            P.op("dve", lambda e: e.tensor_tensor(dsm[:, 2:3], dsm[:, 1:2], dsm[:, 0:1], ALU.subtract),
                 reads=[dsm[:, 0:2]], writes=[dsm[:, 2:3]])
            P.op("dve", lambda e: e.tensor_scalar(dsm[:, 2:3], dsm[:, 2:3], -lam_init, None, ALU.add),
                 reads=[dsm[:, 2:3]], writes=[dsm[:, 2:3]])
            P.op("dve", lambda e: e.tensor_scalar(gtab, gtab, 1.0 - lam_init, None, ALU.mult), reads=[gtab], writes=[gtab])
            nlam = dsm[:, 2:3]
            epsc = dsm[:, 5:6]
            P.op("dve", lambda e: e.memset(epsc, LN_EPS), writes=[epsc])

            def load_w(h):
                srcs = [wqkv[:, n * 1024 + h * 128:n * 1024 + (h + 1) * 128].rearrange("(c p) n -> p c n", p=128) for n in range(3)]
                P.dma("pool", "daw", [(daw[:, n], srcs[n]) for n in range(3)], writes=[daw[:, 0:3]])
                for n in range(2):
                    src = daw[:, n].rearrange("p c (g f d) -> p c g f d", g=2, f=2)
                    dst = daw[:, 3 + n].rearrange("p c (g f d) -> p c g f d", g=2, f=2)
                    for f in range(2):
                        sv = src[:, :, :, 1 - f, :].rearrange("p c g d -> p (c g) d")
                        dv = dst[:, :, :, f, :].rearrange("p c g d -> p (c g) d")
                        if f == 0:
                            P.op("act", lambda e, sv=sv, dv=dv: e.copy(dv, sv), reads=[daw[:, n]], writes=[daw[:, 3 + n]])
                        else:
                            P.op("dve", lambda e, sv=sv, dv=dv: e.tensor_copy(dv, sv), reads=[daw[:, n]], writes=[daw[:, 3 + n]])

            def project(h):
                pp = h % 2
                for n, dstT in ((0, qT[pp]), (1, None)):
                    for tg in range(4):
                        ba, bb = bankA(), bankA()
                        cols = slice(tg * 512, (tg + 1) * 512)
                        for kc in range(8):
                            mm(ps[:, ba, :], daw[:, n, kc, :], xT[:, kc, cols], kc == 0, kc == 7)
                        for kc in range(8):
                            mm(ps[:, bb, :], daw[:, 3 + n, kc, :], xT[:, kc, cols], kc == 0, kc == 7)
                        P.op("dve", lambda e, ba=ba, cols=cols: e.tensor_tensor(tA, ps[:, ba, :], cosT[:, cols], ALU.mult),
                             reads=[ps[:, ba, :], cosT[:, cols]], writes=[tA])
                        P.op("dve", lambda e, bb=bb, cols=cols: e.tensor_tensor(tB, ps[:, bb, :], sinT[:, cols], ALU.mult),
                             reads=[ps[:, bb, :], sinT[:, cols]], writes=[tB])
                        if dstT is not None:
                            P.op("dve", lambda e, dstT=dstT, cols=cols: e.tensor_tensor(dstT[:, cols], tA, tB, ALU.add),
                                 reads=[tA, tB], writes=[dstT[:, cols]])
                        else:
                            for c in range(2):
                                rows = slice(c * 64, (c + 1) * 64)
                                dk = kz[pp][c]
                                P.op("dve", lambda e, dk=dk, rows=rows, cols=cols: e.tensor_tensor(dk[rows, cols], tA[rows, :], tB[rows, :], ALU.add),
                                     reads=[tA, tB], writes=[dk[:, cols]])
                for t4 in range(4):
                    b = bankA()
                    for tl in range(4):
                        t = t4 * 4 + tl
                        for kc in range(8):
                            mm(ps[:, b, tl * 128:(tl + 1) * 128], xT[:, kc, t * 128:(t + 1) * 128], daw[:, 2, kc, :], kc == 0, kc == 7)
                    dst = vv[pp][:, t4 * 4:(t4 + 1) * 4, 0:128]
                    P.op("act", lambda e, dst=dst, b=b: e.copy(dst, ps[:, b, :].rearrange("p (a n) -> p a n", n=128)),
                         reads=[ps[:, b, :]], writes=[dst])

            ACC_ORDER = [(0, 0), (1, 0), (0, 1), (1, 1), (0, 2), (1, 2), (0, 3), (1, 3)]
            ACC_IDX = {cq: k for k, cq in enumerate(ACC_ORDER)}

            def acc_ap(c, qt):
                k = ACC_IDX[(c, qt)]
                return ps[:, 4 + k // 3, (k % 3) * 130:(k % 3) * 130 + 130]

            deferred = []

            def run_deferred(n=None):
                while deferred and (n is None or n > 0):
                    deferred.pop(0)()
                    if n is not None:
                        n -= 1

            def attend(h, G):
                pp = h % 2
                items = [(jj, c) for jj in range(4 * G + 4) for c in range(2)]
                LAG = 3
                info = {}

                def scores(idx):
                    jj, c = items[idx]
                    q0 = max(jj - 4 * G, 0) * 128
                    sb = bankA()
                    mm(ps[:, sb, q0:512], kz[pp][c][:, jj * 128:(jj + 1) * 128], qT[pp][:, G * 512 + q0:(G + 1) * 512], True, True)
                    et = eT[(idx + 5 * (h * 4 + G)) % 5]
                    P.op("act", lambda e: e.activation(et[:, q0:512], ps[:, sb, q0:512], AF.Exp, scale=0.125),
                         reads=[ps[:, sb, q0:512]], writes=[et[:, q0:512]])
                    if jj >= 4 * G:
                        P.op("dve", lambda e: e.tensor_tensor(et[:, q0:q0 + 128], et[:, q0:q0 + 128], tri[:], ALU.mult),
                             reads=[et[:, q0:q0 + 128], tri[:]], writes=[et[:, q0:q0 + 128]])
                    info[idx] = et

                seq = [(idx, qt) for idx, (jj, c) in enumerate(items) for qt in range(max(jj - 4 * G, 0), 4)]
                first, last = {}, {}
                for n_, (idx, qt) in enumerate(seq):
                    bk = 4 + ACC_IDX[(items[idx][1], qt)] // 3
                    first.setdefault(bk, n_)
                    last[bk] = n_
                pos = {iq: n_ for n_, iq in enumerate(seq)}

                def pv(idx):
                    jj, c = items[idx]
                    et = info.pop(idx)
                    if DA_STOP == 5:
                        return
                    for qt in range(max(jj - 4 * G, 0), 4):
                        bk = 4 + ACC_IDX[(c, qt)] // 3
                        n_ = pos[(idx, qt)]
                        mm(acc_ap(c, qt), et[:, qt * 128:(qt + 1) * 128], vv[pp][:, jj, 0:130], first[bk] == n_, last[bk] == n_)

                for idx in range(len(items)):
                    scores(idx)
                    if idx >= LAG:
                        pv(idx - LAG)
                    run_deferred(1)
                for idx in range(len(items) - LAG, len(items)):
                    pv(idx)
                run_deferred()
                if DA_STOP == 5:
                    return
                for k, (c, qt) in enumerate(ACC_ORDER):
                    src = acc_ap(c, qt)
                    dstv = accS[:, k, 0:130]
                    if (k // 3) % 2:
                        P.op("act", lambda e, src=src, dstv=dstv: e.copy(dstv, src), reads=[src], writes=[dstv])
                    else:
                        P.op("dve", lambda e, src=src, dstv=dstv: e.tensor_copy(dstv, src), reads=[src], writes=[dstv])
                dst = onT[:, h % 4, G * 512:(G + 1) * 512]

                def n1(qt):
                    k1, k2 = ACC_IDX[(0, qt)], ACC_IDX[(1, qt)]
                    o1, s1 = accS[:, k1, 0:128], accS[:, k1, 128:129]
                    o2, s2 = accS[:, k2, 0:128], accS[:, k2, 128:129]
                    rc1, rc2, ss, rs = (dsc[:, qt * 4 + n:qt * 4 + n + 1] for n in range(4))
                    dd = dds[qt % 2]
                    P.op("dve", lambda e: e.reciprocal(rc1, s1), reads=[s1], writes=[rc1])
                    P.op("dve", lambda e: e.reciprocal(rc2, s2), reads=[s2], writes=[rc2])
                    P.op("dve", lambda e: e.tensor_tensor(rc2, rc2, nlam, ALU.mult), reads=[rc2, nlam], writes=[rc2])
                    P.op("dve", lambda e: e.tensor_scalar(tt, o2, rc2, None, ALU.mult), reads=[o2, rc2], writes=[tt])
                    P.op("dve", lambda e: e.scalar_tensor_tensor(dd, o1, rc1, tt, ALU.mult, ALU.add),
                         reads=[o1, rc1, tt], writes=[dd])
                    P.op("act", lambda e: e.activation(junk, dd, AF.Square, accum_out=ss), reads=[dd], writes=[junk, ss])

                def n2(qt):
                    ss, rs = dsc[:, qt * 4 + 2:qt * 4 + 3], dsc[:, qt * 4 + 3:qt * 4 + 4]
                    P.op("dve", lambda e: e.tensor_scalar(rs, ss, 1.0 / 128.0, LN_EPS, ALU.mult, ALU.add), reads=[ss], writes=[rs])
                    P.op("pool", lambda e: e.tensor_tensor(rs, rs, mhalf[:, 0:1], ALU.pow), reads=[rs, mhalf[:, 0:1]], writes=[rs])

                def n3a(qt):
                    rs = dsc[:, qt * 4 + 3:qt * 4 + 4]
                    dd, dq = dds[qt % 2], dn[qt % 2]
                    P.op("dve", lambda e: e.scalar_tensor_tensor(dq, dd, rs, gtab, ALU.mult, ALU.mult),
                         reads=[dd, rs, gtab], writes=[dq])

                def n3b(qt):
                    dq = dn[qt % 2]
                    ptT = ps[:, 7, :].bitcast(BF16).rearrange("p (c n) -> p c n", n=128)
                    P.op("pe", lambda e: e.transpose(ptT[:, qt, :], dq, ident[:]),
                         reads=[dq, ident[:]], writes=[ps[:, 7, :]], signal=True)

                def n4():
                    ptT = ps[:, 7, :].bitcast(BF16).rearrange("p (c n) -> p c n", n=128)
                    P.op("act", lambda e: e.copy(dst.rearrange("p (c n) -> p c n", n=128), ptT[:, 0:4, :]),
                         reads=[ps[:, 7, :]], writes=[dst])

                order = [(n1, 0), (n1, 1), (n2, 0), (n2, 1), (n3a, 0), (n3a, 1), (n1, 2), (n1, 3), (n3b, 0), (n3b, 1),
                         (n2, 2), (n2, 3), (n3a, 2), (n3a, 3), None, None, (n3b, 2), (n3b, 3)]
                for it_ in order:
                    if it_ is None:
                        deferred.append(lambda: None)
                    else:
                        deferred.append(lambda fn=it_[0], qt=it_[1]: fn(qt))
                deferred.append(n4)

            def wo_pass(pi, final):
                run_deferred()
                wv_ = w_o[pi * 512:(pi + 1) * 512, :].rearrange("(c p) n -> p c n", p=128)
                P.dma("pool", "dawo", [(wo_sb[:, 0:2, :], wv_[:, 0:2, :]), (wo_sb[:, 2:4, :], wv_[:, 2:4, :])], writes=[wo_sb])
                pending = []
                for t in range(NT):
                    for hf in range(2):
                        b = bankA()
                        for hh in range(4):
                            mm(ps[:, b, :], onT[:, hh, t * 128:(t + 1) * 128], wo_sb[:, hh, hf * 512:(hf + 1) * 512], hh == 0, hh == 3)
                        accum(t, hf, b, 1.0 / ALPHA)
                    if final:
                        ln_tail(t)
                ln_flush()

            load_w(0)
            project(0)
            for h in range(8):
                if h + 1 < 8:
                    load_w(h + 1)
                for G in range(4):
                    attend(h, G)
                    if G == 1 and h + 1 < 8:
                        project(h + 1)
                if h == 3:
                    wo_pass(0, False)
            run_deferred()
            load_ln(i, 1)
            call_pre_hook()
            wo_pass(1, True)

        def retention(i):
            j = i // 3
            wall = W("ret_w_qkvg")[j]
            w_o = W("ret_w_o")[j]
            cosR = av(0, (S,), F32)
            sinR = av(8192, (S,), F32)
            qT = av(16384, (2, S), BF16)
            kT = av(24576, (2, S), BF16)
            ktok = av(32768, (NT, 256), BF16)
            vv = av(40960, (NT, 512), BF16)
            Rf = av(57344, (2, 512), F32)
            Rb = av(61440, (2, 512), BF16)
            wo_h = av(63488, (4, D), BF16)
            T1, T2, T3, T4 = (av(OFF_XN + q * 2048, (512,), F32) for q in range(4))
            T4s = [T2, T4]
            T2s = [av(OFF_LNP + q * 2048, (512,), F32) for q in range(3)]
            attb = attb_sb
            ybf = [av(OFF_XB + q * 1024, (512,), BF16) for q in range(2)]
            yTn = [av(OFF_XB + 2048 + q * 1024, (4, 128), BF16) for q in range(2)]
            rope_tables(cosR, sinR, retcst[:, 0:1], None,
                        av(16384, (S,), I32), av(24576, (S,), F32), av(32768, (S,), F32))
            loaders = []
            for h in range(4):
                loaders.append(col_loader(wall[:, h * 256:(h + 1) * 256]))
                loaders.append(col_loader(wall[:, 1024 + h * 256:1024 + (h + 1) * 256]))
                for cb in range(2):
                    loaders.append(col_loader(wall[:, 2048 + h * 512 + cb * 256:2048 + h * 512 + (cb + 1) * 256]))
                for cb in range(2):
                    loaders.append(col_loader(wall[:, 4096 + h * 512 + cb * 256:4096 + h * 512 + (cb + 1) * 256]))
            get = make_stream(loaders)
            sv = lambda sl: wst[:, sl, :].rearrange("p (c n) -> p c n", c=8)
            for h in range(4):
                gam = 1.0 - 2.0 ** (-5.0 - h)
                gC = gam ** 128
                qdec = retcst[:, 1 + h:2 + h]
                kdec = retcst[:, 5 + h:6 + h]
                wv_ = w_o[h * 512:(h + 1) * 512, :].rearrange("(c p) n -> p c n", p=128)
                P.dma("pool", "retwo", [(wo_h[:, 0:2, :], wv_[:, 0:2, :]), (wo_h[:, 2:4, :], wv_[:, 2:4, :])], writes=[wo_h])
                for n, dstT in ((0, qT), (1, kT)):
                    wv = sv(get(h * 6 + n))
                    for tg in range(4):
                        ba, bb = bankA(), bankA()
                        cols = slice(tg * 512, (tg + 1) * 512)
                        for kc in range(8):
                            mm(ps[:, ba, :], wv[:, kc, 0:128], xT[:, kc, cols], kc == 0, kc == 7)
                        for kc in range(8):
                            mm(ps[:, bb, :], wv[:, kc, 128:256], xT[:, kc, cols], kc == 0, kc == 7)
                        A_, B_ = ps[:, ba, :], ps[:, bb, :]
                        c_, s_ = cosR[:, cols], sinR[:, cols]
                        P.op("dve", lambda e, A_=A_, c_=c_: e.tensor_tensor(T1, A_, c_, ALU.mult), reads=[A_, c_], writes=[T1])
                        P.op("dve", lambda e, B_=B_, s_=s_: e.tensor_tensor(T2, B_, s_, ALU.mult), reads=[B_, s_], writes=[T2])
                        P.op("dve", lambda e, dstT=dstT, cols=cols: e.tensor_tensor(dstT[:, 0, cols], T1, T2, ALU.subtract),
                             reads=[T1, T2], writes=[dstT[:, 0, cols]])
                        P.op("dve", lambda e, B_=B_, c_=c_: e.tensor_tensor(T3, B_, c_, ALU.mult), reads=[B_, c_], writes=[T3])
                        P.op("dve", lambda e, A_=A_, s_=s_: e.tensor_tensor(T4, A_, s_, ALU.mult), reads=[A_, s_], writes=[T4])
                        P.op("dve", lambda e, dstT=dstT, cols=cols: e.tensor_tensor(dstT[:, 1, cols], T3, T4, ALU.add),
                             reads=[T3, T4], writes=[dstT[:, 1, cols]])
                for n4 in range(4):
                    b = bankA()
                    pt = ps[:, b, :].bitcast(BF16).rearrange("p (c n) -> p c n", n=128)
                    for nl in range(4):
                        n = n4 * 4 + nl
                        for ab in range(2):
                            src = kT[:, ab, n * 128:(n + 1) * 128]
                            P.op("pe", lambda e, src=src, pt=pt, idx=nl * 2 + ab: e.transpose(pt[:, idx, :], src, ident[:]),
                                 reads=[src, ident[:]], writes=[ps[:, b, :]], signal=(nl == 3 and ab == 1))
                    dst = ktok[:, n4 * 4:(n4 + 1) * 4, :].rearrange("p a (b n) -> p (a b) n", n=128)
                    P.op("dve", lambda e, dst=dst, pt=pt, kdec=kdec: e.tensor_scalar(dst, pt, kdec, None, ALU.mult),
                         reads=[ps[:, b, :], kdec], writes=[dst])
                v0 = sv(get(h * 6 + 2))
                v1 = sv(get.peek(h * 6 + 3))
                for n in range(NT):
                    b = bankA()
                    for cb, wv in enumerate((v0, v1)):
                        for kc in range(8):
                            mm(ps[:, b, cb * 256:(cb + 1) * 256], xT[:, kc, n * 128:(n + 1) * 128], wv[:, kc, :], kc == 0, kc == 7)
                    if n % 2:
                        P.op("act", lambda e, n=n, b=b: e.copy(vv[:, n, :], ps[:, b, :]), reads=[ps[:, b, :]], writes=[vv[:, n, :]])
                    else:
                        P.op("dve", lambda e, n=n, b=b: e.tensor_copy(vv[:, n, :], ps[:, b, :]), reads=[ps[:, b, :]], writes=[vv[:, n, :]])
                g0 = sv(get(h * 6 + 4))
                g1 = sv(get.peek(h * 6 + 5))
                DT = retdt[:, h, :]

                def sA(n, DT=DT):
                    tok = slice(n * 128, (n + 1) * 128)
                    ab_ = attb[:, n % 2, :]
                    b = bankA()
                    for ab in range(2):
                        mm(ps[:, b, 0:128], kT[:, ab, tok], qT[:, ab, tok], ab == 0, ab == 1)
                    P.op("dve", lambda e: e.tensor_tensor(ab_, ps[:, b, 0:128], DT, ALU.mult),
                         reads=[ps[:, b, 0:128], DT], writes=[ab_])

                def sB(n, qdec=qdec, gC=gC):
                    tok = slice(n * 128, (n + 1) * 128)
                    ab_ = attb[:, n % 2, :]
                    o_ = T2s[n % 3]
                    bx = bankB()
                    mm(ps[:, bx, :], ab_, vv[:, n, :], True, True)
                    if n > 0:
                        by = bankB()
                        for ab in range(2):
                            mm(ps[:, by, :], qT[:, ab, tok], Rb[:, ab, :], ab == 0, ab == 1)
                        P.op("act", lambda e: e.copy(T1, ps[:, bx, :]), reads=[ps[:, bx, :]], writes=[T1])
                        P.op("dve", lambda e: e.scalar_tensor_tensor(o_, ps[:, by, :], qdec, T1, ALU.mult, ALU.add),
                             reads=[ps[:, by, :], qdec, T1], writes=[o_])
                    else:
                        P.op("act", lambda e: e.copy(o_, ps[:, bx, :]), reads=[ps[:, bx, :]], writes=[o_])
                    if n < NT - 1:
                        for ab in range(2):
                            bz = bankB()
                            mm(ps[:, bz, :], ktok[:, n, ab * 128:(ab + 1) * 128], vv[:, n, :], True, True)
                            if n == 0:
                                P.op("dve", lambda e, ab=ab, bz=bz: e.tensor_copy(Rf[:, ab, :], ps[:, bz, :]),
                                     reads=[ps[:, bz, :]], writes=[Rf[:, ab, :]])
                            else:
                                P.op("dve", lambda e, ab=ab, bz=bz: e.scalar_tensor_tensor(Rf[:, ab, :], Rf[:, ab, :], gC, ps[:, bz, :], ALU.mult, ALU.add),
                                     reads=[Rf[:, ab, :], ps[:, bz, :]], writes=[Rf[:, ab, :]])
                        P.op("act", lambda e: e.copy(Rb, Rf), reads=[Rf], writes=[Rb])

                def sC(n, st_):
                    o_ = T2s[n % 3]
                    q = cnt["sm"] % 8
                    cnt["sm"] += 1
                    stf = small[:, q * 16:q * 16 + 6]
                    mv = small[:, q * 16 + 12:q * 16 + 14]
                    rs = small[:, q * 16 + 14:q * 16 + 15]
                    nmr = small[:, q * 16 + 15:q * 16 + 16]
                    st_[n] = (mv, rs, nmr)
                    P.op("dve", lambda e: e.bn_stats(stf, o_), reads=[o_], writes=[stf])
                    P.op("dve", lambda e: e.bn_aggr(mv, stf), reads=[stf], writes=[mv])
                    P.op("dve", lambda e: e.tensor_scalar(rs, mv[:, 1:2], LN_EPS, None, ALU.add), reads=[mv], writes=[rs])
                    P.op("pool", lambda e: e.tensor_tensor(rs, rs, mhalf[:, 0:1], ALU.pow), reads=[rs, mhalf[:, 0:1]], writes=[rs])

                def sD(n, st_, g0=g0, g1=g1):
                    tok = slice(n * 128, (n + 1) * 128)
                    o_ = T2s[n % 3]
                    sg = T4s[n % 2]
                    yb = ybf[n % 2]
                    mv, rs, nmr = st_.pop(n)
                    bg = bankA()
                    for cb, wv in enumerate((g0, g1)):
                        for kc in range(8):
                            mm(ps[:, bg, cb * 256:(cb + 1) * 256], xT[:, kc, tok], wv[:, kc, :], kc == 0, kc == 7)
                    P.op("act", lambda e: e.activation(sg, ps[:, bg, :], AF.Silu), reads=[ps[:, bg, :]], writes=[sg])
                    P.op("dve", lambda e: e.tensor_scalar(nmr, mv[:, 0:1], rs, -1.0, ALU.mult, ALU.mult),
                         reads=[mv, rs], writes=[nmr])
                    P.op("act", lambda e: e.activation(T3, o_, AF.Identity, bias=nmr, scale=rs),
                         reads=[o_, nmr, rs], writes=[T3])
                    P.op("dve", lambda e: e.tensor_tensor(yb, T3, sg, ALU.mult), reads=[T3, sg], writes=[yb])

                def sE(n):
                    yb = ybf[n % 2]
                    b = bankA()
                    pt = ps[:, b, :].bitcast(BF16).rearrange("p (c n) -> p c n", n=128)
                    for ec in range(4):
                        src = yb[:, ec * 128:(ec + 1) * 128]
                        P.op("pe", lambda e, ec=ec, src=src: e.transpose(pt[:, ec, :], src, ident[:]),
                             reads=[src, ident[:]], writes=[ps[:, b, :]], signal=(ec == 3))
                    yt = yTn[n % 2]
                    P.op("act", lambda e: e.copy(yt, pt[:, 0:4, :]), reads=[ps[:, b, :]], writes=[yt])

                def sF(n):
                    yt = yTn[n % 2]
                    for hf in range(2):
                        b = bankA()
                        for ec in range(4):
                            mm(ps[:, b, :], yt[:, ec, :], wo_h[:, ec, hf * 512:(hf + 1) * 512], ec == 0, ec == 3)
                        accum(n, hf, b, 1.0 / ALPHA)

                st_ = {}
                for it in range(NT + 5):
                    if it < NT:
                        sA(it)
                    if 0 <= it - 1 < NT:
                        sB(it - 1)
                    if 0 <= it - 2 < NT:
                        sC(it - 2, st_)
                    if 0 <= it - 3 < NT:
                        sD(it - 3, st_)
                    if 0 <= it - 4 < NT:
                        sE(it - 4)
                    if 0 <= it - 5 < NT:
                        sF(it - 5)
            load_ln(i, 1)
            call_pre_hook()
            if not DEBUG_NO_LN:
                for t in range(NT):
                    ln_tail(t)
                ln_flush()

        xv = dr["x"].rearrange("(t p) d -> p t d", p=128)
        for t4 in range(4):
            P.dma("sp", f"xin{t4}", [(xres[:, t4 * 4:(t4 + 1) * 4, :], xv[:, t4 * 4:(t4 + 1) * 4, :])],
                  writes=[xres[:, t4 * 4:(t4 + 1) * 4, :]])
        for t in range(NT):
            ln_tail(t, cast_only=True)
        ln_flush()
        xa_gen = {"g": None}
        for n_item, item in enumerate(plan):
            final_store["on"] = (n_item == len(plan) - 1) and item[0] in ("ffn", "xa")
            nxt = plan[n_item + 1] if n_item + 1 < len(plan) else None
            if item[0] == "mix" and nxt is not None and nxt[0] == "xa":
                xa_gen["g"] = xa(nxt[1])
                pre_hook["fn"] = lambda: next(xa_gen["g"])
            if item[0] == "ffn":
                ffn(item[1], item[2])
            elif item[0] == "xa":
                if xa_gen["g"] is None:
                    xa_gen["g"] = xa(item[1])
                    next(xa_gen["g"])
                for _ in xa_gen["g"]:
                    pass
                xa_gen["g"] = None
            elif item[0] == "mix" and item[1] % 3 == 1:
                poolmix(item[1])
            elif item[0] == "mix" and item[1] % 3 == 0:
                diffattn(item[1])
            elif item[0] == "mix" and item[1] % 3 == 2:
                retention(item[1])
            else:
                raise NotImplementedError(item)
            call_pre_hook()
        outs = list(final_store["idents"])
        if not outs:
            for t4 in range(4):
                outs.append(P.dma("sp", f"yout{t4}", [(yv[:, t4 * 4:(t4 + 1) * 4, :], xres[:, t4 * 4:(t4 + 1) * 4, :])],
                                  reads=[xres[:, t4 * 4:(t4 + 1) * 4, :]]))
        P.wait_all("sp", outs)
        P.emit()
    return nc, P, sorted(k for k in dr)


def const_inputs():
    c = {"ident": np.eye(128, dtype=np.float32)}
    pm = np.zeros((8, 128, 128), np.float32)
    pc = np.zeros((1, 4, 128), np.float32)
    sidx = np.arange(128)[:, None]
    tidx = np.arange(128)[None, :]
    for g, w in enumerate((2, 4, 8, 16)):
        pm[2 * g] = ((tidx - sidx >= 0) & (tidx - sidx < w)).astype(np.float32)
        pm[2 * g + 1] = ((tidx + 128 - sidx) < w).astype(np.float32)
        pc[0, g] = 1.0 / np.minimum(np.arange(128) + 1, w)
    kk = np.arange(128)
    c["trimask"] = (kk[:, None] <= kk[None, :]).astype(np.float32)
    inv = (1.0 / (np.float32(10000.0) ** (np.arange(0, 64, 2, dtype=np.float32) / np.float32(64)))).astype(np.float32)
    dac = np.zeros((128, 2), np.float32)
    dac[:, 0] = inv[kk % 32]
    dac[:, 1] = np.where((kk % 64) < 32, -1.0, 1.0)
    c["dacst"] = dac
    rc = np.zeros((128, 16), np.float32)
    rc[:, 0] = (1.0 / (np.float32(10000.0) ** np.linspace(0.0, 1.0, 128, dtype=np.float32))).astype(np.float32)
    rdt = np.zeros((4, 128, 128), np.float32)
    for h in range(4):
        gam = 1.0 - 2.0 ** (-5.0 - h)
        rc[:, 1 + h] = gam ** (kk + 1.0)
        rc[:, 5 + h] = gam ** (127.0 - kk) / 16.0
        dif = kk[None, :] - kk[:, None]
        rdt[h] = np.where(dif >= 0, gam ** np.maximum(dif, 0), 0.0) / 16.0
    c["retcst"] = rc
    c["retdt"] = rdt
    c["poolmat"] = pm
    c["poolcnt"] = pc
    return c


def layer_plan(i):
    return [("ffn", i, 0), ("mix", i), ("xa", i), ("ffn", i, 1)]


FUSED = True
_CACHE = {}


def _program(plan):
    key = tuple(plan)
    if key not in _CACHE:
        _CACHE[key] = build_program(list(plan))
    return _CACHE[key]


def kernel(**inputs):
    n = 8
    consts = const_inputs()
    x = np.asarray(inputs["x"], dtype=np.float32)
    cur = [np.ascontiguousarray(x[b]) for b in range(n)]
    if FUSED:
        plans = [sum((layer_plan(i) for i in range(DEPTH)), [])]
    else:
        plans = [layer_plan(i) for i in range(DEPTH)]
    for plan in plans:
        nc, _, names = _program(plan)
        in_maps = []
        for b in range(n):
            m = {"x": cur[b],
                 "mem": np.ascontiguousarray(inputs["mem"][b], dtype=np.float32),
                 "positions": np.ascontiguousarray(inputs["positions"][b:b + 1], dtype=np.int32)}
            m.update(consts)
            for name in names:
                if name not in m:
                    m[name] = np.ascontiguousarray(inputs[name], dtype=np.float32)
            in_maps.append({k: m[k] for k in names})
        res = run_bass_kernel_spmd(nc, in_maps, core_ids=list(range(n)))
        cur = [np.asarray(res.results[b]["y"], dtype=np.float32) for b in range(n)]
    return np.stack(cur, axis=0)
```
